# Optimizing a Trainium2 kernel written in Bass

```python
import jax, jax.numpy as jnp
from jax import lax
import numpy as np

D_MODEL = 1024
BATCH = 4
SEQ = 8192
DEPTH = 2

CHUNK = 64
MIX = D_MODEL
HEAD_DIM = 64
A_WIDTH = 3 * MIX // 8
A_HEADS = A_WIDTH // HEAD_DIM
GMLP_BLOCK = 128
B_WIDTH = MIX // 4
B_GROUPS = 4
B_GROUP_DIM = B_WIDTH // B_GROUPS
POOL_WINDOWS = (2, 4, 8, 16)
C_WIDTH = MIX - A_WIDTH - B_WIDTH
C_KERNEL = 31
PROJ_WIDTH = 2 * A_WIDTH + B_WIDTH + 2 * C_WIDTH
N_MEM = 256
X_HEADS = 4
X_HEAD_DIM = D_MODEL // X_HEADS
D_FF = ((int(8 * D_MODEL / 3) + 127) // 128) * 128
FFN_KERNEL = 3
RMS_EPS = 1e-6
LN_EPS = 1e-5

kernel_name = "hybrid_chunk_causal_gmlp_pool_conformer_encoder"


def rms_norm(x, g):
    xf = x.astype(jnp.float32)
    y = xf * lax.rsqrt(jnp.mean(xf * xf, axis=-1, keepdims=True) + RMS_EPS)
    return (y * g.astype(jnp.float32)).astype(x.dtype)


def layer_norm(x, g, b):
    xf = x.astype(jnp.float32)
    mu = jnp.mean(xf, axis=-1, keepdims=True)
    xc = xf - mu
    var = jnp.mean(xc * xc, axis=-1, keepdims=True)
    y = xc * lax.rsqrt(var + LN_EPS)
    return (y * g.astype(jnp.float32) + b.astype(jnp.float32)).astype(x.dtype)


def causal_dwconv(x, w, b):
    k = w.shape[0]
    xp = jnp.pad(x, ((0, 0), (k - 1, 0), (0, 0)))
    y = lax.conv_general_dilated(
        xp, w[:, None, :], window_strides=(1,), padding='VALID',
        dimension_numbers=('NWC', 'WIO', 'NWC'), feature_group_count=x.shape[-1])
    return y + b


def gmlp_spatial_gate(z, ln_g, ln_b, w_s, b_s):
    z = jax.nn.gelu(z, approximate=False)
    u, v = jnp.split(z, 2, axis=-1)
    v = layer_norm(v, ln_g, ln_b)
    bsz, s, _ = v.shape
    nb = s // GMLP_BLOCK
    v = v.reshape(bsz, nb, GMLP_BLOCK, A_HEADS, HEAD_DIM)
    pos = jnp.arange(GMLP_BLOCK)
    mask = (pos[None, :] // CHUNK) <= (pos[:, None] // CHUNK)
    w = jnp.where(mask[None], w_s, 0.0)
    mixed = jnp.einsum('hij,bnjhc->bnihc', w, v) + b_s.T[:, :, None]
    return u * mixed.reshape(bsz, s, A_WIDTH)


def multiscale_pool(p, w_pool, b_pool, scale):
    bsz, s, _ = p.shape
    pg = p.reshape(bsz, s, B_GROUPS, B_GROUP_DIM)
    pf = pg.astype(jnp.float32)
    cs = jnp.pad(jnp.cumsum(pf, axis=1), ((0, 0), (1, 0), (0, 0), (0, 0)))
    t = jnp.arange(s)
    outs = []
    for g, win in enumerate(POOL_WINDOWS):
        lo = jnp.maximum(t + 1 - win, 0)
        cnt = (t + 1 - lo).astype(jnp.float32)
        outs.append((cs[:, 1:, g] - cs[:, lo, g]) / cnt[None, :, None])
    pooled = jnp.stack(outs, axis=2)
    y = (pooled - pf).astype(p.dtype)
    y = jnp.einsum('bsgc,gcd->bsgd', y, w_pool) + b_pool
    return y.reshape(bsz, s, B_WIDTH) * scale


def conformer_conv(z, conv_w, conv_b, ln_g, ln_b):
    a, g = jnp.split(z, 2, axis=-1)
    h = a * jax.nn.sigmoid(g)
    h = causal_dwconv(h, conv_w, conv_b)
    h = layer_norm(h, ln_g, ln_b)
    return jax.nn.silu(h)


def cross_attention(h, mem_n, wq, wk, wv, wo):
    bsz, s, _ = h.shape
    m = mem_n.shape[1]
    q = (h @ wq).reshape(bsz, s, X_HEADS, X_HEAD_DIM)
    k = (mem_n @ wk).reshape(bsz, m, X_HEADS, X_HEAD_DIM)
    v = (mem_n @ wv).reshape(bsz, m, X_HEADS, X_HEAD_DIM)
    sc = jnp.einsum('bshd,bmhd->bhsm', q, k).astype(jnp.float32) * (X_HEAD_DIM ** -0.5)
    pr = jax.nn.softmax(sc, axis=-1).astype(v.dtype)
    o = jnp.einsum('bhsm,bmhd->bshd', pr, v).reshape(bsz, s, D_MODEL)
    return o @ wo


def conv_ffn(h, w_up, conv_w, conv_b, w_down):
    u = h @ w_up
    u = causal_dwconv(u, conv_w, conv_b)
    g, v = jnp.split(u, 2, axis=-1)
    return (jax.nn.silu(g) * v) @ w_down


def setup_inputs(seed: int = 0) -> dict:
    key = jax.random.key(seed)
    ks = jax.random.split(key, 32)
    L = DEPTH
    f32 = jnp.float32

    def nrm(k, shape, scale):
        return jax.random.normal(k, shape, f32) * scale

    def gain(k, shape):
        return 1.0 + 0.02 * jax.random.normal(k, shape, f32)

    return {
        "x": jax.random.normal(ks[0], (BATCH, SEQ, D_MODEL), f32),
        "mem": jax.random.normal(ks[1], (BATCH, N_MEM, D_MODEL), f32),
        "norm_mix": gain(ks[2], (L, D_MODEL)),
        "w_in": nrm(ks[3], (L, D_MODEL, PROJ_WIDTH), D_MODEL ** -0.5),
        "gmlp_ln_g": gain(ks[4], (L, A_WIDTH)),
        "gmlp_ln_b": nrm(ks[5], (L, A_WIDTH), 0.02),
        "gmlp_ws": nrm(ks[6], (L, A_HEADS, GMLP_BLOCK, GMLP_BLOCK), GMLP_BLOCK ** -0.5),
        "gmlp_bs": gain(ks[7], (L, A_HEADS, GMLP_BLOCK)),
        "pool_w": nrm(ks[8], (L, B_GROUPS, B_GROUP_DIM, B_GROUP_DIM), B_GROUP_DIM ** -0.5),
        "pool_b": nrm(ks[9], (L, B_GROUPS, B_GROUP_DIM), 0.02),
        "pool_scale": gain(ks[10], (L, B_WIDTH)),
        "conv_w": nrm(ks[11], (L, C_KERNEL, C_WIDTH), C_KERNEL ** -0.5),
        "conv_b": nrm(ks[12], (L, C_WIDTH), 0.02),
        "conv_ln_g": gain(ks[13], (L, C_WIDTH)),
        "conv_ln_b": nrm(ks[14], (L, C_WIDTH), 0.02),
        "w_out": nrm(ks[15], (L, MIX, D_MODEL), MIX ** -0.5),
        "norm_x": gain(ks[16], (L, D_MODEL)),
        "norm_mem": gain(ks[17], (L, D_MODEL)),
        "wq": nrm(ks[18], (L, D_MODEL, D_MODEL), D_MODEL ** -0.5),
        "wk": nrm(ks[19], (L, D_MODEL, D_MODEL), D_MODEL ** -0.5),
        "wv": nrm(ks[20], (L, D_MODEL, D_MODEL), D_MODEL ** -0.5),
        "wo": nrm(ks[21], (L, D_MODEL, D_MODEL), D_MODEL ** -0.5),
        "norm_ffn": gain(ks[22], (L, D_MODEL)),
        "w_up": nrm(ks[23], (L, D_MODEL, 2 * D_FF), D_MODEL ** -0.5),
        "ffn_conv_w": nrm(ks[24], (L, FFN_KERNEL, 2 * D_FF), FFN_KERNEL ** -0.5),
        "ffn_conv_b": nrm(ks[25], (L, 2 * D_FF), 0.02),
        "w_down": nrm(ks[26], (L, D_FF, D_MODEL), D_FF ** -0.5),
        "norm_final": gain(ks[27], (D_MODEL,)),
    }


def reference(x, mem, norm_mix, w_in, gmlp_ln_g, gmlp_ln_b, gmlp_ws, gmlp_bs,
              pool_w, pool_b, pool_scale, conv_w, conv_b, conv_ln_g, conv_ln_b,
              w_out, norm_x, norm_mem, wq, wk, wv, wo, norm_ffn, w_up,
              ffn_conv_w, ffn_conv_b, w_down, norm_final):
    a_end = 2 * A_WIDTH
    b_end = a_end + B_WIDTH
    for l in range(DEPTH):
        h = rms_norm(x, norm_mix[l])
        z = h @ w_in[l]
        y_a = gmlp_spatial_gate(z[..., :a_end], gmlp_ln_g[l], gmlp_ln_b[l], gmlp_ws[l], gmlp_bs[l])
        y_b = multiscale_pool(z[..., a_end:b_end], pool_w[l], pool_b[l], pool_scale[l])
        y_c = conformer_conv(z[..., b_end:], conv_w[l], conv_b[l], conv_ln_g[l], conv_ln_b[l])
        y = jnp.concatenate([y_a, y_b, y_c], axis=-1)
        x = x + y @ w_out[l]
        h = rms_norm(x, norm_x[l])
        mem_n = rms_norm(mem, norm_mem[l])
        x = x + cross_attention(h, mem_n, wq[l], wk[l], wv[l], wo[l])
        h = rms_norm(x, norm_ffn[l])
        x = x + conv_ffn(h, w_up[l], ffn_conv_w[l], ffn_conv_b[l], w_down[l])
    return rms_norm(x, norm_final)
```

```python
import numpy as np
from contextlib import ExitStack
import concourse.bass as bass
import concourse.mybir as mybir
from concourse.bass_utils import run_bass_kernel_spmd

F32 = mybir.dt.float32
BF16 = mybir.dt.bfloat16
ALU = mybir.AluOpType
AF = mybir.ActivationFunctionType

D = 1024
NCH = 8
AW = 384
PROJ = 1792
DFF = 2816
NJ = 22
NMEM = 256
HALO = 256
CK = 31
RMS_EPS = 1e-6
LN_EPS = 1e-5
NVL = 314
GSZ = 4
SUBMAX = 384
NOMASK = False
SKIP_HALO_TILE = True

V_NMIX, V_NX, V_NFFN, V_NMEM = 0, 8, 16, 24
V_CB, V_CLG, V_CLB = 32, 35, 38
V_PB, V_PS = 41, 43
V_CW = 45
V_FW = 138
V_FB = 270


ENGS = ['pe', 'act', 'dve', 'pool', 'sp']
BLK = {'pe': 'tensor', 'act': 'scalar', 'dve': 'vector', 'pool': 'gpsimd', 'sp': 'sync'}


class Res:
    __slots__ = ('name', 'w', 'rd')

    def __init__(self, name):
        self.name = name
        self.w = None
        self.rd = {}


class DmaSem:
    def __init__(self, h):
        self.h = h
        self.count = 0


class Prog:
    def __init__(self):
        self.ops = {e: [] for e in ENGS}

    def op(self, eng, fn, reads=(), writes=(), dsem=None):
        idx = len(self.ops[eng])
        deps = []
        for r in reads:
            if r.w is not None:
                deps.append((r.w, True))
        for r in writes:
            if r.w is not None:
                deps.append((r.w, False))
            for t in r.rd.values():
                deps.append((t, False))
        waits = []
        for tok, raw in deps:
            if tok[0] == 'eng':
                _, e, i = tok
                if e == eng:
                    if eng == 'pe' or (not raw) or idx - i >= 3:
                        continue
                self.ops[e][i]['sig'] = True
            waits.append(tok)
        rec = dict(fn=fn, waits=waits, sig=False, dsem=None)
        if dsem is not None:
            dsem.count += 16
            rec['dsem'] = dsem
            mytok = ('dma', dsem, dsem.count)
        else:
            mytok = ('eng', eng, idx)
        self.ops[eng].append(rec)
        key = (mytok[0], mytok[1])
        for r in reads:
            r.rd[key] = mytok
        for r in writes:
            r.w = mytok
            r.rd = {}
        return mytok

    def emit(self, block, sems, final_waits):
        cnt = {}
        for e in ENGS:
            c = 0
            arr = []
            for o in self.ops[e]:
                if o['sig']:
                    c += 1
                arr.append(c)
            cnt[e] = arr
        for e in ENGS:
            def body(engobj, e=e):
                seen = {}
                for o in self.ops[e]:
                    for tok in o['waits']:
                        if tok[0] == 'eng':
                            sem = sems[tok[1]]
                            val = cnt[tok[1]][tok[2]]
                            key = ('eng', tok[1])
                        else:
                            sem = tok[1].h
                            val = tok[2]
                            key = ('dma', id(tok[1]))
                        if seen.get(key, 0) >= val:
                            continue
                        seen[key] = val
                        engobj.wait_ge(sem, val)
                    ins = o['fn'](engobj)
                    if o['sig']:
                        ins.then_inc(sems[e], 1)
                    if o['dsem'] is not None:
                        ins.then_inc(o['dsem'].h, 16)
                if e == 'sp':
                    for ds in final_waits:
                        if ds.count:
                            engobj.wait_ge(ds.h, ds.count)
            getattr(block, BLK[e])(body)


def split_subs(P):
    subs = []
    c = 0
    while c < P:
        n = min(SUBMAX, P - c)
        subs.append((c, n))
        c += n
    return subs


def ffn_groups():
    gs = []
    j = 0
    first = NJ % GSZ
    if first:
        gs.append((0, first))
        j = first
    while j < NJ:
        g = min(GSZ, NJ - j)
        gs.append((j, g))
        j += g
    return gs


def build_program(T, passes, L=2, do_mix=True, do_attn=True, do_ffn=True):
    PMAX = max(p[1] for p in passes)
    TOUT = T - HALO
    NV = NVL * L + 9
    nc = bass.Bass("TRN2", target_bir_lowering=False)

    def din(name, shape):
        return nc.dram_tensor(name, shape, F32, kind="ExternalInput").ap()

    x_in = din("x_in", [T, D])
    mem_d = din("mem", [NMEM, D])
    vecs_d = din("vecs", [128, NV])
    lnbc_d = din("lnbc", [128, L * 2 * AW])
    cmask_d = din("cmask", [128, 128])
    apool_d = din("apool", [128, 3 * 4 * 128])
    ind_d = din("ind", [2, 128])
    ident_d = din("ident", [128, 128])
    w_in_d = din("w_in", [L, 128, NCH * PROJ])
    w_out_d = din("w_out", [L, 128, NCH * D])
    wq_d = din("wq", [L, 128, NCH * D])
    wk_d = din("wk", [L, 128, NCH * D])
    wv_d = din("wv", [L, 128, NCH * D])
    wo_d = din("wo", [L, 128, NCH * D])
    NGR = len(ffn_groups())
    GCOLS = 2 * NCH * 512 + GSZ * D
    wffn_d = din("wffn", [L, NGR, 128, GCOLS])
    ws_d = din("gmlp_ws", [L, 6, 128, 128])
    bs_d = din("gmlp_bs", [L, 6, 128])
    pw_d = din("pool_w", [L, 4, 64, 64])
    out_d = nc.dram_tensor("out", [TOUT, D], F32, kind="ExternalOutput").ap()
    kvs_d = nc.dram_tensor("kv_scratch", [L, 128, 2 * NCH * NMEM], BF16, kind="Internal").ap()

    pr = Prog()
    es = ExitStack()
    with es:
        def sb(name, shape, dt):
            return es.enter_context(nc.sbuf_tensor("sb_" + name, shape, dt))

        def new_sem(name):
            return es.enter_context(nc.semaphore(name))

        sems = {e: new_sem("s_" + e) for e in ENGS}
        dsems = []

        def dsem(name):
            d = DmaSem(new_sem(name))
            dsems.append(d)
            return d

        x_fm = sb("x_fm", [128, NCH, PMAX], F32)
        ident_f = sb("ident_f", [128, 128], F32)
        ident_b = sb("ident_b", [128, 128], BF16)
        ones_m = sb("ones_m", [128, 128], BF16)
        ones_1 = sb("ones_1", [128, 128], BF16)
        vecs = sb("vecs", [128, NV], F32)
        lnbc = sb("lnbc", [128, L * 2 * AW], BF16)
        cmask = sb("cmask", [128, 128], F32)
        apool = sb("apool", [128, 3 * 4 * 128], BF16)
        ind = sb("ind", [2, 128], BF16)
        bsrow = sb("bsrow", [2, L * 3 * 128], BF16)
        poolW = sb("poolW", [128, L * 2 * 128], BF16)
        WsT = sb("WsT", [128, L * 6 * 128], BF16)
        S1 = sb("S1", [128, NCH * PROJ], BF16)
        S2 = sb("S2", [128, NCH * D], BF16)
        S3 = sb("S3", [128, 2 * NCH * 512 + GSZ * D], BF16)
        S4 = sb("S4", [128, NCH * D], BF16)
        K_fm = sb("K_fm", [128, NCH, NMEM], BF16)
        V_tm = sb("V_tm", [128, 2, D], BF16)
        p_prev = sb("p_prev", [128, L, 256], BF16)
        hg_tail = sb("hg_tail", [128, L, 3 * 30], BF16)
        hp_tail = sb("hp_tail", [128, L, NCH * 2], BF16)
        sq = sb("sq", [128, NCH, SUBMAX], BF16)
        rstd = sb("rstd", [128, SUBMAX], F32)
        xs = sb("xs", [128, 2, D], F32)
        memn = sb("memn", [128, NCH, NMEM], BF16)
        st6 = sb("st6", [128, 2, 6], F32)
        mv = sb("mv", [128, 2, 2], F32)
        rsv = sb("rsv", [128, 2], F32)
        ssm = sb("ssm", [128, 2], F32)
        vst6 = sb("vst6", [128, 3, 6], F32)
        vmv = sb("vmv", [128, 3, 2], F32)
        vrs = sb("vrs", [128, 3], F32)
        cst6 = sb("cst6", [128, 3, 6], F32)
        cmv = sb("cmv", [128, 3, 2], F32)
        crsv = sb("crsv", [128, 3], F32)
        epsv = sb("epsv", [128, 2], F32)
        h_pass = sb("h_pass", [128, NCH, PMAX + 2], BF16)
        ARENA_BYTES = 35584
        arena_f = sb("arena", [128, ARENA_BYTES // 4], F32)
        arena_b = arena_f.bitcast(BF16)
        AR = {}

        def carve(name, phase, off, shape, dt):
            nel = 1
            for d_ in shape[1:]:
                nel *= d_
            esz = 4 if dt == F32 else 2
            assert off % 4 == 0 and off + nel * esz <= ARENA_BYTES, name
            base = arena_f if dt == F32 else arena_b
            v = base[:, off // esz:off // esz + nel]
            if len(shape) == 3:
                v = v.rearrange("p (a b) -> p a b", a=shape[1])
            elif len(shape) == 4:
                v = v.rearrange("p (a b c) -> p a b c", a=shape[1], b=shape[2])
            AR[name] = (phase, off, off + nel * esz)
            return v

        S_ = SUBMAX
        h_sub = carve("h_sub", "ma", 0, [128, NCH, S_], BF16)
        y_sub = carve("y_sub", "ma", 6144, [128, NCH, S_], BF16)
        ug = carve("ug", "m", 12288, [128, 3, S_], BF16)
        vg = carve("vg", "m", 14592, [128, 2, AW], F32)
        v_tm = carve("v_tm", "m", 17664, [128, 3, AW], BF16)
        p_cur = carve("p_cur", "m", 19968, [128, 3, 256], BF16)
        sig = carve("sig", "m", 21504, [128, 2, S_], F32)
        hglu = carve("hglu", "m", 24576, [128, 3, 30 + S_], BF16)
        pd = carve("pd", "m", 27136, [128, 2, S_], BF16)
        hc = carve("hc", "m", 28672, [128, 3, S_], F32)
        hn = carve("hn", "m", 33280, [128, 3, AW], BF16)
        q_sub = carve("q_sub", "a", 12288, [128, NCH, S_], BF16)
        Eb = carve("Eb", "a", 18432, [128, 2, 2, S_], BF16)
        rden = carve("rden", "a", 21504, [128, 2, S_], F32)
        tg = carve("tg", "f", 6144, [128, 2, S_], F32)
        tv = carve("tv", "f", 9216, [128, 2, S_], F32)
        sgb = carve("sgb", "f", 12288, [128, 2, S_], F32)
        gated = carve("gated", "f", 15360, [128, 2, GSZ, S_], BF16)
        hfin = carve("hfin", "f", 21504, [128, NCH, S_], F32)
        wstage = carve("wstage", "s", 28672, [128, 6, 128], F32)

        banks = [es.enter_context(nc.psum_tensor("ps%d" % i, [128, 512], F32)) for i in range(8)]
        r_bank = [Res("bank%d" % i) for i in range(8)]
        bank_ctr = [0]

        def newbank():
            i = bank_ctr[0] % 8
            bank_ctr[0] += 1
            return banks[i], r_bank[i]

        R = {}

        def res(name):
            if name not in R:
                R[name] = Res(name)
            return R[name]

        RES_VIEW = {"vg0": "vg", "vg1": "vg", "sig0": "sig", "sig1": "sig", "hn0": "hn", "hn1": "hn", "hn2": "hn",
                    "E0": "Eb", "E1": "Eb", "rden0": "rden", "rden1": "rden", "t0_0": "tg", "t0_1": "tg",
                    "t1_0": "tv", "t1_1": "tv", "sg0": "sgb", "sg1": "sgb", "gated0": "gated", "gated1": "gated"}
        PHASE_RES = {
            "m": ["h_sub", "y_sub", "ug", "vg0", "vg1", "v_tm", "p_cur", "sig0", "sig1", "hglu", "pd", "hc", "hn0", "hn1", "hn2"],
            "a": ["h_sub", "y_sub", "q_sub", "E0", "E1", "rden0", "rden1"],
            "f": ["t0_0", "t0_1", "t1_0", "t1_1", "sg0", "sg1", "gated0", "gated1", "hfin"],
            "s": ["wstage"],
        }

        def _merge(dst, tok):
            key = (tok[0], tok[1])
            old = dst.rd.get(key)
            if old is None or old[2] < tok[2]:
                dst.rd[key] = tok

        def phase_begin(ph):
            mine = PHASE_RES[ph]
            for dn in mine:
                dv = RES_VIEW.get(dn, dn)
                _, dlo, dhi = AR[dv]
                dst = res(dn)
                for sn, src in list(R.items()):
                    sv = RES_VIEW.get(sn, sn)
                    if sv not in AR or sv == dv or sn in mine:
                        continue
                    _, slo, shi = AR[sv]
                    if slo < dhi and dlo < shi:
                        if src.w is not None:
                            _merge(dst, src.w)
                        for t in src.rd.values():
                            _merge(dst, t)

        r_const = res("const")
        s_misc = dsem("d_misc")
        s_pw = dsem("d_pw")
        s_ws = dsem("d_ws")
        s_xs = [dsem("d_xs0"), dsem("d_xs1")]
        s_S = {k: dsem("d_" + k) for k in ("S1", "S2", "S3", "S4")}
        s_out = [dsem("d_out0"), dsem("d_out1")]
        s_mem = dsem("d_mem")

        def V(l, off, c=0):
            col = NVL * l + off + c
            return vecs[:, col:col + 1]

        VFIN = NVL * L
        VMASK = NVL * L + 8

        S1v = S1[:, :].rearrange("p (k n) -> p k n", k=NCH)
        S2v = S2[:, :].rearrange("p (k n) -> p k n", k=NCH)
        S4v = S4[:, :].rearrange("p (k n) -> p k n", k=NCH)
        S3d = S3[:, 0:93 * 128].rearrange("p (t n) -> p t n", n=128)

        def grp_views(buf):
            wg = buf[:, 0:NCH * 512].rearrange("p (k n) -> p k n", k=NCH)
            wv_ = buf[:, NCH * 512:2 * NCH * 512].rearrange("p (k n) -> p k n", k=NCH)
            wd = buf[:, 2 * NCH * 512:2 * NCH * 512 + GSZ * D].rearrange("p (j n) -> p j n", j=GSZ)
            return wg, wv_, wd

        apv = apool[:, :].rearrange("p (a g n) -> p a g n", a=3, g=4)
        lnv = lnbc[:, :].rearrange("p (l t n) -> p l t n", l=L, t=2)
        bsv = bsrow[:, :].rearrange("p (l m n) -> p l m n", l=L, m=3)
        pwv = poolW[:, :].rearrange("p (l m n) -> p l m n", l=L, m=2)
        wsv = WsT[:, :].rearrange("p (l h n) -> p l h n", l=L, h=6)
        hgt = hg_tail[:, :, :].rearrange("p l (c n) -> p l c n", c=3)
        hpt = hp_tail[:, :, :].rearrange("p l (c n) -> p l c n", c=NCH)

        rS = {k: res(k) for k in ("S1", "S2", "S3", "S4")}

        SLOTBUF = {"S1": S1, "S2": S2, "S4": S4}

        def load_full(slot, view, src):
            ncol = src.shape[-1]
            pr.op('pool', lambda e: e.dma_start(out=SLOTBUF[slot][:, 0:ncol].rearrange("p (a b) -> p a b", b=2048),
                                                in_=src.rearrange("p (a b) -> p a b", b=2048)),
                  writes=[rS[slot]], dsem=s_S[slot])

        load_full("S4", S4v, wk_d[0])
        load_full("S1", S1v, w_in_d[0])
        load_full("S2", S2v, w_out_d[0])

        pr.op('sp', lambda e: e.dma_start(out=ident_f[:, :], in_=ident_d), writes=[r_const], dsem=s_misc)
        pr.op('sp', lambda e: e.dma_start(out=vecs[:, :], in_=vecs_d), writes=[r_const], dsem=s_misc)
        pr.op('sp', lambda e: e.dma_start(out=cmask[:, :], in_=cmask_d), writes=[r_const], dsem=s_misc)
        pr.op('pool', lambda e: e.dma_start(out=lnbc[:, :], in_=lnbc_d), writes=[r_const], dsem=s_misc)
        pr.op('pool', lambda e: e.dma_start(out=apool[:, :], in_=apool_d), writes=[r_const], dsem=s_misc)
        pr.op('pool', lambda e: e.dma_start(out=ind[:, :], in_=ind_d), writes=[r_const], dsem=s_misc)
        pr.op('pool', lambda e: e.dma_start(
            out=bsrow[:, :].rearrange("p (l m n) -> p l m n", l=L, m=3),
            in_=bs_d.rearrange("l (m r) i -> r l m i", r=2)), writes=[r_const], dsem=s_misc)
        r_pw = res("poolW")
        pr.op('dve', lambda e: e.memset(poolW[:, :], 0.0), writes=[r_pw])
        for l in range(L):
            for g in range(4):
                gg, m2 = g % 2, g // 2
                pr.op('pool', lambda e, l=l, g=g, gg=gg, m2=m2: e.dma_start(
                    out=pwv[gg * 64:(gg + 1) * 64, l, m2, gg * 64:(gg + 1) * 64], in_=pw_d[l, g]),
                    writes=[r_pw], dsem=s_pw)
        pr.op('dve', lambda e: e.tensor_copy(out=ident_b[:, :], in_=ident_f[:, :]), reads=[r_const], writes=[res("identb")])
        pr.op('dve', lambda e: e.memset(ones_m[:, :], 1.0 / 1024.0), writes=[res("ones_m")])
        pr.op('dve', lambda e: e.memset(ones_1[:, :], 1.0), writes=[res("ones_1")])
        pr.op('dve', lambda e: e.memset(epsv[:, 0:1], RMS_EPS), writes=[res("epsv")])
        pr.op('dve', lambda e: e.memset(epsv[:, 1:2], LN_EPS), writes=[res("epsv")])
        pr.op('dve', lambda e: e.memset(p_prev[:, :, :], 0.0), writes=[res("p_prev")])
        pr.op('dve', lambda e: e.memset(hg_tail[:, :, :], 0.0), writes=[res("hg_tail")])
        pr.op('dve', lambda e: e.memset(hp_tail[:, :, :], 0.0), writes=[res("hp_tail")])
        phase_begin("s")
        r_wst = res("wstage")
        for l in range(L):
            pr.op('sp', lambda e, l=l: e.dma_start(out=wstage[:, :, :], in_=ws_d[l].rearrange("h i j -> i h j")),
                  writes=[r_wst], dsem=s_ws)
            for h in range(6):
                pr.op('dve', lambda e, h=h: e.tensor_tensor(out=wstage[:, h, :], in0=wstage[:, h, :], in1=cmask[:, :], op=ALU.mult),
                      reads=[r_wst, r_const], writes=[r_wst])
            for h0 in range(0, 6, 3):
                bk, rb = newbank()
                for h in range(h0, h0 + 3):
                    pr.op('pe', lambda e, h=h, bk=bk, h0=h0: e.transpose(
                        out=bk[:, (h - h0) * 128:(h - h0 + 1) * 128], in_=wstage[:, h, :], identity=ident_f[:, :]),
                        reads=[r_wst, r_const], writes=[rb])
                pr.op('act', lambda e, l=l, h0=h0, bk=bk: e.activation(
                    out=wsv[:, l, h0:h0 + 3, :], in_=bk[:, 0:384].rearrange("p (h n) -> p h n", h=3), func=AF.Copy),
                    reads=[rb], writes=[res("WsT")])

        def load_group(l, gi):
            slot = "S1" if gi % 2 == 0 else "S3"
            buf = S1 if gi % 2 == 0 else S3
            pr.op('pool', lambda e: e.dma_start(out=buf[:, 0:GCOLS].rearrange("p (a b) -> p a b", b=2048),
                                                in_=wffn_d[l, gi].rearrange("p (a b) -> p a b", b=2048)),
                  writes=[rS[slot]], dsem=s_S[slot])

        S3c = [res("S3c%d" % c) for c in range(3)]

        def diag_chunk(l, c):
            for k in range(CK):
                t = k * 3 + c
                wr = [S3c[c], rS["S3"]] if k == 0 else [S3c[c]]
                if c <= 1:
                    pr.op('dve', lambda e, t=t, l=l: e.tensor_scalar(
                        out=S3d[:, t, :], in0=ident_b[:, :], scalar1=V(l, V_CW, t), scalar2=None, op0=ALU.mult),
                        reads=[res("identb"), r_const], writes=wr)
                else:
                    pr.op('act', lambda e, t=t, l=l: e.activation(out=S3d[:, t, :], in_=ident_b[:, :], func=AF.Identity,
                                                                 scale=V(l, V_CW, t)),
                          reads=[res("identb"), r_const], writes=wr)

        def build_diag(l):
            for c in range(3):
                diag_chunk(l, c)

        diag_pending = []

        def diag_hook():
            if diag_pending:
                diag_pending.pop(0)()

        def xres(j, m):
            return res("x_%d_%d" % (j, m))

        xs_b = xs.bitcast(BF16)
        sq2 = xs_b[:, 0, :].rearrange("p (a b) -> p a b", a=NCH)[:, :, 0:SUBMAX] if False else None
        xs_flat_b = xs_b[:, :, :].rearrange("p a b -> p (a b)")
        sq2 = xs_flat_b[:, 0:NCH * SUBMAX].rearrange("p (a b) -> p a b", a=NCH)
        xs_flat_f = xs[:, :, :].rearrange("p a b -> p (a b)")
        rstd2 = xs_flat_f[:, (NCH * SUBMAX) // 2:(NCH * SUBMAX) // 2 + SUBMAX]
        NB = {0: (sq, rstd, [res("sq")], [res("rstd")]),
              1: (sq2, rstd2, [res("xs0"), res("xs1")], [res("xs0"), res("xs1")])}
        norm_banks = {}

        def normA(key, j, c0, n, bs=0):
            sq_, rstd_, r_sq, r_rs = NB[bs]
            for m in range(NCH):
                pr.op('act', lambda e, m=m: e.activation(out=sq_[:, m, 0:n], in_=x_fm[:, m, c0:c0 + n], func=AF.Square),
                      reads=[xres(j, m)], writes=r_sq)

        def normB(key, j, c0, n, gcol, dst, dst_res, dst_off=0, mask=False, bs=0):
            sq_, rstd_, r_sq, r_rs = NB[bs]
            bk, rb = newbank()
            for m in range(NCH):
                pr.op('pe', lambda e, m=m, bk=bk: e.matmul(bk[:, 0:n], lhsT=ones_m[:, :], rhs=sq_[:, m, 0:n],
                                                          start=(m == 0), stop=(m == NCH - 1)),
                      reads=r_sq + [res("ones_m")], writes=[rb])
            pr.op('act', lambda e, bk=bk: e.activation(out=rstd_[:, 0:n], in_=bk[:, 0:n], func=AF.Ln, bias=epsv[:, 0:1]),
                  reads=[rb, res("epsv")], writes=r_rs)
            pr.op('act', lambda e: e.activation(out=rstd_[:, 0:n], in_=rstd_[:, 0:n], func=AF.Exp, scale=-0.5), reads=r_rs, writes=r_rs)
            for m in range(NCH):
                pr.op('dve', lambda e, m=m: e.scalar_tensor_tensor(
                    out=dst[:, m, dst_off:dst_off + n], in0=x_fm[:, m, c0:c0 + n], scalar=vecs[:, gcol + m:gcol + m + 1],
                    in1=rstd_[:, 0:n], op0=ALU.mult, op1=ALU.mult),
                    reads=[xres(j, m), r_const] + r_rs, writes=[dst_res])
            if mask and not NOMASK:
                mw = HALO - c0
                pr.op('dve', lambda e: e.tensor_scalar(
                    out=dst[:, :, dst_off:dst_off + mw], in0=dst[:, :, dst_off:dst_off + mw],
                    scalar1=vecs[:, VMASK:VMASK + 1], scalar2=None, op0=ALU.mult),
                    reads=[dst_res, r_const], writes=[dst_res])

        def norm(j, c0, n, gcol, dst, dst_res, dst_off=0, mask=False, bs=0):
            normA(None, j, c0, n, bs)
            normB(None, j, c0, n, gcol, dst, dst_res, dst_off, mask, bs)

        def proj_add(j, c0, n, Wv, slot, src, src_res):
            for m in range(NCH):
                bk, rb = newbank()
                for k in range(NCH):
                    pr.op('pe', lambda e, m=m, k=k, bk=bk: e.matmul(bk[:, 0:n], lhsT=Wv[:, k, m * 128:(m + 1) * 128],
                                                                   rhs=src[:, k, 0:n], start=(k == 0), stop=(k == NCH - 1)),
                          reads=[rS[slot], src_res], writes=[rb])
                pr.op('dve', lambda e, m=m, bk=bk: e.tensor_tensor(out=x_fm[:, m, c0:c0 + n], in0=bk[:, 0:n],
                                                                  in1=x_fm[:, m, c0:c0 + n], op=ALU.add),
                      reads=[rb, xres(j, m)], writes=[xres(j, m)])

        def ln_rows(src_ap, k, width):
            rk = res("stat%d" % k)
            pr_reads = []
            return rk

        def load_x_sub(ps0, subs, j):
            c0j, nj = subs[j]
            for ti in range(c0j // 128, (c0j + nj) // 128):
                b = ti % 2
                t0 = ps0 + ti * 128
                rx = res("xs%d" % b)
                pr.op('sp', lambda e, b=b, t0=t0: e.dma_start(out=xs[:, b, :], in_=x_in[t0:t0 + 128, :]),
                      writes=[rx], dsem=s_xs[b])
                for half in range(2):
                    bk, rb = newbank()
                    for c in range(4):
                        pr.op('pe', lambda e, b=b, c=c, half=half, bk=bk: e.transpose(
                            out=bk[:, c * 128:(c + 1) * 128], in_=xs[:, b, (half * 4 + c) * 128:(half * 4 + c + 1) * 128],
                            identity=ident_f[:, :]), reads=[rx, r_const], writes=[rb])
                    eng = 'act' if half == 0 else 'dve'
                    outap = x_fm[:, half * 4:half * 4 + 4, ti * 128:(ti + 1) * 128]
                    inap = bk[:, 0:512].rearrange("p (c t) -> p c t", c=4)
                    if eng == 'act':
                        f = lambda e, outap=outap, inap=inap: e.activation(out=outap, in_=inap, func=AF.Copy)
                    else:
                        f = lambda e, outap=outap, inap=inap: e.tensor_copy(out=outap, in_=inap)
                    pr.op(eng, f, reads=[rb], writes=[xres(j, half * 4 + c) for c in range(4)])

        def load_x(ps0, P, subs):
            for j in range(len(subs)):
                load_x_sub(ps0, subs, j)

        def kv_k(l):
            rxs = [res("xs0"), res("xs1")]
            pr.op('sp', lambda e: e.dma_start(out=xs[:, :, :], in_=mem_d.rearrange("(t p) d -> p t d", p=128)),
                  reads=[], writes=rxs, dsem=s_mem)
            r_ss = res("ssm")
            sqf = sq[:, :, :].rearrange("p a b -> p (a b)")
            for t in range(2):
                pr.op('act', lambda e, t=t: e.activation(out=sqf[:, 0:D], in_=xs[:, t, :], func=AF.Square,
                                                        accum_out=ssm[:, t:t + 1]),
                      reads=rxs, writes=[res("sq"), r_ss])
            pr.op('act', lambda e: e.activation(out=ssm[:, :], in_=ssm[:, :], func=AF.Ln, scale=1.0 / D, bias=epsv[:, 0:1]),
                  reads=[r_ss, res("epsv")], writes=[r_ss])
            pr.op('act', lambda e: e.activation(out=ssm[:, :], in_=ssm[:, :], func=AF.Exp, scale=-0.5), reads=[r_ss], writes=[r_ss])
            for t in range(2):
                pr.op('dve', lambda e, t=t: e.tensor_scalar(out=xs[:, t, :], in0=xs[:, t, :], scalar1=ssm[:, t:t + 1],
                                                           scalar2=None, op0=ALU.mult),
                      reads=rxs + [r_ss], writes=rxs)
            r_memn = res("memn")
            for k in range(NCH):
                bk, rb = newbank()
                for t in range(2):
                    pr.op('pe', lambda e, k=k, t=t, bk=bk: e.transpose(
                        out=bk[:, t * 128:(t + 1) * 128], in_=xs[:, t, k * 128:(k + 1) * 128], identity=ident_f[:, :]),
                        reads=rxs + [r_const], writes=[rb])
                pr.op('act', lambda e, k=k, bk=bk: e.activation(out=memn[:, k, :], in_=bk[:, 0:256], func=AF.Identity,
                                                               scale=V(l, V_NMEM, k)),
                      reads=[rb, r_const], writes=[r_memn])
            for m in range(NCH):
                bk, rb = newbank()
                for k in range(NCH):
                    pr.op('pe', lambda e, m=m, k=k, bk=bk: e.matmul(bk[:, 0:NMEM], lhsT=S4v[:, k, m * 128:(m + 1) * 128],
                                                                   rhs=memn[:, k, :], start=(k == 0), stop=(k == NCH - 1)),
                          reads=[rS["S4"], r_memn], writes=[rb])
                pr.op('act', lambda e, m=m, bk=bk: e.activation(out=K_fm[:, m, :], in_=bk[:, 0:NMEM], func=AF.Identity, scale=1.0 / 16.0),
                      reads=[rb], writes=[res("K_fm")])

        def kv_v(l):
            r_memn = res("memn")
            for t in range(2):
                for half in range(2):
                    bk, rb = newbank()
                    for k in range(NCH):
                        pr.op('pe', lambda e, t=t, half=half, k=k, bk=bk: e.matmul(
                            bk[:, 0:512], lhsT=memn[:, k, t * 128:(t + 1) * 128], rhs=S4v[:, k, half * 512:(half + 1) * 512],
                            start=(k == 0), stop=(k == NCH - 1)), reads=[rS["S4"], r_memn], writes=[rb])
                    pr.op('act', lambda e, t=t, half=half, bk=bk: e.activation(
                        out=V_tm[:, t, half * 512:(half + 1) * 512], in_=bk[:, 0:512], func=AF.Copy),
                        reads=[rb], writes=[res("V_tm")])

        s_kvs = dsem("d_kvs")
        s_kvl = dsem("d_kvl")
        KVW = NCH * NMEM

        def store_kv(l):
            pr.op('sp', lambda e: e.dma_start(out=kvs_d[l][:, 0:KVW], in_=K_fm[:, :, :].rearrange("p a b -> p (a b)")),
                  reads=[res("K_fm")], writes=[res("kvs%d" % l)], dsem=s_kvs)
            pr.op('sp', lambda e: e.dma_start(out=kvs_d[l][:, KVW:2 * KVW], in_=V_tm[:, :, :].rearrange("p a b -> p (a b)")),
                  reads=[res("V_tm")], writes=[res("kvs%d" % l)], dsem=s_kvs)

        def load_kv(l):
            pr.op('sp', lambda e: e.dma_start(out=K_fm[:, :, :].rearrange("p a b -> p (a b)"), in_=kvs_d[l][:, 0:KVW]),
                  reads=[res("kvs%d" % l)], writes=[res("K_fm")], dsem=s_kvl)
            pr.op('sp', lambda e: e.dma_start(out=V_tm[:, :, :].rearrange("p a b -> p (a b)"), in_=kvs_d[l][:, KVW:2 * KVW]),
                  reads=[res("kvs%d" % l)], writes=[res("V_tm")], dsem=s_kvl)

        def mix_stages(l, ip, j, c0, n, gt0):
            nt = n // 128
            r_h = res("h_sub")
            W = S1v
            rW = rS["S1"]
            r_ug = res("ug")
            r_vtm = res("v_tm")
            r_pc = res("p_cur")
            r_hg = res("hglu")
            r_hgt = res("hg_tail")
            r_y = res("y_sub")
            r_pp = res("p_prev")
            r_pd = res("pd")
            r_hc = res("hc")

            def st_P():
                VG = [(vg[:, 0, :], res("vg0")), (vg[:, 1, :], res("vg1")), (hc[:, 0, :], res("hc"))]
                r_vs = res("vst")
                for i in range(nt):
                    vb, r_vg = VG[i]
                    bk, rb = newbank()
                    for k in range(NCH):
                        pr.op('pe', lambda e, i=i, k=k, bk=bk: e.matmul(bk[:, 0:AW], lhsT=h_sub[:, k, i * 128:(i + 1) * 128],
                                                                       rhs=W[:, k, AW:2 * AW], start=(k == 0), stop=(k == NCH - 1)),
                              reads=[rW, r_h], writes=[rb])
                    pr.op('act', lambda e, vb=vb, bk=bk: e.activation(out=vb, in_=bk[:, 0:AW], func=AF.Gelu),
                          reads=[rb], writes=[r_vg])
                    pr.op('dve', lambda e, i=i, vb=vb: e.bn_stats(out=vst6[:, i, :], in_=vb), reads=[r_vg], writes=[r_vs])
                    pr.op('dve', lambda e, i=i: e.bn_aggr(out=vmv[:, i, :], in_=vst6[:, i, :]), reads=[r_vs], writes=[r_vs])
                diag_hook()
                for m in range(3):
                    bk, rb = newbank()
                    for k in range(NCH):
                        pr.op('pe', lambda e, m=m, k=k, bk=bk: e.matmul(bk[:, 0:n], lhsT=W[:, k, m * 128:(m + 1) * 128],
                                                                       rhs=h_sub[:, k, 0:n], start=(k == 0), stop=(k == NCH - 1)),
                              reads=[rW, r_h], writes=[rb])
                    pr.op('act', lambda e, m=m, bk=bk: e.activation(out=ug[:, m, 0:n], in_=bk[:, 0:n], func=AF.Gelu),
                          reads=[rb], writes=[r_ug])
                for i in range(nt):
                    bk, rb = newbank()
                    for k in range(NCH):
                        pr.op('pe', lambda e, i=i, k=k, bk=bk: e.matmul(bk[:, 0:256], lhsT=h_sub[:, k, i * 128:(i + 1) * 128],
                                                                       rhs=W[:, k, 768:1024], start=(k == 0), stop=(k == NCH - 1)),
                              reads=[rW, r_h], writes=[rb])
                    pr.op('act', lambda e, i=i, bk=bk: e.activation(out=p_cur[:, i, :], in_=bk[:, 0:256], func=AF.Copy),
                          reads=[rb], writes=[r_pc])
                pr.op('act', lambda e: e.activation(out=hglu[:, :, 0:30], in_=hgt[:, l, :, :], func=AF.Copy),
                      reads=[r_hgt], writes=[r_hg])
                for c in range(3):
                    b = c % 2
                    r_sig = res("sig%d" % b)
                    bk, rb = newbank()
                    for k in range(NCH):
                        pr.op('pe', lambda e, c=c, k=k, bk=bk: e.matmul(bk[:, 0:n], lhsT=W[:, k, 1408 + c * 128:1408 + (c + 1) * 128],
                                                                       rhs=h_sub[:, k, 0:n], start=(k == 0), stop=(k == NCH - 1)),
                              reads=[rW, r_h], writes=[rb])
                    pr.op('act', lambda e, b=b, bk=bk: e.activation(out=sig[:, b, 0:n], in_=bk[:, 0:n], func=AF.Sigmoid),
                          reads=[rb], writes=[r_sig])
                    bk2, rb2 = newbank()
                    for k in range(NCH):
                        pr.op('pe', lambda e, c=c, k=k, bk2=bk2: e.matmul(bk2[:, 0:n], lhsT=W[:, k, 1024 + c * 128:1024 + (c + 1) * 128],
                                                                         rhs=h_sub[:, k, 0:n], start=(k == 0), stop=(k == NCH - 1)),
                              reads=[rW, r_h], writes=[rb2])
                    pr.op('dve', lambda e, c=c, b=b, bk2=bk2: e.tensor_tensor(out=hglu[:, c, 30:30 + n], in0=bk2[:, 0:n],
                                                                             in1=sig[:, b, 0:n], op=ALU.mult),
                          reads=[rb2, r_sig], writes=[r_hg])
                pr.op('act', lambda e: e.activation(out=hgt[:, l, :, :], in_=hglu[:, :, n:n + 30], func=AF.Copy),
                      reads=[r_hg], writes=[r_hgt])
                diag_hook()
                diag_hook()
                pr.op('act', lambda e: e.activation(out=vrs[:, 0:nt], in_=vmv[:, 0:nt, 1], func=AF.Ln, bias=epsv[:, 1:2]),
                      reads=[r_vs, res("epsv")], writes=[r_vs])
                pr.op('act', lambda e: e.activation(out=vrs[:, 0:nt], in_=vrs[:, 0:nt], func=AF.Exp, scale=-0.5),
                      reads=[r_vs], writes=[r_vs])
                for i in range(nt):
                    vb, r_vg = VG[i]
                    pr.op('dve', lambda e, i=i, vb=vb: e.tensor_scalar(out=vb, in0=vb, scalar1=vmv[:, i, 0:1],
                                                                      scalar2=vrs[:, i:i + 1], op0=ALU.subtract, op1=ALU.mult),
                          reads=[r_vg, r_vs], writes=[r_vg])
                    pr.op('pool', lambda e, vb=vb: e.tensor_tensor(out=vb, in0=vb, in1=lnv[:, l, 0, :], op=ALU.mult),
                          reads=[r_vg, r_const], writes=[r_vg])
                    pr.op('pool', lambda e, vb=vb, i=i: e.tensor_tensor(out=v_tm[:, i, :], in0=vb, in1=lnv[:, l, 1, :], op=ALU.add),
                          reads=[r_vg, r_const], writes=[r_vtm])

            def st_conv():
                for c in range(3):
                    bk, rb = newbank()
                    for k in range(CK):
                        pr.op('pe', lambda e, c=c, k=k, bk=bk: e.matmul(bk[:, 0:n], lhsT=S3d[:, k * 3 + c, :], rhs=hglu[:, c, k:k + n],
                                                                       start=(k == 0), stop=(k == CK - 1)),
                              reads=[S3c[c], rS["S3"], r_hg], writes=[rb])
                    pr.op('act', lambda e, c=c, bk=bk: e.activation(out=hc[:, c, 0:n], in_=bk[:, 0:n], func=AF.Identity,
                                                                   bias=V(l, V_CB, c)),
                          reads=[rb, r_const], writes=[r_hc])

            def st_fwd():
                r_st = res("cst")
                tb = []
                for i in range(nt):
                    bk, rb = newbank()
                    tb.append((bk, rb))
                    for c in range(3):
                        pr.op('pe', lambda e, i=i, c=c, bk=bk: e.transpose(out=bk[:, c * 128:(c + 1) * 128],
                                                                          in_=hc[:, c, i * 128:(i + 1) * 128], identity=ident_f[:, :]),
                              reads=[r_hc, r_const], writes=[rb])
                    pr.op('dve', lambda e, i=i, bk=bk: e.bn_stats(out=cst6[:, i, :], in_=bk[:, 0:AW]), reads=[rb], writes=[r_st])
                    pr.op('dve', lambda e, i=i: e.bn_aggr(out=cmv[:, i, :], in_=cst6[:, i, :]), reads=[r_st], writes=[r_st])
                pr.op('act', lambda e: e.activation(out=crsv[:, 0:nt], in_=cmv[:, 0:nt, 1], func=AF.Ln, bias=epsv[:, 1:2]),
                      reads=[r_st, res("epsv")], writes=[r_st])
                pr.op('act', lambda e: e.activation(out=crsv[:, 0:nt], in_=crsv[:, 0:nt], func=AF.Exp, scale=-0.5), reads=[r_st], writes=[r_st])
                for i in range(nt):
                    bk, rb = tb[i]
                    pr.op('dve', lambda e, i=i, bk=bk: e.tensor_scalar(out=hn[:, i, :], in0=bk[:, 0:AW], scalar1=cmv[:, i, 0:1],
                                                                      scalar2=crsv[:, i:i + 1], op0=ALU.subtract, op1=ALU.mult),
                          reads=[rb, r_st], writes=[res("hn%d" % i)])

            def st_gp():
                for m in range(3):
                    bk, rb = newbank()
                    for i in range(nt):
                        cs = slice(i * 128, (i + 1) * 128)
                        pr.op('pe', lambda e, m=m, cs=cs, bk=bk: e.matmul(bk[:, cs], lhsT=ind[0:2, :], rhs=bsv[0:2, l, m, :],
                                                                         start=True, stop=False),
                              reads=[r_const], writes=[rb])
                        for hh in range(2):
                            h = 2 * m + hh
                            pr.op('pe', lambda e, i=i, h=h, hh=hh, cs=cs, bk=bk: e.matmul(
                                bk[hh * 64:(hh + 1) * 64, cs], lhsT=v_tm[:, i, h * 64:(h + 1) * 64], rhs=wsv[:, l, h, :],
                                start=False, stop=(hh == 1), tile_position=(0, hh * 64), skip_group_check=True),
                                reads=[r_vtm, res("WsT")], writes=[rb])
                    pr.op('dve', lambda e, m=m, bk=bk: e.tensor_tensor(out=y_sub[:, m, 0:n], in0=bk[:, 0:n], in1=ug[:, m, 0:n], op=ALU.mult),
                          reads=[rb, r_ug], writes=[r_y])
                for m2 in range(2):
                    bk, rb = newbank()
                    for i in range(nt):
                        cs = slice(i * 128, (i + 1) * 128)
                        first = (gt0 + i == HALO // 128)
                        for gg in range(2):
                            g = 2 * m2 + gg
                            A = apv[:, 2 if first else 0, g, :]
                            prev = p_prev[:, l, g * 64:(g + 1) * 64] if i == 0 else p_cur[:, i - 1, g * 64:(g + 1) * 64]
                            pr.op('pe', lambda e, i=i, g=g, gg=gg, A=A, cs=cs, bk=bk: e.matmul(
                                bk[gg * 64:(gg + 1) * 64, cs], lhsT=p_cur[:, i, g * 64:(g + 1) * 64], rhs=A,
                                start=True, stop=False, tile_position=(0, gg * 64), skip_group_check=True),
                                reads=[r_pc, r_const], writes=[rb])
                            pr.op('pe', lambda e, g=g, gg=gg, prev=prev, cs=cs, bk=bk: e.matmul(
                                bk[gg * 64:(gg + 1) * 64, cs], lhsT=prev, rhs=apv[:, 1, g, :],
                                start=False, stop=True, tile_position=(0, gg * 64), skip_group_check=True),
                                reads=[r_pc, r_pp, r_const], writes=[rb])
                    pr.op('act', lambda e, m2=m2, bk=bk: e.activation(out=pd[:, m2, 0:n], in_=bk[:, 0:n], func=AF.Copy),
                          reads=[rb], writes=[r_pd])
                    bk2, rb2 = newbank()
                    pr.op('pe', lambda e, m2=m2, bk2=bk2: e.matmul(bk2[:, 0:n], lhsT=pwv[:, l, m2, :], rhs=pd[:, m2, 0:n],
                                                                  start=True, stop=True),
                          reads=[r_pd, r_pw], writes=[rb2])
                    pr.op('dve', lambda e, m2=m2, bk2=bk2: e.tensor_scalar(
                        out=y_sub[:, 3 + m2, 0:n], in0=bk2[:, 0:n], scalar1=V(l, V_PB, m2), scalar2=V(l, V_PS, m2),
                        op0=ALU.add, op1=ALU.mult), reads=[rb2, r_const], writes=[r_y])
                pr.op('act', lambda e: e.activation(out=p_prev[:, l, :], in_=p_cur[:, nt - 1, :], func=AF.Copy),
                      reads=[r_pc], writes=[r_pp])

            def st_back():
                for i in range(nt):
                    b = i
                    r_hn = res("hn%d" % b)
                    bk2, rb2 = newbank()
                    bk2b = bk2.bitcast(BF16)
                    for c in range(3):
                        pr.op('pe', lambda e, b=b, c=c, bk2b=bk2b: e.transpose(out=bk2b[:, c * 128:(c + 1) * 128],
                                                                              in_=hn[:, b, c * 128:(c + 1) * 128], identity=ident_b[:, :]),
                              reads=[r_hn, res("identb")], writes=[rb2])
                    for c in range(3):
                        pr.op('act', lambda e, i=i, c=c, bk2b=bk2b: e.activation(
                            out=y_sub[:, 5 + c, i * 128:(i + 1) * 128], in_=bk2b[:, c * 128:(c + 1) * 128], func=AF.Silu,
                            scale=V(l, V_CLG, c), bias=V(l, V_CLB, c)), reads=[rb2, r_const], writes=[r_y])

            def st_wout():
                proj_add(j, c0, n, S2v, "S2", y_sub, r_y)
            return dict(P=st_P, conv=st_conv, fwd=st_fwd, gp=st_gp, back=st_back, wout=st_wout)

        def attn_stages(l, ip, j, c0, n):
            r_h = res("hp_%d" % j)
            r_q = res("q_sub")
            r_o = res("y_sub")
            r_K = res("K_fm")
            r_V = res("V_tm")

            def st_Q():
              for m in range(NCH):
                bk, rb = newbank()
                for k in range(NCH):
                    pr.op('pe', lambda e, m=m, k=k, bk=bk: e.matmul(bk[:, 0:n], lhsT=S4v[:, k, m * 128:(m + 1) * 128],
                                                                   rhs=h_pass[:, k, 2 + c0:2 + c0 + n], start=(k == 0), stop=(k == NCH - 1)),
                          reads=[rS["S4"], r_h], writes=[rb])
                pr.op('act', lambda e, m=m, bk=bk: e.activation(out=q_sub[:, m, 0:n], in_=bk[:, 0:n], func=AF.Copy),
                      reads=[rb], writes=[r_q])

            def scores(hd):
                b = hd % 2
                r_E = res("E%d" % b)
                for mc in range(2):
                    bk, rb = newbank()
                    for dc in range(2):
                        pr.op('pe', lambda e, hd=hd, mc=mc, dc=dc, bk=bk: e.matmul(
                            bk[:, 0:n], lhsT=K_fm[:, hd * 2 + dc, mc * 128:(mc + 1) * 128], rhs=q_sub[:, hd * 2 + dc, 0:n],
                            start=(dc == 0), stop=(dc == 1)), reads=[r_K, r_q], writes=[rb])
                    pr.op('act', lambda e, b=b, mc=mc, bk=bk: e.activation(out=Eb[:, b, mc, 0:n], in_=bk[:, 0:n], func=AF.Exp),
                          reads=[rb], writes=[r_E])

            def pv(hd):
                b = hd % 2
                r_E = res("E%d" % b)
                r_rd = res("rden%d" % b)
                bk, rb = newbank()
                for mc in range(2):
                    pr.op('pe', lambda e, b=b, mc=mc, bk=bk: e.matmul(bk[:, 0:n], lhsT=ones_1[:, :], rhs=Eb[:, b, mc, 0:n],
                                                                     start=(mc == 0), stop=(mc == 1)),
                          reads=[r_E, res("ones_1")], writes=[rb])
                pr.op('act', lambda e, b=b, bk=bk: e.activation(out=rden[:, b, 0:n], in_=bk[:, 0:n], func=AF.Ln), reads=[rb], writes=[r_rd])
                pr.op('act', lambda e, b=b: e.activation(out=rden[:, b, 0:n], in_=rden[:, b, 0:n], func=AF.Exp, scale=-1.0),
                      reads=[r_rd], writes=[r_rd])
                for dc in range(2):
                    bk, rb = newbank()
                    for mc in range(2):
                        pr.op('pe', lambda e, hd=hd, b=b, mc=mc, dc=dc, bk=bk: e.matmul(
                            bk[:, 0:n], lhsT=V_tm[:, mc, (hd * 2 + dc) * 128:(hd * 2 + dc + 1) * 128], rhs=Eb[:, b, mc, 0:n],
                            start=(mc == 0), stop=(mc == 1)), reads=[r_V, r_E], writes=[rb])
                    pr.op('dve', lambda e, hd=hd, b=b, dc=dc, bk=bk: e.tensor_tensor(
                        out=y_sub[:, hd * 2 + dc, 0:n], in0=bk[:, 0:n], in1=rden[:, b, 0:n], op=ALU.mult),
                        reads=[rb, r_rd], writes=[r_o])

            def st_heads_a():
                scores(0)
                scores(1)
                pv(0)

            def st_heads_b():
                for hd in range(1, 4):
                    if hd + 1 < 4:
                        scores(hd + 1)
                    pv(hd)

            def st_wo():
                proj_add(j, c0, n, S2v, "S2", y_sub, r_o)
            return dict(Q=st_Q, heads_a=st_heads_a, heads_b=st_heads_b, wo=st_wo)

        def ffn_tail_in(l):
            pr.op('act', lambda e: e.activation(out=h_pass[:, :, 0:2], in_=hpt[:, l, :, :], func=AF.Copy),
                  reads=[res("hp_tail")], writes=[res("hp_0")])

        def ffn_tail_out(l, P, jl):
            pr.op('act', lambda e: e.activation(out=hpt[:, l, :, :], in_=h_pass[:, :, P:P + 2], func=AF.Copy),
                  reads=[res("hp_%d" % jl)], writes=[res("hp_tail")])

        def ffn_up(l, gi, u, j, c0, n):
            j0, g = ffn_groups()[gi]
            slot = "S1" if gi % 2 == 0 else "S3"
            buf = S1 if gi % 2 == 0 else S3
            wg, wv_, wd = grp_views(buf)
            hp_reads = [res("hp_%d" % j)] + ([res("hp_%d" % (j - 1))] if j > 0 else [])
            gb = u % 2
            r_gt = res("gated%d" % gb)
            for jj in range(g):
                jc = j0 + jj
                b = jj % 2
                outs = []
                for which, wsrc, dstt, jcol in ((0, wg, tg, jc), (1, wv_, tv, NJ + jc)):
                    r_t = res("t%d_%d" % (which, b))
                    bk, rb = newbank()
                    for k in range(NCH):
                        pr.op('pe', lambda e, wsrc=wsrc, jj=jj, k=k, bk=bk: e.matmul(
                            bk[:, 0:n + 2], lhsT=wsrc[:, k, jj * 128:(jj + 1) * 128], rhs=h_pass[:, k, c0:c0 + n + 2],
                            start=(k == 0), stop=(k == NCH - 1)), reads=[rS[slot]] + hp_reads, writes=[rb])
                    pr.op('act', lambda e, dstt=dstt, b=b, jcol=jcol, bk=bk: e.activation(
                        out=dstt[:, b, 0:n], in_=bk[:, 2:n + 2], func=AF.Identity,
                        scale=V(l, V_FW, 2 * 44 + jcol), bias=V(l, V_FB, jcol)), reads=[rb, r_const], writes=[r_t])
                    for tap, sh in ((1, 1), (0, 0)):
                        pr.op('dve', lambda e, dstt=dstt, b=b, jcol=jcol, tap=tap, sh=sh, bk=bk: e.scalar_tensor_tensor(
                            out=dstt[:, b, 0:n], in0=bk[:, sh:sh + n], scalar=V(l, V_FW, tap * 44 + jcol),
                            in1=dstt[:, b, 0:n], op0=ALU.mult, op1=ALU.add), reads=[rb, r_t, r_const], writes=[r_t])
                    outs.append(r_t)
                r_sg = res("sg%d" % b)
                pr.op('act', lambda e, b=b: e.activation(out=sgb[:, b, 0:n], in_=tg[:, b, 0:n], func=AF.Silu),
                      reads=[outs[0]], writes=[r_sg])
                pr.op('pool', lambda e, b=b, gb=gb, jj=jj: e.tensor_tensor(out=gated[:, gb, jj, 0:n], in0=sgb[:, b, 0:n],
                                                                          in1=tv[:, b, 0:n], op=ALU.mult),
                      reads=[r_sg, outs[1]], writes=[r_gt])

        def ffn_down(l, gi, u, j, c0, n):
            j0, g = ffn_groups()[gi]
            slot = "S1" if gi % 2 == 0 else "S3"
            buf = S1 if gi % 2 == 0 else S3
            wg, wv_, wd = grp_views(buf)
            gb = u % 2
            r_gt = res("gated%d" % gb)
            for m in range(NCH):
                bk, rb = newbank()
                for jj in range(g):
                    pr.op('pe', lambda e, m=m, jj=jj, bk=bk: e.matmul(
                        bk[:, 0:n], lhsT=wd[:, jj, m * 128:(m + 1) * 128], rhs=gated[:, gb, jj, 0:n],
                        start=(jj == 0), stop=(jj == g - 1)), reads=[rS[slot], r_gt], writes=[rb])
                pr.op('dve', lambda e, m=m, bk=bk: e.tensor_tensor(out=x_fm[:, m, c0:c0 + n], in0=bk[:, 0:n],
                                                                  in1=x_fm[:, m, c0:c0 + n], op=ALU.add),
                      reads=[rb, xres(j, m)], writes=[xres(j, m)])

        out_ctr = [0]

        def final_store(ps0, j, c0, n):
            r_hf = res("hfin")
            for i in range(n // 128):
                t0 = ps0 + c0 + i * 128
                if t0 < HALO:
                    continue
                b = out_ctr[0] % 2
                out_ctr[0] += 1
                r_os = res("xs%d" % b)
                for half in range(2):
                    bk, rb = newbank()
                    for c in range(4):
                        pr.op('pe', lambda e, i=i, c=c, half=half, bk=bk: e.transpose(
                            out=bk[:, c * 128:(c + 1) * 128], in_=hfin[:, half * 4 + c, i * 128:(i + 1) * 128],
                            identity=ident_f[:, :]), reads=[r_hf, r_const], writes=[rb])
                    if half == 0:
                        pr.op('act', lambda e, b=b, bk=bk: e.activation(out=xs[:, b, 0:512], in_=bk[:, 0:512], func=AF.Copy),
                              reads=[rb], writes=[r_os])
                    else:
                        pr.op('dve', lambda e, b=b, bk=bk: e.tensor_copy(out=xs[:, b, 512:1024], in_=bk[:, 0:512]),
                              reads=[rb], writes=[r_os])
                pr.op('sp', lambda e, b=b, t0=t0: e.dma_start(out=out_d[t0 - HALO:t0 - HALO + 128, :], in_=xs[:, b, :]),
                      reads=[r_os], dsem=s_out[b])

        early_norm = [False]
        kv_done = [False]
        x_loaded = [False]
        lps = [(ip, l) for ip in range(len(passes)) for l in range(L)]
        ngr = len(ffn_groups())
        build_diag(0)
        for idx, (ip, l) in enumerate(lps):
            ps0, P = passes[ip]
            def subs_for(ip_, l_):
                sb_ = split_subs(passes[ip_][1])
                if ip_ == 0 and l_ >= 1 and SKIP_HALO_TILE:
                    sb_ = [(128, sb_[0][1] - 128)] + sb_[1:]
                return sb_
            subs = subs_for(ip, l)
            nxt = lps[idx + 1] if idx + 1 < len(lps) else None
            if l == 0 and not x_loaded[0]:
                load_x(ps0, P, subs)
            x_loaded[0] = False
            if do_attn and not kv_done[0]:
                kv_k(l)
                load_full("S4", S4v, wv_d[l])
            phase_begin("m")
            ns = len(subs)
            r_hs = res("h_sub")
            r_hp = res("h_pass")

            def mixnorm(j, part, l_=l, ip_=ip, subs_=subs):
                c0_, n_ = subs_[j]
                if part == 'A':
                    normA(None, j, c0_, n_, 0)
                else:
                    normB(None, j, c0_, n_, NVL * l_ + V_NMIX, h_sub, r_hs, 0, (ip_ == 0 and j == 0), 0)

            def attnorm(j, part):
                c0_, n_ = subs[j]
                if part == 'A':
                    normA(None, j, c0_, n_, 1)
                else:
                    normB(None, j, c0_, n_, NVL * l + V_NX, h_pass, res("hp_%d" % j), 2 + c0_, (ip == 0 and j == 0), 1)

            def ffnnorm(j, part):
                c0_, n_ = subs[j]
                if part == 'A':
                    normA(None, j, c0_, n_, 1)
                else:
                    if j == 0:
                        ffn_tail_in(l)
                    normB(None, j, c0_, n_, NVL * l + V_NFFN, h_pass, res("hp_%d" % j), 2 + c0_, (ip == 0 and j == 0), 1)
                    if j == ns - 1:
                        ffn_tail_out(l, P, j)

            assert do_mix and do_attn and do_ffn
            if not kv_done[0]:
                pass
            if not early_norm[0]:
                mixnorm(0, 'A')
                mixnorm(0, 'B')
            early_norm[0] = False
            MS = [mix_stages(l, ip, j, c0, n, (ps0 + c0) // 128) for j, (c0, n) in enumerate(subs)]
            if ns > 1:
                mixnorm(1, 'A')
            MS[0]['P']()
            for j in range(ns):
                if j + 1 < ns:
                    mixnorm(j + 1, 'B')
                MS[j]['conv']()
                MS[j]['fwd']()
                if j > 0:
                    attnorm(j - 1, 'B')
                MS[j]['gp']()
                MS[j]['back']()
                if j + 2 < ns:
                    mixnorm(j + 2, 'A')
                if j + 1 < ns:
                    MS[j + 1]['P']()
                elif ns > 1:
                    phase_begin("a")
                    AS = [attn_stages(l, ip, jj, cc, nn) for jj, (cc, nn) in enumerate(subs)]
                    AS[0]['Q']()
                MS[j]['wout']()
                attnorm(j, 'A')
                if j == 0 and not kv_done[0]:
                    kv_v(l)
                    store_kv(l)
                    load_full("S4", S4v, wq_d[l])
            kv_done[0] = False
            load_full("S2", S2v, wo_d[l])
            load_group(l, 0)
            load_group(l, 1)
            attnorm(ns - 1, 'B')
            if ns == 1:
                phase_begin("a")
                AS = [attn_stages(l, ip, j, c0, n) for j, (c0, n) in enumerate(subs)]
                AS[0]['Q']()
            for j in range(ns):
                AS[j]['heads_a']()
                if j > 0:
                    ffnnorm(j - 1, 'B')
                AS[j]['heads_b']()
                if j + 1 < ns:
                    AS[j + 1]['Q']()
                AS[j]['wo']()
                ffnnorm(j, 'A')
            if nxt is not None:
                load_full("S2", S2v, w_out_d[nxt[1]])
                if nxt[0] == 0:
                    load_full("S4", S4v, wk_d[nxt[1]])
                else:
                    load_kv(nxt[1])
                    load_full("S4", S4v, wq_d[nxt[1]])
                    kv_done[0] = True
            phase_begin("f")
            units = [(gi, j, c0, n) for gi in range(ngr) for j, (c0, n) in enumerate(subs)]

            def after_group(gi):
                if gi + 2 < ngr:
                    load_group(l, gi + 2)
                elif nxt is not None:
                    if gi % 2 == 0:
                        load_full("S1", S1v, w_in_d[nxt[1]])
                    else:
                        for c_ in range(3):
                            diag_pending.append(lambda c_=c_, l_=nxt[1]: diag_chunk(l_, c_))
            def tail_stages(pj):
                pc0_, pn_ = subs[pj]
                st = [lambda: normA(None, pj, pc0_, pn_, 0),
                      lambda: normB(None, pj, pc0_, pn_, VFIN, hfin, res("hfin"), 0, False, 0),
                      lambda: final_store(ps0, pj, pc0_, pn_)]
                if ip + 1 < len(passes):
                    nps0, nP = passes[ip + 1]
                    nsubs = split_subs(nP)
                    assert all(nsubs[q][0] + nsubs[q][1] == subs[q][0] + subs[q][1] for q in range(len(nsubs)))
                    if pj < len(nsubs):
                        st.append(lambda: load_x_sub(nps0, nsubs, pj))
                        if pj == 0:
                            def en():
                                mixnorm(0, 'A', 0, ip + 1, nsubs)
                                mixnorm(0, 'B', 0, ip + 1, nsubs)
                                early_norm[0] = True
                            st.append(en)
                    x_loaded[0] = True
                return st
            tail_q = []

            def drain(k):
                for _ in range(k):
                    if tail_q:
                        tail_q.pop(0)()

            prev = None
            for u, (gi, j, c0, n) in enumerate(units):
                if u == 0 and ns == 1:
                    ffnnorm(ns - 1, 'B')
                ffn_up(l, gi, u, j, c0, n)
                if u == 0 and ns > 1:
                    ffnnorm(ns - 1, 'B')
                drain(1)
                if prev is not None:
                    pu, (pgi, pj, pc0, pn) = prev
                    ffn_down(l, pgi, pu, pj, pc0, pn)
                    if pj == ns - 1:
                        after_group(pgi)
                    if pgi == ngr - 1 and pj == 0 and l + 1 < L:
                        mixnorm(0, 'A', l + 1, ip, subs_for(ip, l + 1))
                        mixnorm(0, 'B', l + 1, ip, subs_for(ip, l + 1))
                        early_norm[0] = True
                    if pgi == ngr - 1 and l == L - 1:
                        tail_q.extend(tail_stages(pj))
                        drain(1)
                if nxt is not None and nxt[0] == 0 and u == 2:
                    kv_k(nxt[1])
                    load_full("S4", S4v, wv_d[nxt[1]])
                if nxt is not None and nxt[0] == 0 and u == 5:
                    kv_v(nxt[1])
                    store_kv(nxt[1])
                    load_full("S4", S4v, wq_d[nxt[1]])
                    kv_done[0] = True
                prev = (u, (gi, j, c0, n))
            pu, (pgi, pj, pc0, pn) = prev
            ffn_down(l, pgi, pu, pj, pc0, pn)
            after_group(pgi)
            if l == L - 1:
                tail_q.extend(tail_stages(pj))
            drain(len(tail_q))

        with nc.Block() as block:
            pr.emit(block, sems, dsems)
    return nc


def _cols(v):
    v = np.asarray(v, np.float32).reshape(-1)
    return v.reshape(-1, 128).T


def make_vecs(inp, L, mask):
    NV = NVL * L + 9
    a = np.zeros((128, NV), np.float32)
    for l in range(L):
        b = NVL * l
        a[:, b + V_NMIX:b + V_NMIX + 8] = _cols(inp["norm_mix"][l])
        a[:, b + V_NX:b + V_NX + 8] = _cols(inp["norm_x"][l])
        a[:, b + V_NFFN:b + V_NFFN + 8] = _cols(inp["norm_ffn"][l])
        a[:, b + V_NMEM:b + V_NMEM + 8] = _cols(inp["norm_mem"][l])
        a[:, b + V_CB:b + V_CB + 3] = _cols(inp["conv_b"][l])
        a[:, b + V_CLG:b + V_CLG + 3] = _cols(inp["conv_ln_g"][l])
        a[:, b + V_CLB:b + V_CLB + 3] = _cols(inp["conv_ln_b"][l])
        a[:, b + V_PB:b + V_PB + 2] = _cols(inp["pool_b"][l])
        a[:, b + V_PS:b + V_PS + 2] = _cols(inp["pool_scale"][l])
        cw = np.asarray(inp["conv_w"][l], np.float32)
        for k in range(CK):
            a[:, b + V_CW + 3 * k:b + V_CW + 3 * k + 3] = _cols(cw[k])
        fw = np.asarray(inp["ffn_conv_w"][l], np.float32)
        for k in range(3):
            a[:, b + V_FW + 44 * k:b + V_FW + 44 * k + 44] = _cols(fw[k])
        a[:, b + V_FB:b + V_FB + 44] = _cols(inp["ffn_conv_b"][l])
    a[:, NVL * L:NVL * L + 8] = _cols(inp["norm_final"])
    a[:, NVL * L + 8] = mask
    return a


def make_consts(first_half):
    i = np.arange(128)
    cmask = ((i[None, :] // 64) <= (i[:, None] // 64)).astype(np.float32)
    ap = np.zeros((128, 3, 4, 128), np.float32)
    tp = i[:, None]
    t = i[None, :]
    for g, win in enumerate((2, 4, 8, 16)):
        diag = ((tp <= t) & (tp > t - win)).astype(np.float32) / win - (tp == t)
        off = ((tp - 128) > (t - win)).astype(np.float32) / win
        cnt = np.minimum(t + 1, win).astype(np.float32)
        fst = ((tp <= t) & (tp > t - win)).astype(np.float32) / cnt - (tp == t)
        ap[:, 0, g, :] = diag
        ap[:, 1, g, :] = off
        ap[:, 2, g, :] = fst if first_half else diag
    ind = np.zeros((2, 128), np.float32)
    ind[0, :64] = 1.0
    ind[1, 64:] = 1.0
    return cmask, ap.reshape(128, -1), ind, np.eye(128, dtype=np.float32)


def make_in_map(inp, xloc, memb, first_half, L):
    cmask, ap, ind, ident = make_consts(first_half)
    f = lambda k: np.ascontiguousarray(np.asarray(inp[k], np.float32))
    lnbc = np.stack([np.stack([np.asarray(inp["gmlp_ln_g"][l], np.float32), np.asarray(inp["gmlp_ln_b"][l], np.float32)])
                     for l in range(L)]).reshape(1, -1)
    lnbc = np.ascontiguousarray(np.broadcast_to(lnbc, (128, lnbc.shape[1])))
    return {
        "x_in": np.ascontiguousarray(xloc, np.float32), "mem": np.ascontiguousarray(memb, np.float32),
        "vecs": make_vecs(inp, L, 0.0 if first_half else 1.0), "lnbc": lnbc, "cmask": cmask, "apool": ap, "ind": ind,
        "ident": ident, "gmlp_ws": f("gmlp_ws"), "gmlp_bs": f("gmlp_bs"), "pool_w": f("pool_w"),
    }


def _pk(w):
    w = np.asarray(w, np.float32)
    Lw, R, N = w.shape
    return np.ascontiguousarray(w.reshape(Lw, R // 128, 128, N).transpose(0, 2, 1, 3).reshape(Lw, 128, (R // 128) * N))


def make_weights(inp, L):
    out = {k: _pk(inp[k][:L]) for k in ("w_in", "w_out", "wq", "wk", "wv", "wo")}
    groups = ffn_groups()
    gcols = 2 * NCH * 512 + GSZ * D
    wf = np.zeros((L, len(groups), 128, gcols), np.float32)
    w_up = np.asarray(inp["w_up"], np.float32)[:L]
    w_dn = np.asarray(inp["w_down"], np.float32)[:L]
    for gi, (j0, g) in enumerate(groups):
        for part, off in ((0, 0), (1, DFF)):
            blk = w_up[:, :, off + j0 * 128:off + (j0 + g) * 128]
            blk = blk.reshape(L, NCH, 128, g * 128).transpose(0, 2, 1, 3)
            dst = wf[:, gi, :, part * NCH * 512:(part + 1) * NCH * 512].reshape(L, 128, NCH, 512)
            dst[:, :, :, 0:g * 128] = blk
        blk = w_dn[:, j0 * 128:(j0 + g) * 128, :].reshape(L, g, 128, D).transpose(0, 2, 1, 3)
        dst = wf[:, gi, :, 2 * NCH * 512:].reshape(L, 128, GSZ, D)
        dst[:, :, 0:g, :] = blk
    out["wffn"] = wf
    return out


_NC_CACHE = {}

FULL_PASSES = [(0, 896), (896, 896), (1792, 896), (2688, 896), (3584, 768)]


def kernel(**inputs):
    x = np.asarray(inputs["x"], np.float32)
    mem = np.asarray(inputs["mem"], np.float32)
    B, S, _ = x.shape
    L = inputs["w_in"].shape[0]
    half = S // 2
    T = half + HALO
    key = (T, L)
    if key not in _NC_CACHE:
        _NC_CACHE[key] = build_program(T, FULL_PASSES, L)
    nc = _NC_CACHE[key]
    in_maps = []
    wts = make_weights(inputs, L)
    for core in range(8):
        b, hf = core // 2, core % 2
        if hf == 0:
            xloc = np.concatenate([np.zeros((HALO, D), np.float32), x[b, :half]], axis=0)
        else:
            xloc = x[b, half - HALO:]
        im = make_in_map(inputs, xloc, mem[b], hf == 0, L)
        im.update(wts)
        in_maps.append(im)
    res = run_bass_kernel_spmd(nc, in_maps, core_ids=list(range(8)))
    out = np.empty((B, S, D), np.float32)
    for core in range(8):
        b, hf = core // 2, core % 2
        out[b, hf * half:(hf + 1) * half] = res.results[core]["out"]
    return out
```

```python
import numpy as np
from contextlib import ExitStack
import concourse.bass as bass
import concourse.mybir as mybir
from concourse.bass_utils import run_bass_kernel_spmd

F32 = mybir.dt.float32
BF16 = mybir.dt.bfloat16
ALU = mybir.AluOpType
AF = mybir.ActivationFunctionType

D = 1024
NCH = 8
AW = 384
PROJ = 1792
DFF = 2816
NJ = 22
NMEM = 256
HALO = 256
CK = 31
RMS_EPS = 1e-6
LN_EPS = 1e-5
NVL = 314
GSZ = 4
SUBMAX = 384
NOMASK = False
SKIP_HALO_TILE = True

V_NMIX, V_NX, V_NFFN, V_NMEM = 0, 8, 16, 24
V_CB, V_CLG, V_CLB = 32, 35, 38
V_PB, V_PS = 41, 43
V_CW = 45
V_FW = 138
V_FB = 270


ENGS = ['pe', 'act', 'dve', 'pool', 'sp']
BLK = {'pe': 'tensor', 'act': 'scalar', 'dve': 'vector', 'pool': 'gpsimd', 'sp': 'sync'}


class Res:
    __slots__ = ('name', 'w', 'rd')

    def __init__(self, name):
        self.name = name
        self.w = None
        self.rd = {}


class DmaSem:
    def __init__(self, h):
        self.h = h
        self.count = 0


class Prog:
    def __init__(self):
        self.ops = {e: [] for e in ENGS}

    def op(self, eng, fn, reads=(), writes=(), dsem=None):
        idx = len(self.ops[eng])
        deps = []
        for r in reads:
            if r.w is not None:
                deps.append((r.w, True))
        for r in writes:
            if r.w is not None:
                deps.append((r.w, False))
            for t in r.rd.values():
                deps.append((t, False))
        waits = []
        for tok, raw in deps:
            if tok[0] == 'eng':
                _, e, i = tok
                if e == eng:
                    if eng == 'pe' or (not raw) or idx - i >= 3:
                        continue
                self.ops[e][i]['sig'] = True
            waits.append(tok)
        rec = dict(fn=fn, waits=waits, sig=False, dsem=None)
        if dsem is not None:
            dsem.count += 16
            rec['dsem'] = dsem
            mytok = ('dma', dsem, dsem.count)
        else:
            mytok = ('eng', eng, idx)
        self.ops[eng].append(rec)
        key = (mytok[0], mytok[1])
        for r in reads:
            r.rd[key] = mytok
        for r in writes:
            r.w = mytok
            r.rd = {}
        return mytok

    def emit(self, block, sems, final_waits):
        cnt = {}
        for e in ENGS:
            c = 0
            arr = []
            for o in self.ops[e]:
                if o['sig']:
                    c += 1
                arr.append(c)
            cnt[e] = arr
        for e in ENGS:
            def body(engobj, e=e):
                seen = {}
                for o in self.ops[e]:
                    for tok in o['waits']:
                        if tok[0] == 'eng':
                            sem = sems[tok[1]]
                            val = cnt[tok[1]][tok[2]]
                            key = ('eng', tok[1])
                        else:
                            sem = tok[1].h
                            val = tok[2]
                            key = ('dma', id(tok[1]))
                        if seen.get(key, 0) >= val:
                            continue
                        seen[key] = val
                        engobj.wait_ge(sem, val)
                    ins = o['fn'](engobj)
                    if o['sig']:
                        ins.then_inc(sems[e], 1)
                    if o['dsem'] is not None:
                        ins.then_inc(o['dsem'].h, 16)
                if e == 'sp':
                    for ds in final_waits:
                        if ds.count:
                            engobj.wait_ge(ds.h, ds.count)
            getattr(block, BLK[e])(body)


def split_subs(P):
    tiles = P // 128
    sizes = [min(3, tiles)]
    r = tiles - sizes[0]
    while r > 0:
        t = 3 if r >= 5 else (2 if r >= 2 else 1)
        sizes.append(t)
        r -= t
    subs = []
    c = 0
    for t in sizes:
        subs.append((c, t * 128))
        c += t * 128
    return subs


def ffn_groups():
    gs = []
    j = 0
    first = NJ % GSZ
    if first:
        gs.append((0, first))
        j = first
    while j < NJ:
        g = min(GSZ, NJ - j)
        gs.append((j, g))
        j += g
    return gs


def build_program(T, passes, L=2, do_mix=True, do_attn=True, do_ffn=True):
    PMAX = max(p[1] for p in passes)
    TOUT = T - HALO
    NV = NVL * L + 9
    nc = bass.Bass("TRN2", target_bir_lowering=False)

    def din(name, shape):
        return nc.dram_tensor(name, shape, F32, kind="ExternalInput").ap()

    x_in = din("x_in", [T, D])
    mem_d = din("mem", [NMEM, D])
    vecs_d = din("vecs", [128, NV])
    lnbc_d = din("lnbc", [128, L * 2 * AW])
    cmask_d = din("cmask", [128, 128])
    apool_d = din("apool", [128, 3 * 4 * 128])
    ind_d = din("ind", [2, 128])
    ident_d = din("ident", [128, 128])
    w_in_d = din("w_in", [L, 128, NCH * PROJ])
    w_out_d = din("w_out", [L, 128, NCH * D])
    wq_d = din("wq", [L, 128, NCH * D])
    wk_d = din("wk", [L, 128, NCH * D])
    wv_d = din("wv", [L, 128, NCH * D])
    wo_d = din("wo", [L, 128, NCH * D])
    NGR = len(ffn_groups())
    GCOLS = 2 * NCH * 512 + GSZ * D
    wffn_d = din("wffn", [L, NGR, 128, GCOLS])
    ws_d = din("gmlp_ws", [L, 6, 128, 128])
    bs_d = din("gmlp_bs", [L, 6, 128])
    pw_d = din("pool_w", [L, 4, 64, 64])
    out_d = nc.dram_tensor("out", [TOUT, D], F32, kind="ExternalOutput").ap()
    kvs_d = nc.dram_tensor("kv_scratch", [L, 128, 2 * NCH * NMEM], BF16, kind="Internal").ap()

    pr = Prog()
    es = ExitStack()
    with es:
        def sb(name, shape, dt):
            return es.enter_context(nc.sbuf_tensor("sb_" + name, shape, dt))

        def new_sem(name):
            return es.enter_context(nc.semaphore(name))

        sems = {e: new_sem("s_" + e) for e in ENGS}
        dsems = []

        def dsem(name):
            d = DmaSem(new_sem(name))
            dsems.append(d)
            return d

        x_fm = sb("x_fm", [128, NCH, PMAX], F32)
        ident_f = sb("ident_f", [128, 128], F32)
        ident_b = sb("ident_b", [128, 128], BF16)
        ones_m = sb("ones_m", [128, 128], BF16)
        ones_1 = sb("ones_1", [128, 128], BF16)
        vecs = sb("vecs", [128, NV], F32)
        lnbc = sb("lnbc", [128, L * 2 * AW], BF16)
        cmask = sb("cmask", [128, 128], F32)
        apool = sb("apool", [128, 3 * 4 * 128], BF16)
        ind = sb("ind", [2, 128], BF16)
        bsrow = sb("bsrow", [2, L * 3 * 128], BF16)
        poolW = sb("poolW", [128, L * 2 * 128], BF16)
        WsT = sb("WsT", [128, L * 6 * 128], BF16)
        S1 = sb("S1", [128, NCH * PROJ], BF16)
        S2 = sb("S2", [128, NCH * D], BF16)
        S3 = sb("S3", [128, 2 * NCH * 512 + GSZ * D], BF16)
        S4 = sb("S4", [128, NCH * D], BF16)
        K_fm = sb("K_fm", [128, NCH, NMEM], BF16)
        V_tm = sb("V_tm", [128, 2, D], BF16)
        p_prev = sb("p_prev", [128, L, 256], BF16)
        hg_tail = sb("hg_tail", [128, L, 3 * 30], BF16)
        hp_tail = sb("hp_tail", [128, L, NCH * 2], BF16)
        sq = sb("sq", [128, NCH, SUBMAX], BF16)
        rstd = sb("rstd", [128, SUBMAX], F32)
        xs = sb("xs", [128, 2, D], F32)
        memn = sb("memn", [128, NCH, NMEM], BF16)
        st6 = sb("st6", [128, 2, 6], F32)
        mv = sb("mv", [128, 2, 2], F32)
        rsv = sb("rsv", [128, 2], F32)
        ssm = sb("ssm", [128, 2], F32)
        vst6 = sb("vst6", [128, 3, 6], F32)
        vmv = sb("vmv", [128, 3, 2], F32)
        vrs = sb("vrs", [128, 3], F32)
        cst6 = sb("cst6", [128, 3, 6], F32)
        cmv = sb("cmv", [128, 3, 2], F32)
        crsv = sb("crsv", [128, 3], F32)
        epsv = sb("epsv", [128, 2], F32)
        h_pass = sb("h_pass", [128, NCH, PMAX + 2], BF16)
        ARENA_BYTES = 35584
        arena_f = sb("arena", [128, ARENA_BYTES // 4], F32)
        arena_b = arena_f.bitcast(BF16)
        AR = {}

        def carve(name, phase, off, shape, dt):
            nel = 1
            for d_ in shape[1:]:
                nel *= d_
            esz = 4 if dt == F32 else 2
            assert off % 4 == 0 and off + nel * esz <= ARENA_BYTES, name
            base = arena_f if dt == F32 else arena_b
            v = base[:, off // esz:off // esz + nel]
            if len(shape) == 3:
                v = v.rearrange("p (a b) -> p a b", a=shape[1])
            elif len(shape) == 4:
                v = v.rearrange("p (a b c) -> p a b c", a=shape[1], b=shape[2])
            AR[name] = (phase, off, off + nel * esz)
            return v

        S_ = SUBMAX
        h_sub = carve("h_sub", "ma", 0, [128, NCH, S_], BF16)
        y_sub = carve("y_sub", "ma", 6144, [128, NCH, S_], BF16)
        ug = carve("ug", "m", 12288, [128, 3, S_], BF16)
        vg = carve("vg", "m", 14592, [128, 2, AW], F32)
        v_tm = carve("v_tm", "m", 17664, [128, 3, AW], BF16)
        p_cur = carve("p_cur", "m", 19968, [128, 3, 256], BF16)
        sig = carve("sig", "m", 21504, [128, 2, S_], F32)
        hglu = carve("hglu", "m", 24576, [128, 3, 30 + S_], BF16)
        pd = carve("pd", "m", 27136, [128, 2, S_], BF16)
        hc = carve("hc", "m", 28672, [128, 3, S_], F32)
        hn = carve("hn", "m", 33280, [128, 3, AW], BF16)
        q_sub = carve("q_sub", "a", 12288, [128, NCH, S_], BF16)
        Eb = carve("Eb", "a", 18432, [128, 2, 2, S_], BF16)
        rden = carve("rden", "a", 21504, [128, 2, S_], F32)
        tg = carve("tg", "f", 6144, [128, 2, S_], F32)
        tv = carve("tv", "f", 9216, [128, 2, S_], F32)
        sgb = carve("sgb", "f", 12288, [128, 2, S_], F32)
        gated = carve("gated", "f", 15360, [128, 2, GSZ, S_], BF16)
        hfin = carve("hfin", "f", 21504, [128, NCH, S_], F32)
        wstage = carve("wstage", "s", 28672, [128, 6, 128], F32)

        banks = [es.enter_context(nc.psum_tensor("ps%d" % i, [128, 512], F32)) for i in range(8)]
        r_bank = [Res("bank%d" % i) for i in range(8)]
        bank_ctr = [0]

        def newbank():
            i = bank_ctr[0] % 8
            bank_ctr[0] += 1
            return banks[i], r_bank[i]

        R = {}

        def res(name):
            if name not in R:
                R[name] = Res(name)
            return R[name]

        RES_VIEW = {"vg0": "vg", "vg1": "vg", "sig0": "sig", "sig1": "sig", "hn0": "hn", "hn1": "hn", "hn2": "hn",
                    "E0": "Eb", "E1": "Eb", "rden0": "rden", "rden1": "rden", "t0_0": "tg", "t0_1": "tg",
                    "t1_0": "tv", "t1_1": "tv", "sg0": "sgb", "sg1": "sgb", "gated0": "gated", "gated1": "gated"}
        PHASE_RES = {
            "m": ["h_sub", "y_sub", "ug", "vg0", "vg1", "v_tm", "p_cur", "sig0", "sig1", "hglu", "pd", "hc", "hn0", "hn1", "hn2"],
            "a": ["h_sub", "y_sub", "q_sub", "E0", "E1", "rden0", "rden1"],
            "f": ["t0_0", "t0_1", "t1_0", "t1_1", "sg0", "sg1", "gated0", "gated1", "hfin"],
            "s": ["wstage"],
        }

        def _merge(dst, tok):
            key = (tok[0], tok[1])
            old = dst.rd.get(key)
            if old is None or old[2] < tok[2]:
                dst.rd[key] = tok

        def phase_begin(ph):
            mine = PHASE_RES[ph]
            for dn in mine:
                dv = RES_VIEW.get(dn, dn)
                _, dlo, dhi = AR[dv]
                dst = res(dn)
                for sn, src in list(R.items()):
                    sv = RES_VIEW.get(sn, sn)
                    if sv not in AR or sv == dv or sn in mine:
                        continue
                    _, slo, shi = AR[sv]
                    if slo < dhi and dlo < shi:
                        if src.w is not None:
                            _merge(dst, src.w)
                        for t in src.rd.values():
                            _merge(dst, t)

        r_const = res("const")
        s_misc = dsem("d_misc")
        s_pw = dsem("d_pw")
        s_ws = dsem("d_ws")
        s_xs = [dsem("d_xs0"), dsem("d_xs1")]
        s_S = {k: dsem("d_" + k) for k in ("S1", "S2", "S3", "S4")}
        s_out = [dsem("d_out0"), dsem("d_out1")]
        s_mem = dsem("d_mem")

        def V(l, off, c=0):
            col = NVL * l + off + c
            return vecs[:, col:col + 1]

        VFIN = NVL * L
        VMASK = NVL * L + 8

        S1v = S1[:, :].rearrange("p (k n) -> p k n", k=NCH)
        S2v = S2[:, :].rearrange("p (k n) -> p k n", k=NCH)
        S4v = S4[:, :].rearrange("p (k n) -> p k n", k=NCH)
        S3d = S3[:, 0:93 * 128].rearrange("p (t n) -> p t n", n=128)

        def grp_views(buf):
            wg = buf[:, 0:NCH * 512].rearrange("p (k n) -> p k n", k=NCH)
            wv_ = buf[:, NCH * 512:2 * NCH * 512].rearrange("p (k n) -> p k n", k=NCH)
            wd = buf[:, 2 * NCH * 512:2 * NCH * 512 + GSZ * D].rearrange("p (j n) -> p j n", j=GSZ)
            return wg, wv_, wd

        apv = apool[:, :].rearrange("p (a g n) -> p a g n", a=3, g=4)
        lnv = lnbc[:, :].rearrange("p (l t n) -> p l t n", l=L, t=2)
        bsv = bsrow[:, :].rearrange("p (l m n) -> p l m n", l=L, m=3)
        pwv = poolW[:, :].rearrange("p (l m n) -> p l m n", l=L, m=2)
        wsv = WsT[:, :].rearrange("p (l h n) -> p l h n", l=L, h=6)
        hgt = hg_tail[:, :, :].rearrange("p l (c n) -> p l c n", c=3)
        hpt = hp_tail[:, :, :].rearrange("p l (c n) -> p l c n", c=NCH)

        rS = {k: res(k) for k in ("S1", "S2", "S3", "S4")}

        SLOTBUF = {"S1": S1, "S2": S2, "S4": S4}

        def load_full(slot, view, src):
            ncol = src.shape[-1]
            pr.op('pool', lambda e: e.dma_start(out=SLOTBUF[slot][:, 0:ncol].rearrange("p (a b) -> p a b", b=2048),
                                                in_=src.rearrange("p (a b) -> p a b", b=2048)),
                  writes=[rS[slot]], dsem=s_S[slot])

        load_full("S4", S4v, wk_d[0])
        load_full("S1", S1v, w_in_d[0])
        load_full("S2", S2v, w_out_d[0])

        pr.op('sp', lambda e: e.dma_start(out=ident_f[:, :], in_=ident_d), writes=[r_const], dsem=s_misc)
        pr.op('sp', lambda e: e.dma_start(out=vecs[:, :], in_=vecs_d), writes=[r_const], dsem=s_misc)
        pr.op('sp', lambda e: e.dma_start(out=cmask[:, :], in_=cmask_d), writes=[r_const], dsem=s_misc)
        pr.op('pool', lambda e: e.dma_start(out=lnbc[:, :], in_=lnbc_d), writes=[r_const], dsem=s_misc)
        pr.op('pool', lambda e: e.dma_start(out=apool[:, :], in_=apool_d), writes=[r_const], dsem=s_misc)
        pr.op('pool', lambda e: e.dma_start(out=ind[:, :], in_=ind_d), writes=[r_const], dsem=s_misc)
        pr.op('pool', lambda e: e.dma_start(
            out=bsrow[:, :].rearrange("p (l m n) -> p l m n", l=L, m=3),
            in_=bs_d.rearrange("l (m r) i -> r l m i", r=2)), writes=[r_const], dsem=s_misc)
        r_pw = res("poolW")
        pr.op('dve', lambda e: e.memset(poolW[:, :], 0.0), writes=[r_pw])
        for l in range(L):
            for g in range(4):
                gg, m2 = g % 2, g // 2
                pr.op('pool', lambda e, l=l, g=g, gg=gg, m2=m2: e.dma_start(
                    out=pwv[gg * 64:(gg + 1) * 64, l, m2, gg * 64:(gg + 1) * 64], in_=pw_d[l, g]),
                    writes=[r_pw], dsem=s_pw)
        pr.op('dve', lambda e: e.tensor_copy(out=ident_b[:, :], in_=ident_f[:, :]), reads=[r_const], writes=[res("identb")])
        pr.op('dve', lambda e: e.memset(ones_m[:, :], 1.0 / 1024.0), writes=[res("ones_m")])
        pr.op('dve', lambda e: e.memset(ones_1[:, :], 1.0), writes=[res("ones_1")])
        pr.op('dve', lambda e: e.memset(epsv[:, 0:1], RMS_EPS), writes=[res("epsv")])
        pr.op('dve', lambda e: e.memset(epsv[:, 1:2], LN_EPS), writes=[res("epsv")])
        pr.op('dve', lambda e: e.memset(p_prev[:, :, :], 0.0), writes=[res("p_prev")])
        pr.op('dve', lambda e: e.memset(hg_tail[:, :, :], 0.0), writes=[res("hg_tail")])
        pr.op('dve', lambda e: e.memset(hp_tail[:, :, :], 0.0), writes=[res("hp_tail")])
        phase_begin("s")
        r_wst = res("wstage")
        for l in range(L):
            pr.op('sp', lambda e, l=l: e.dma_start(out=wstage[:, :, :], in_=ws_d[l].rearrange("h i j -> i h j")),
                  writes=[r_wst], dsem=s_ws)
            for h in range(6):
                pr.op('dve', lambda e, h=h: e.tensor_tensor(out=wstage[:, h, :], in0=wstage[:, h, :], in1=cmask[:, :], op=ALU.mult),
                      reads=[r_wst, r_const], writes=[r_wst])
            for h0 in range(0, 6, 3):
                bk, rb = newbank()
                for h in range(h0, h0 + 3):
                    pr.op('pe', lambda e, h=h, bk=bk, h0=h0: e.transpose(
                        out=bk[:, (h - h0) * 128:(h - h0 + 1) * 128], in_=wstage[:, h, :], identity=ident_f[:, :]),
                        reads=[r_wst, r_const], writes=[rb])
                pr.op('act', lambda e, l=l, h0=h0, bk=bk: e.activation(
                    out=wsv[:, l, h0:h0 + 3, :], in_=bk[:, 0:384].rearrange("p (h n) -> p h n", h=3), func=AF.Copy),
                    reads=[rb], writes=[res("WsT")])

        def load_group(l, gi):
            slot = "S1" if gi % 2 == 0 else "S3"
            buf = S1 if gi % 2 == 0 else S3
            pr.op('pool', lambda e: e.dma_start(out=buf[:, 0:GCOLS].rearrange("p (a b) -> p a b", b=2048),
                                                in_=wffn_d[l, gi].rearrange("p (a b) -> p a b", b=2048)),
                  writes=[rS[slot]], dsem=s_S[slot])

        S3c = [res("S3c%d" % c) for c in range(3)]

        def diag_chunk(l, c):
            for k in range(CK):
                t = k * 3 + c
                wr = [S3c[c], rS["S3"]] if k == 0 else [S3c[c]]
                if c <= 1:
                    pr.op('dve', lambda e, t=t, l=l: e.tensor_scalar(
                        out=S3d[:, t, :], in0=ident_b[:, :], scalar1=V(l, V_CW, t), scalar2=None, op0=ALU.mult),
                        reads=[res("identb"), r_const], writes=wr)
                else:
                    pr.op('act', lambda e, t=t, l=l: e.activation(out=S3d[:, t, :], in_=ident_b[:, :], func=AF.Identity,
                                                                 scale=V(l, V_CW, t)),
                          reads=[res("identb"), r_const], writes=wr)

        def build_diag(l):
            for c in range(3):
                diag_chunk(l, c)

        diag_pending = []

        def diag_hook():
            if diag_pending:
                diag_pending.pop(0)()

        def xres(j, m):
            return res("x_%d_%d" % (j, m))

        xs_b = xs.bitcast(BF16)
        sq2 = xs_b[:, 0, :].rearrange("p (a b) -> p a b", a=NCH)[:, :, 0:SUBMAX] if False else None
        xs_flat_b = xs_b[:, :, :].rearrange("p a b -> p (a b)")
        sq2 = xs_flat_b[:, 0:NCH * SUBMAX].rearrange("p (a b) -> p a b", a=NCH)
        xs_flat_f = xs[:, :, :].rearrange("p a b -> p (a b)")
        rstd2 = xs_flat_f[:, (NCH * SUBMAX) // 2:(NCH * SUBMAX) // 2 + SUBMAX]
        NB = {0: (sq, rstd, [res("sq")], [res("rstd")]),
              1: (sq2, rstd2, [res("xs0"), res("xs1")], [res("xs0"), res("xs1")])}
        norm_banks = {}

        def normA(key, j, c0, n, bs=0):
            sq_, rstd_, r_sq, r_rs = NB[bs]
            for m in range(NCH):
                pr.op('act', lambda e, m=m: e.activation(out=sq_[:, m, 0:n], in_=x_fm[:, m, c0:c0 + n], func=AF.Square),
                      reads=[xres(j, m)], writes=r_sq)

        def normB(key, j, c0, n, gcol, dst, dst_res, dst_off=0, mask=False, bs=0):
            sq_, rstd_, r_sq, r_rs = NB[bs]
            bk, rb = newbank()
            for m in range(NCH):
                pr.op('pe', lambda e, m=m, bk=bk: e.matmul(bk[:, 0:n], lhsT=ones_m[:, :], rhs=sq_[:, m, 0:n],
                                                          start=(m == 0), stop=(m == NCH - 1)),
                      reads=r_sq + [res("ones_m")], writes=[rb])
            pr.op('act', lambda e, bk=bk: e.activation(out=rstd_[:, 0:n], in_=bk[:, 0:n], func=AF.Ln, bias=epsv[:, 0:1]),
                  reads=[rb, res("epsv")], writes=r_rs)
            pr.op('act', lambda e: e.activation(out=rstd_[:, 0:n], in_=rstd_[:, 0:n], func=AF.Exp, scale=-0.5), reads=r_rs, writes=r_rs)
            for m in range(NCH):
                pr.op('dve', lambda e, m=m: e.scalar_tensor_tensor(
                    out=dst[:, m, dst_off:dst_off + n], in0=x_fm[:, m, c0:c0 + n], scalar=vecs[:, gcol + m:gcol + m + 1],
                    in1=rstd_[:, 0:n], op0=ALU.mult, op1=ALU.mult),
                    reads=[xres(j, m), r_const] + r_rs, writes=[dst_res])
            if mask and not NOMASK:
                mw = HALO - c0
                pr.op('dve', lambda e: e.tensor_scalar(
                    out=dst[:, :, dst_off:dst_off + mw], in0=dst[:, :, dst_off:dst_off + mw],
                    scalar1=vecs[:, VMASK:VMASK + 1], scalar2=None, op0=ALU.mult),
                    reads=[dst_res, r_const], writes=[dst_res])

        def norm(j, c0, n, gcol, dst, dst_res, dst_off=0, mask=False, bs=0):
            normA(None, j, c0, n, bs)
            normB(None, j, c0, n, gcol, dst, dst_res, dst_off, mask, bs)

        def proj_add(j, c0, n, Wv, slot, src, src_res):
            for m in range(NCH):
                bk, rb = newbank()
                for k in range(NCH):
                    pr.op('pe', lambda e, m=m, k=k, bk=bk: e.matmul(bk[:, 0:n], lhsT=Wv[:, k, m * 128:(m + 1) * 128],
                                                                   rhs=src[:, k, 0:n], start=(k == 0), stop=(k == NCH - 1)),
                          reads=[rS[slot], src_res], writes=[rb])
                pr.op('dve', lambda e, m=m, bk=bk: e.tensor_tensor(out=x_fm[:, m, c0:c0 + n], in0=bk[:, 0:n],
                                                                  in1=x_fm[:, m, c0:c0 + n], op=ALU.add),
                      reads=[rb, xres(j, m)], writes=[xres(j, m)])

        def ln_rows(src_ap, k, width):
            rk = res("stat%d" % k)
            pr_reads = []
            return rk

        def load_x_sub(ps0, subs, j):
            c0j, nj = subs[j]
            for ti in range(c0j // 128, (c0j + nj) // 128):
                b = ti % 2
                t0 = ps0 + ti * 128
                rx = res("xs%d" % b)
                pr.op('sp', lambda e, b=b, t0=t0: e.dma_start(out=xs[:, b, :], in_=x_in[t0:t0 + 128, :]),
                      writes=[rx], dsem=s_xs[b])
                for half in range(2):
                    bk, rb = newbank()
                    for c in range(4):
                        pr.op('pe', lambda e, b=b, c=c, half=half, bk=bk: e.transpose(
                            out=bk[:, c * 128:(c + 1) * 128], in_=xs[:, b, (half * 4 + c) * 128:(half * 4 + c + 1) * 128],
                            identity=ident_f[:, :]), reads=[rx, r_const], writes=[rb])
                    eng = 'act' if half == 0 else 'dve'
                    outap = x_fm[:, half * 4:half * 4 + 4, ti * 128:(ti + 1) * 128]
                    inap = bk[:, 0:512].rearrange("p (c t) -> p c t", c=4)
                    if eng == 'act':
                        f = lambda e, outap=outap, inap=inap: e.activation(out=outap, in_=inap, func=AF.Copy)
                    else:
                        f = lambda e, outap=outap, inap=inap: e.tensor_copy(out=outap, in_=inap)
                    pr.op(eng, f, reads=[rb], writes=[xres(j, half * 4 + c) for c in range(4)])

        def load_x(ps0, P, subs):
            for j in range(len(subs)):
                load_x_sub(ps0, subs, j)

        def kv_k(l):
            rxs = [res("xs0"), res("xs1")]
            pr.op('sp', lambda e: e.dma_start(out=xs[:, :, :], in_=mem_d.rearrange("(t p) d -> p t d", p=128)),
                  reads=[], writes=rxs, dsem=s_mem)
            r_ss = res("ssm")
            sqf = sq[:, :, :].rearrange("p a b -> p (a b)")
            for t in range(2):
                pr.op('act', lambda e, t=t: e.activation(out=sqf[:, 0:D], in_=xs[:, t, :], func=AF.Square,
                                                        accum_out=ssm[:, t:t + 1]),
                      reads=rxs, writes=[res("sq"), r_ss])
            pr.op('act', lambda e: e.activation(out=ssm[:, :], in_=ssm[:, :], func=AF.Ln, scale=1.0 / D, bias=epsv[:, 0:1]),
                  reads=[r_ss, res("epsv")], writes=[r_ss])
            pr.op('act', lambda e: e.activation(out=ssm[:, :], in_=ssm[:, :], func=AF.Exp, scale=-0.5), reads=[r_ss], writes=[r_ss])
            for t in range(2):
                pr.op('dve', lambda e, t=t: e.tensor_scalar(out=xs[:, t, :], in0=xs[:, t, :], scalar1=ssm[:, t:t + 1],
                                                           scalar2=None, op0=ALU.mult),
                      reads=rxs + [r_ss], writes=rxs)
            r_memn = res("memn")
            for k in range(NCH):
                bk, rb = newbank()
                for t in range(2):
                    pr.op('pe', lambda e, k=k, t=t, bk=bk: e.transpose(
                        out=bk[:, t * 128:(t + 1) * 128], in_=xs[:, t, k * 128:(k + 1) * 128], identity=ident_f[:, :]),
                        reads=rxs + [r_const], writes=[rb])
                pr.op('act', lambda e, k=k, bk=bk: e.activation(out=memn[:, k, :], in_=bk[:, 0:256], func=AF.Identity,
                                                               scale=V(l, V_NMEM, k)),
                      reads=[rb, r_const], writes=[r_memn])
            for m in range(NCH):
                bk, rb = newbank()
                for k in range(NCH):
                    pr.op('pe', lambda e, m=m, k=k, bk=bk: e.matmul(bk[:, 0:NMEM], lhsT=S4v[:, k, m * 128:(m + 1) * 128],
                                                                   rhs=memn[:, k, :], start=(k == 0), stop=(k == NCH - 1)),
                          reads=[rS["S4"], r_memn], writes=[rb])
                pr.op('act', lambda e, m=m, bk=bk: e.activation(out=K_fm[:, m, :], in_=bk[:, 0:NMEM], func=AF.Identity, scale=1.0 / 16.0),
                      reads=[rb], writes=[res("K_fm")])

        def kv_v(l):
            r_memn = res("memn")
            for t in range(2):
                for half in range(2):
                    bk, rb = newbank()
                    for k in range(NCH):
                        pr.op('pe', lambda e, t=t, half=half, k=k, bk=bk: e.matmul(
                            bk[:, 0:512], lhsT=memn[:, k, t * 128:(t + 1) * 128], rhs=S4v[:, k, half * 512:(half + 1) * 512],
                            start=(k == 0), stop=(k == NCH - 1)), reads=[rS["S4"], r_memn], writes=[rb])
                    pr.op('act', lambda e, t=t, half=half, bk=bk: e.activation(
                        out=V_tm[:, t, half * 512:(half + 1) * 512], in_=bk[:, 0:512], func=AF.Copy),
                        reads=[rb], writes=[res("V_tm")])

        s_kvs = dsem("d_kvs")
        s_kvl = dsem("d_kvl")
        KVW = NCH * NMEM

        def store_kv(l):
            pr.op('sp', lambda e: e.dma_start(out=kvs_d[l][:, 0:KVW], in_=K_fm[:, :, :].rearrange("p a b -> p (a b)")),
                  reads=[res("K_fm")], writes=[res("kvs%d" % l)], dsem=s_kvs)
            pr.op('sp', lambda e: e.dma_start(out=kvs_d[l][:, KVW:2 * KVW], in_=V_tm[:, :, :].rearrange("p a b -> p (a b)")),
                  reads=[res("V_tm")], writes=[res("kvs%d" % l)], dsem=s_kvs)

        def load_kv(l):
            pr.op('sp', lambda e: e.dma_start(out=K_fm[:, :, :].rearrange("p a b -> p (a b)"), in_=kvs_d[l][:, 0:KVW]),
                  reads=[res("kvs%d" % l)], writes=[res("K_fm")], dsem=s_kvl)
            pr.op('sp', lambda e: e.dma_start(out=V_tm[:, :, :].rearrange("p a b -> p (a b)"), in_=kvs_d[l][:, KVW:2 * KVW]),
                  reads=[res("kvs%d" % l)], writes=[res("V_tm")], dsem=s_kvl)

        def mix_stages(l, ip, j, c0, n, gt0):
            nt = n // 128
            r_h = res("h_sub")
            W = S1v
            rW = rS["S1"]
            r_ug = res("ug")
            r_vtm = res("v_tm")
            r_pc = res("p_cur")
            r_hg = res("hglu")
            r_hgt = res("hg_tail")
            r_y = res("y_sub")
            r_pp = res("p_prev")
            r_pd = res("pd")
            r_hc = res("hc")

            def st_P():
                VG = [(vg[:, 0, :], res("vg0")), (vg[:, 1, :], res("vg1")), (hc[:, 0, :], res("hc"))]
                r_vs = res("vst")
                for i in range(nt):
                    vb, r_vg = VG[i]
                    bk, rb = newbank()
                    for k in range(NCH):
                        pr.op('pe', lambda e, i=i, k=k, bk=bk: e.matmul(bk[:, 0:AW], lhsT=h_sub[:, k, i * 128:(i + 1) * 128],
                                                                       rhs=W[:, k, AW:2 * AW], start=(k == 0), stop=(k == NCH - 1)),
                              reads=[rW, r_h], writes=[rb])
                    pr.op('act', lambda e, vb=vb, bk=bk: e.activation(out=vb, in_=bk[:, 0:AW], func=AF.Gelu),
                          reads=[rb], writes=[r_vg])
                    pr.op('dve', lambda e, i=i, vb=vb: e.bn_stats(out=vst6[:, i, :], in_=vb), reads=[r_vg], writes=[r_vs])
                    pr.op('dve', lambda e, i=i: e.bn_aggr(out=vmv[:, i, :], in_=vst6[:, i, :]), reads=[r_vs], writes=[r_vs])
                diag_hook()
                for m in range(3):
                    bk, rb = newbank()
                    for k in range(NCH):
                        pr.op('pe', lambda e, m=m, k=k, bk=bk: e.matmul(bk[:, 0:n], lhsT=W[:, k, m * 128:(m + 1) * 128],
                                                                       rhs=h_sub[:, k, 0:n], start=(k == 0), stop=(k == NCH - 1)),
                              reads=[rW, r_h], writes=[rb])
                    pr.op('act', lambda e, m=m, bk=bk: e.activation(out=ug[:, m, 0:n], in_=bk[:, 0:n], func=AF.Gelu),
                          reads=[rb], writes=[r_ug])
                for i in range(nt):
                    bk, rb = newbank()
                    for k in range(NCH):
                        pr.op('pe', lambda e, i=i, k=k, bk=bk: e.matmul(bk[:, 0:256], lhsT=h_sub[:, k, i * 128:(i + 1) * 128],
                                                                       rhs=W[:, k, 768:1024], start=(k == 0), stop=(k == NCH - 1)),
                              reads=[rW, r_h], writes=[rb])
                    pr.op('act', lambda e, i=i, bk=bk: e.activation(out=p_cur[:, i, :], in_=bk[:, 0:256], func=AF.Copy),
                          reads=[rb], writes=[r_pc])
                pr.op('act', lambda e: e.activation(out=hglu[:, :, 0:30], in_=hgt[:, l, :, :], func=AF.Copy),
                      reads=[r_hgt], writes=[r_hg])
                for c in range(3):
                    b = c % 2
                    r_sig = res("sig%d" % b)
                    bk, rb = newbank()
                    for k in range(NCH):
                        pr.op('pe', lambda e, c=c, k=k, bk=bk: e.matmul(bk[:, 0:n], lhsT=W[:, k, 1408 + c * 128:1408 + (c + 1) * 128],
                                                                       rhs=h_sub[:, k, 0:n], start=(k == 0), stop=(k == NCH - 1)),
                              reads=[rW, r_h], writes=[rb])
                    pr.op('act', lambda e, b=b, bk=bk: e.activation(out=sig[:, b, 0:n], in_=bk[:, 0:n], func=AF.Sigmoid),
                          reads=[rb], writes=[r_sig])
                    bk2, rb2 = newbank()
                    for k in range(NCH):
                        pr.op('pe', lambda e, c=c, k=k, bk2=bk2: e.matmul(bk2[:, 0:n], lhsT=W[:, k, 1024 + c * 128:1024 + (c + 1) * 128],
                                                                         rhs=h_sub[:, k, 0:n], start=(k == 0), stop=(k == NCH - 1)),
                              reads=[rW, r_h], writes=[rb2])
                    pr.op('dve', lambda e, c=c, b=b, bk2=bk2: e.tensor_tensor(out=hglu[:, c, 30:30 + n], in0=bk2[:, 0:n],
                                                                             in1=sig[:, b, 0:n], op=ALU.mult),
                          reads=[rb2, r_sig], writes=[r_hg])
                pr.op('act', lambda e: e.activation(out=hgt[:, l, :, :], in_=hglu[:, :, n:n + 30], func=AF.Copy),
                      reads=[r_hg], writes=[r_hgt])
                pr.op('act', lambda e: e.activation(out=vrs[:, 0:nt], in_=vmv[:, 0:nt, 1], func=AF.Ln, bias=epsv[:, 1:2]),
                      reads=[r_vs, res("epsv")], writes=[r_vs])
                pr.op('act', lambda e: e.activation(out=vrs[:, 0:nt], in_=vrs[:, 0:nt], func=AF.Exp, scale=-0.5),
                      reads=[r_vs], writes=[r_vs])
                for i in range(nt):
                    vb, r_vg = VG[i]
                    pr.op('dve', lambda e, i=i, vb=vb: e.tensor_scalar(out=vb, in0=vb, scalar1=vmv[:, i, 0:1],
                                                                      scalar2=vrs[:, i:i + 1], op0=ALU.subtract, op1=ALU.mult),
                          reads=[r_vg, r_vs], writes=[r_vg])
                    pr.op('pool', lambda e, vb=vb: e.tensor_tensor(out=vb, in0=vb, in1=lnv[:, l, 0, :], op=ALU.mult),
                          reads=[r_vg, r_const], writes=[r_vg])
                    pr.op('pool', lambda e, vb=vb, i=i: e.tensor_tensor(out=v_tm[:, i, :], in0=vb, in1=lnv[:, l, 1, :], op=ALU.add),
                          reads=[r_vg, r_const], writes=[r_vtm])
                diag_hook()
                diag_hook()

            def st_conv():
                for c in range(3):
                    bk, rb = newbank()
                    for k in range(CK):
                        pr.op('pe', lambda e, c=c, k=k, bk=bk: e.matmul(bk[:, 0:n], lhsT=S3d[:, k * 3 + c, :], rhs=hglu[:, c, k:k + n],
                                                                       start=(k == 0), stop=(k == CK - 1)),
                              reads=[S3c[c], rS["S3"], r_hg], writes=[rb])
                    pr.op('act', lambda e, c=c, bk=bk: e.activation(out=hc[:, c, 0:n], in_=bk[:, 0:n], func=AF.Identity,
                                                                   bias=V(l, V_CB, c)),
                          reads=[rb, r_const], writes=[r_hc])

            def st_fwd():
                r_st = res("cst")
                tb = []
                for i in range(nt):
                    bk, rb = newbank()
                    tb.append((bk, rb))
                    for c in range(3):
                        pr.op('pe', lambda e, i=i, c=c, bk=bk: e.transpose(out=bk[:, c * 128:(c + 1) * 128],
                                                                          in_=hc[:, c, i * 128:(i + 1) * 128], identity=ident_f[:, :]),
                              reads=[r_hc, r_const], writes=[rb])
                    pr.op('dve', lambda e, i=i, bk=bk: e.bn_stats(out=cst6[:, i, :], in_=bk[:, 0:AW]), reads=[rb], writes=[r_st])
                    pr.op('dve', lambda e, i=i: e.bn_aggr(out=cmv[:, i, :], in_=cst6[:, i, :]), reads=[r_st], writes=[r_st])
                pr.op('act', lambda e: e.activation(out=crsv[:, 0:nt], in_=cmv[:, 0:nt, 1], func=AF.Ln, bias=epsv[:, 1:2]),
                      reads=[r_st, res("epsv")], writes=[r_st])
                pr.op('act', lambda e: e.activation(out=crsv[:, 0:nt], in_=crsv[:, 0:nt], func=AF.Exp, scale=-0.5), reads=[r_st], writes=[r_st])
                for i in range(nt):
                    bk, rb = tb[i]
                    pr.op('dve', lambda e, i=i, bk=bk: e.tensor_scalar(out=hn[:, i, :], in0=bk[:, 0:AW], scalar1=cmv[:, i, 0:1],
                                                                      scalar2=crsv[:, i:i + 1], op0=ALU.subtract, op1=ALU.mult),
                          reads=[rb, r_st], writes=[res("hn%d" % i)])

            def st_gp():
                for m in range(3):
                    bk, rb = newbank()
                    for i in range(nt):
                        cs = slice(i * 128, (i + 1) * 128)
                        pr.op('pe', lambda e, m=m, cs=cs, bk=bk: e.matmul(bk[:, cs], lhsT=ind[0:2, :], rhs=bsv[0:2, l, m, :],
                                                                         start=True, stop=False),
                              reads=[r_const], writes=[rb])
                        for hh in range(2):
                            h = 2 * m + hh
                            pr.op('pe', lambda e, i=i, h=h, hh=hh, cs=cs, bk=bk: e.matmul(
                                bk[hh * 64:(hh + 1) * 64, cs], lhsT=v_tm[:, i, h * 64:(h + 1) * 64], rhs=wsv[:, l, h, :],
                                start=False, stop=(hh == 1), tile_position=(0, hh * 64), skip_group_check=True),
                                reads=[r_vtm, res("WsT")], writes=[rb])
                    pr.op('dve', lambda e, m=m, bk=bk: e.tensor_tensor(out=y_sub[:, m, 0:n], in0=bk[:, 0:n], in1=ug[:, m, 0:n], op=ALU.mult),
                          reads=[rb, r_ug], writes=[r_y])
                for m2 in range(2):
                    bk, rb = newbank()
                    for i in range(nt):
                        cs = slice(i * 128, (i + 1) * 128)
                        first = (gt0 + i == HALO // 128)
                        for gg in range(2):
                            g = 2 * m2 + gg
                            A = apv[:, 2 if first else 0, g, :]
                            prev = p_prev[:, l, g * 64:(g + 1) * 64] if i == 0 else p_cur[:, i - 1, g * 64:(g + 1) * 64]
                            pr.op('pe', lambda e, i=i, g=g, gg=gg, A=A, cs=cs, bk=bk: e.matmul(
                                bk[gg * 64:(gg + 1) * 64, cs], lhsT=p_cur[:, i, g * 64:(g + 1) * 64], rhs=A,
                                start=True, stop=False, tile_position=(0, gg * 64), skip_group_check=True),
                                reads=[r_pc, r_const], writes=[rb])
                            pr.op('pe', lambda e, g=g, gg=gg, prev=prev, cs=cs, bk=bk: e.matmul(
                                bk[gg * 64:(gg + 1) * 64, cs], lhsT=prev, rhs=apv[:, 1, g, :],
                                start=False, stop=True, tile_position=(0, gg * 64), skip_group_check=True),
                                reads=[r_pc, r_pp, r_const], writes=[rb])
                    pr.op('act', lambda e, m2=m2, bk=bk: e.activation(out=pd[:, m2, 0:n], in_=bk[:, 0:n], func=AF.Copy),
                          reads=[rb], writes=[r_pd])
                    bk2, rb2 = newbank()
                    pr.op('pe', lambda e, m2=m2, bk2=bk2: e.matmul(bk2[:, 0:n], lhsT=pwv[:, l, m2, :], rhs=pd[:, m2, 0:n],
                                                                  start=True, stop=True),
                          reads=[r_pd, r_pw], writes=[rb2])
                    pr.op('dve', lambda e, m2=m2, bk2=bk2: e.tensor_scalar(
                        out=y_sub[:, 3 + m2, 0:n], in0=bk2[:, 0:n], scalar1=V(l, V_PB, m2), scalar2=V(l, V_PS, m2),
                        op0=ALU.add, op1=ALU.mult), reads=[rb2, r_const], writes=[r_y])
                pr.op('act', lambda e: e.activation(out=p_prev[:, l, :], in_=p_cur[:, nt - 1, :], func=AF.Copy),
                      reads=[r_pc], writes=[r_pp])

            def st_back():
                for i in range(nt):
                    b = i
                    r_hn = res("hn%d" % b)
                    bk2, rb2 = newbank()
                    bk2b = bk2.bitcast(BF16)
                    for c in range(3):
                        pr.op('pe', lambda e, b=b, c=c, bk2b=bk2b: e.transpose(out=bk2b[:, c * 128:(c + 1) * 128],
                                                                              in_=hn[:, b, c * 128:(c + 1) * 128], identity=ident_b[:, :]),
                              reads=[r_hn, res("identb")], writes=[rb2])
                    for c in range(3):
                        pr.op('act', lambda e, i=i, c=c, bk2b=bk2b: e.activation(
                            out=y_sub[:, 5 + c, i * 128:(i + 1) * 128], in_=bk2b[:, c * 128:(c + 1) * 128], func=AF.Silu,
                            scale=V(l, V_CLG, c), bias=V(l, V_CLB, c)), reads=[rb2, r_const], writes=[r_y])

            def st_wout():
                proj_add(j, c0, n, S2v, "S2", y_sub, r_y)
            return dict(P=st_P, conv=st_conv, fwd=st_fwd, gp=st_gp, back=st_back, wout=st_wout)

        def attn_stages(l, ip, j, c0, n):
            r_h = res("hp_%d" % j)
            r_q = res("q_sub")
            r_o = res("y_sub")
            r_K = res("K_fm")
            r_V = res("V_tm")

            def st_Q():
              for m in range(NCH):
                bk, rb = newbank()
                for k in range(NCH):
                    pr.op('pe', lambda e, m=m, k=k, bk=bk: e.matmul(bk[:, 0:n], lhsT=S4v[:, k, m * 128:(m + 1) * 128],
                                                                   rhs=h_pass[:, k, 2 + c0:2 + c0 + n], start=(k == 0), stop=(k == NCH - 1)),
                          reads=[rS["S4"], r_h], writes=[rb])
                pr.op('act', lambda e, m=m, bk=bk: e.activation(out=q_sub[:, m, 0:n], in_=bk[:, 0:n], func=AF.Copy),
                      reads=[rb], writes=[r_q])

            def scores(hd):
                b = hd % 2
                r_E = res("E%d" % b)
                for mc in range(2):
                    bk, rb = newbank()
                    for dc in range(2):
                        pr.op('pe', lambda e, hd=hd, mc=mc, dc=dc, bk=bk: e.matmul(
                            bk[:, 0:n], lhsT=K_fm[:, hd * 2 + dc, mc * 128:(mc + 1) * 128], rhs=q_sub[:, hd * 2 + dc, 0:n],
                            start=(dc == 0), stop=(dc == 1)), reads=[r_K, r_q], writes=[rb])
                    pr.op('act', lambda e, b=b, mc=mc, bk=bk: e.activation(out=Eb[:, b, mc, 0:n], in_=bk[:, 0:n], func=AF.Exp),
                          reads=[rb], writes=[r_E])

            def pv(hd):
                b = hd % 2
                r_E = res("E%d" % b)
                r_rd = res("rden%d" % b)
                bk, rb = newbank()
                for mc in range(2):
                    pr.op('pe', lambda e, b=b, mc=mc, bk=bk: e.matmul(bk[:, 0:n], lhsT=ones_1[:, :], rhs=Eb[:, b, mc, 0:n],
                                                                     start=(mc == 0), stop=(mc == 1)),
                          reads=[r_E, res("ones_1")], writes=[rb])
                pr.op('act', lambda e, b=b, bk=bk: e.activation(out=rden[:, b, 0:n], in_=bk[:, 0:n], func=AF.Ln), reads=[rb], writes=[r_rd])
                pr.op('act', lambda e, b=b: e.activation(out=rden[:, b, 0:n], in_=rden[:, b, 0:n], func=AF.Exp, scale=-1.0),
                      reads=[r_rd], writes=[r_rd])
                for dc in range(2):
                    bk, rb = newbank()
                    for mc in range(2):
                        pr.op('pe', lambda e, hd=hd, b=b, mc=mc, dc=dc, bk=bk: e.matmul(
                            bk[:, 0:n], lhsT=V_tm[:, mc, (hd * 2 + dc) * 128:(hd * 2 + dc + 1) * 128], rhs=Eb[:, b, mc, 0:n],
                            start=(mc == 0), stop=(mc == 1)), reads=[r_V, r_E], writes=[rb])
                    pr.op('dve', lambda e, hd=hd, b=b, dc=dc, bk=bk: e.tensor_tensor(
                        out=y_sub[:, hd * 2 + dc, 0:n], in0=bk[:, 0:n], in1=rden[:, b, 0:n], op=ALU.mult),
                        reads=[rb, r_rd], writes=[r_o])

            def st_heads_a():
                scores(0)
                scores(1)
                pv(0)

            def st_heads_b():
                for hd in range(1, 4):
                    if hd + 1 < 4:
                        scores(hd + 1)
                    pv(hd)

            def st_wo():
                proj_add(j, c0, n, S2v, "S2", y_sub, r_o)
            return dict(Q=st_Q, heads_a=st_heads_a, heads_b=st_heads_b, wo=st_wo)

        def ffn_tail_in(l):
            pr.op('act', lambda e: e.activation(out=h_pass[:, :, 0:2], in_=hpt[:, l, :, :], func=AF.Copy),
                  reads=[res("hp_tail")], writes=[res("hp_0")])

        def ffn_tail_out(l, P, jl):
            pr.op('act', lambda e: e.activation(out=hpt[:, l, :, :], in_=h_pass[:, :, P:P + 2], func=AF.Copy),
                  reads=[res("hp_%d" % jl)], writes=[res("hp_tail")])

        def ffn_up(l, gi, u, j, c0, n):
            j0, g = ffn_groups()[gi]
            slot = "S1" if gi % 2 == 0 else "S3"
            buf = S1 if gi % 2 == 0 else S3
            wg, wv_, wd = grp_views(buf)
            hp_reads = [res("hp_%d" % j)] + ([res("hp_%d" % (j - 1))] if j > 0 else [])
            gb = u % 2
            r_gt = res("gated%d" % gb)
            for jj in range(g):
                jc = j0 + jj
                b = jj % 2
                outs = []
                for which, wsrc, dstt, jcol in ((0, wg, tg, jc), (1, wv_, tv, NJ + jc)):
                    r_t = res("t%d_%d" % (which, b))
                    bk, rb = newbank()
                    for k in range(NCH):
                        pr.op('pe', lambda e, wsrc=wsrc, jj=jj, k=k, bk=bk: e.matmul(
                            bk[:, 0:n + 2], lhsT=wsrc[:, k, jj * 128:(jj + 1) * 128], rhs=h_pass[:, k, c0:c0 + n + 2],
                            start=(k == 0), stop=(k == NCH - 1)), reads=[rS[slot]] + hp_reads, writes=[rb])
                    pr.op('act', lambda e, dstt=dstt, b=b, jcol=jcol, bk=bk: e.activation(
                        out=dstt[:, b, 0:n], in_=bk[:, 2:n + 2], func=AF.Identity,
                        scale=V(l, V_FW, 2 * 44 + jcol), bias=V(l, V_FB, jcol)), reads=[rb, r_const], writes=[r_t])
                    for tap, sh in ((1, 1), (0, 0)):
                        pr.op('dve', lambda e, dstt=dstt, b=b, jcol=jcol, tap=tap, sh=sh, bk=bk: e.scalar_tensor_tensor(
                            out=dstt[:, b, 0:n], in0=bk[:, sh:sh + n], scalar=V(l, V_FW, tap * 44 + jcol),
                            in1=dstt[:, b, 0:n], op0=ALU.mult, op1=ALU.add), reads=[rb, r_t, r_const], writes=[r_t])
                    outs.append(r_t)
                r_sg = res("sg%d" % b)
                pr.op('act', lambda e, b=b: e.activation(out=sgb[:, b, 0:n], in_=tg[:, b, 0:n], func=AF.Silu),
                      reads=[outs[0]], writes=[r_sg])
                pr.op('pool', lambda e, b=b, gb=gb, jj=jj: e.tensor_tensor(out=gated[:, gb, jj, 0:n], in0=sgb[:, b, 0:n],
                                                                          in1=tv[:, b, 0:n], op=ALU.mult),
                      reads=[r_sg, outs[1]], writes=[r_gt])

        def ffn_down(l, gi, u, j, c0, n):
            j0, g = ffn_groups()[gi]
            slot = "S1" if gi % 2 == 0 else "S3"
            buf = S1 if gi % 2 == 0 else S3
            wg, wv_, wd = grp_views(buf)
            gb = u % 2
            r_gt = res("gated%d" % gb)
            for m in range(NCH):
                bk, rb = newbank()
                for jj in range(g):
                    pr.op('pe', lambda e, m=m, jj=jj, bk=bk: e.matmul(
                        bk[:, 0:n], lhsT=wd[:, jj, m * 128:(m + 1) * 128], rhs=gated[:, gb, jj, 0:n],
                        start=(jj == 0), stop=(jj == g - 1)), reads=[rS[slot], r_gt], writes=[rb])
                pr.op('dve', lambda e, m=m, bk=bk: e.tensor_tensor(out=x_fm[:, m, c0:c0 + n], in0=bk[:, 0:n],
                                                                  in1=x_fm[:, m, c0:c0 + n], op=ALU.add),
                      reads=[rb, xres(j, m)], writes=[xres(j, m)])

        out_ctr = [0]

        def final_store(ps0, j, c0, n):
            r_hf = res("hfin")
            for i in range(n // 128):
                t0 = ps0 + c0 + i * 128
                if t0 < HALO:
                    continue
                b = out_ctr[0] % 2
                out_ctr[0] += 1
                r_os = res("xs%d" % b)
                for half in range(2):
                    bk, rb = newbank()
                    for c in range(4):
                        pr.op('pe', lambda e, i=i, c=c, half=half, bk=bk: e.transpose(
                            out=bk[:, c * 128:(c + 1) * 128], in_=hfin[:, half * 4 + c, i * 128:(i + 1) * 128],
                            identity=ident_f[:, :]), reads=[r_hf, r_const], writes=[rb])
                    if half == 0:
                        pr.op('act', lambda e, b=b, bk=bk: e.activation(out=xs[:, b, 0:512], in_=bk[:, 0:512], func=AF.Copy),
                              reads=[rb], writes=[r_os])
                    else:
                        pr.op('dve', lambda e, b=b, bk=bk: e.tensor_copy(out=xs[:, b, 512:1024], in_=bk[:, 0:512]),
                              reads=[rb], writes=[r_os])
                pr.op('sp', lambda e, b=b, t0=t0: e.dma_start(out=out_d[t0 - HALO:t0 - HALO + 128, :], in_=xs[:, b, :]),
                      reads=[r_os], dsem=s_out[b])

        early_norm = [False]
        kv_done = [False]
        x_loaded = [False]
        lps = [(ip, l) for ip in range(len(passes)) for l in range(L)]
        ngr = len(ffn_groups())
        build_diag(0)
        for idx, (ip, l) in enumerate(lps):
            ps0, P = passes[ip]
            def subs_for(ip_, l_):
                sb_ = split_subs(passes[ip_][1])
                if ip_ == 0 and l_ >= 1 and SKIP_HALO_TILE:
                    sb_ = [(128, sb_[0][1] - 128)] + sb_[1:]
                return sb_
            subs = subs_for(ip, l)
            nxt = lps[idx + 1] if idx + 1 < len(lps) else None
            if l == 0 and not x_loaded[0]:
                load_x(ps0, P, subs)
            x_loaded[0] = False
            if do_attn and not kv_done[0]:
                kv_k(l)
                load_full("S4", S4v, wv_d[l])
            phase_begin("m")
            ns = len(subs)
            r_hs = res("h_sub")
            r_hp = res("h_pass")

            def mixnorm(j, part, l_=l, ip_=ip, subs_=subs):
                c0_, n_ = subs_[j]
                if part == 'A':
                    normA(None, j, c0_, n_, 0)
                else:
                    normB(None, j, c0_, n_, NVL * l_ + V_NMIX, h_sub, r_hs, 0, (ip_ == 0 and j == 0), 0)

            def attnorm(j, part):
                c0_, n_ = subs[j]
                if part == 'A':
                    normA(None, j, c0_, n_, 1)
                else:
                    normB(None, j, c0_, n_, NVL * l + V_NX, h_pass, res("hp_%d" % j), 2 + c0_, (ip == 0 and j == 0), 1)

            def ffnnorm(j, part):
                c0_, n_ = subs[j]
                if part == 'A':
                    normA(None, j, c0_, n_, 1)
                else:
                    if j == 0:
                        ffn_tail_in(l)
                    normB(None, j, c0_, n_, NVL * l + V_NFFN, h_pass, res("hp_%d" % j), 2 + c0_, (ip == 0 and j == 0), 1)
                    if j == ns - 1:
                        ffn_tail_out(l, P, j)

            assert do_mix and do_attn and do_ffn
            if not kv_done[0]:
                pass
            if not early_norm[0]:
                mixnorm(0, 'A')
                mixnorm(0, 'B')
            early_norm[0] = False
            MS = [mix_stages(l, ip, j, c0, n, (ps0 + c0) // 128) for j, (c0, n) in enumerate(subs)]
            if ns > 1:
                mixnorm(1, 'A')
            MS[0]['P']()
            for j in range(ns):
                if j + 1 < ns:
                    mixnorm(j + 1, 'B')
                MS[j]['conv']()
                MS[j]['fwd']()
                if j > 0:
                    attnorm(j - 1, 'B')
                MS[j]['gp']()
                MS[j]['back']()
                if j + 2 < ns:
                    mixnorm(j + 2, 'A')
                if j + 1 < ns:
                    MS[j + 1]['P']()
                elif ns > 1:
                    phase_begin("a")
                    AS = [attn_stages(l, ip, jj, cc, nn) for jj, (cc, nn) in enumerate(subs)]
                    AS[0]['Q']()
                MS[j]['wout']()
                attnorm(j, 'A')
                if j == 0 and not kv_done[0]:
                    kv_v(l)
                    store_kv(l)
                    load_full("S4", S4v, wq_d[l])
            kv_done[0] = False
            load_full("S2", S2v, wo_d[l])
            load_group(l, 0)
            load_group(l, 1)
            attnorm(ns - 1, 'B')
            if ns == 1:
                phase_begin("a")
                AS = [attn_stages(l, ip, j, c0, n) for j, (c0, n) in enumerate(subs)]
                AS[0]['Q']()
            for j in range(ns):
                AS[j]['heads_a']()
                if j > 0:
                    ffnnorm(j - 1, 'B')
                AS[j]['heads_b']()
                if j + 1 < ns:
                    AS[j + 1]['Q']()
                AS[j]['wo']()
                ffnnorm(j, 'A')
            if nxt is not None:
                load_full("S2", S2v, w_out_d[nxt[1]])
                if nxt[0] == 0:
                    load_full("S4", S4v, wk_d[nxt[1]])
                else:
                    load_kv(nxt[1])
                    load_full("S4", S4v, wq_d[nxt[1]])
                    kv_done[0] = True
            phase_begin("f")
            units = [(gi, j, c0, n) for gi in range(ngr) for j, (c0, n) in enumerate(subs)]

            def after_group(gi):
                if gi + 2 < ngr:
                    load_group(l, gi + 2)
                elif nxt is not None:
                    if gi % 2 == 0:
                        load_full("S1", S1v, w_in_d[nxt[1]])
                    else:
                        for c_ in range(3):
                            diag_pending.append(lambda c_=c_, l_=nxt[1]: diag_chunk(l_, c_))
            def tail_stages(pj):
                pc0_, pn_ = subs[pj]
                st = [lambda: normA(None, pj, pc0_, pn_, 0),
                      lambda: normB(None, pj, pc0_, pn_, VFIN, hfin, res("hfin"), 0, False, 0),
                      lambda: final_store(ps0, pj, pc0_, pn_)]
                if ip + 1 < len(passes):
                    nps0, nP = passes[ip + 1]
                    nsubs = split_subs(nP)
                    assert all(nsubs[q][0] + nsubs[q][1] <= subs[q][0] + subs[q][1] and nsubs[q][0] >= (subs[q][0] if q else 0) for q in range(len(nsubs)))
                    if pj < len(nsubs):
                        st.append(lambda: load_x_sub(nps0, nsubs, pj))
                        if pj == 0:
                            def en():
                                mixnorm(0, 'A', 0, ip + 1, nsubs)
                                mixnorm(0, 'B', 0, ip + 1, nsubs)
                                early_norm[0] = True
                            st.append(en)
                    x_loaded[0] = True
                return st
            tail_q = []

            def drain(k):
                for _ in range(k):
                    if tail_q:
                        tail_q.pop(0)()

            prev = None
            for u, (gi, j, c0, n) in enumerate(units):
                if u == 0 and ns == 1:
                    ffnnorm(ns - 1, 'B')
                ffn_up(l, gi, u, j, c0, n)
                if u == 0 and ns > 1:
                    ffnnorm(ns - 1, 'B')
                drain(1)
                if prev is not None:
                    pu, (pgi, pj, pc0, pn) = prev
                    ffn_down(l, pgi, pu, pj, pc0, pn)
                    if pj == ns - 1:
                        after_group(pgi)
                    if pgi == ngr - 1 and pj == 0 and l + 1 < L:
                        mixnorm(0, 'A', l + 1, ip, subs_for(ip, l + 1))
                        mixnorm(0, 'B', l + 1, ip, subs_for(ip, l + 1))
                        early_norm[0] = True
                    if pgi == ngr - 1 and l == L - 1:
                        tail_q.extend(tail_stages(pj))
                        drain(1)
                if nxt is not None and nxt[0] == 0 and u == 2:
                    kv_k(nxt[1])
                    load_full("S4", S4v, wv_d[nxt[1]])
                if nxt is not None and nxt[0] == 0 and u == 5:
                    kv_v(nxt[1])
                    store_kv(nxt[1])
                    load_full("S4", S4v, wq_d[nxt[1]])
                    kv_done[0] = True
                prev = (u, (gi, j, c0, n))
            pu, (pgi, pj, pc0, pn) = prev
            ffn_down(l, pgi, pu, pj, pc0, pn)
            after_group(pgi)
            if l == L - 1:
                tail_q.extend(tail_stages(pj))
            drain(len(tail_q))

        with nc.Block() as block:
            pr.emit(block, sems, dsems)
    return nc


def _cols(v):
    v = np.asarray(v, np.float32).reshape(-1)
    return v.reshape(-1, 128).T


def make_vecs(inp, L, mask):
    NV = NVL * L + 9
    a = np.zeros((128, NV), np.float32)
    for l in range(L):
        b = NVL * l
        a[:, b + V_NMIX:b + V_NMIX + 8] = _cols(inp["norm_mix"][l])
        a[:, b + V_NX:b + V_NX + 8] = _cols(inp["norm_x"][l])
        a[:, b + V_NFFN:b + V_NFFN + 8] = _cols(inp["norm_ffn"][l])
        a[:, b + V_NMEM:b + V_NMEM + 8] = _cols(inp["norm_mem"][l])
        a[:, b + V_CB:b + V_CB + 3] = _cols(inp["conv_b"][l])
        a[:, b + V_CLG:b + V_CLG + 3] = _cols(inp["conv_ln_g"][l])
        a[:, b + V_CLB:b + V_CLB + 3] = _cols(inp["conv_ln_b"][l])
        a[:, b + V_PB:b + V_PB + 2] = _cols(inp["pool_b"][l])
        a[:, b + V_PS:b + V_PS + 2] = _cols(inp["pool_scale"][l])
        cw = np.asarray(inp["conv_w"][l], np.float32)
        for k in range(CK):
            a[:, b + V_CW + 3 * k:b + V_CW + 3 * k + 3] = _cols(cw[k])
        fw = np.asarray(inp["ffn_conv_w"][l], np.float32)
        for k in range(3):
            a[:, b + V_FW + 44 * k:b + V_FW + 44 * k + 44] = _cols(fw[k])
        a[:, b + V_FB:b + V_FB + 44] = _cols(inp["ffn_conv_b"][l])
    a[:, NVL * L:NVL * L + 8] = _cols(inp["norm_final"])
    a[:, NVL * L + 8] = mask
    return a


def make_consts(first_half):
    i = np.arange(128)
    cmask = ((i[None, :] // 64) <= (i[:, None] // 64)).astype(np.float32)
    ap = np.zeros((128, 3, 4, 128), np.float32)
    tp = i[:, None]
    t = i[None, :]
    for g, win in enumerate((2, 4, 8, 16)):
        diag = ((tp <= t) & (tp > t - win)).astype(np.float32) / win - (tp == t)
        off = ((tp - 128) > (t - win)).astype(np.float32) / win
        cnt = np.minimum(t + 1, win).astype(np.float32)
        fst = ((tp <= t) & (tp > t - win)).astype(np.float32) / cnt - (tp == t)
        ap[:, 0, g, :] = diag
        ap[:, 1, g, :] = off
        ap[:, 2, g, :] = fst if first_half else diag
    ind = np.zeros((2, 128), np.float32)
    ind[0, :64] = 1.0
    ind[1, 64:] = 1.0
    return cmask, ap.reshape(128, -1), ind, np.eye(128, dtype=np.float32)


def make_in_map(inp, xloc, memb, first_half, L):
    cmask, ap, ind, ident = make_consts(first_half)
    f = lambda k: np.ascontiguousarray(np.asarray(inp[k], np.float32))
    lnbc = np.stack([np.stack([np.asarray(inp["gmlp_ln_g"][l], np.float32), np.asarray(inp["gmlp_ln_b"][l], np.float32)])
                     for l in range(L)]).reshape(1, -1)
    lnbc = np.ascontiguousarray(np.broadcast_to(lnbc, (128, lnbc.shape[1])))
    return {
        "x_in": np.ascontiguousarray(xloc, np.float32), "mem": np.ascontiguousarray(memb, np.float32),
        "vecs": make_vecs(inp, L, 0.0 if first_half else 1.0), "lnbc": lnbc, "cmask": cmask, "apool": ap, "ind": ind,
        "ident": ident, "gmlp_ws": f("gmlp_ws"), "gmlp_bs": f("gmlp_bs"), "pool_w": f("pool_w"),
    }


def _pk(w):
    w = np.asarray(w, np.float32)
    Lw, R, N = w.shape
    return np.ascontiguousarray(w.reshape(Lw, R // 128, 128, N).transpose(0, 2, 1, 3).reshape(Lw, 128, (R // 128) * N))


def make_weights(inp, L):
    out = {k: _pk(inp[k][:L]) for k in ("w_in", "w_out", "wq", "wk", "wv", "wo")}
    groups = ffn_groups()
    gcols = 2 * NCH * 512 + GSZ * D
    wf = np.zeros((L, len(groups), 128, gcols), np.float32)
    w_up = np.asarray(inp["w_up"], np.float32)[:L]
    w_dn = np.asarray(inp["w_down"], np.float32)[:L]
    for gi, (j0, g) in enumerate(groups):
        for part, off in ((0, 0), (1, DFF)):
            blk = w_up[:, :, off + j0 * 128:off + (j0 + g) * 128]
            blk = blk.reshape(L, NCH, 128, g * 128).transpose(0, 2, 1, 3)
            dst = wf[:, gi, :, part * NCH * 512:(part + 1) * NCH * 512].reshape(L, 128, NCH, 512)
            dst[:, :, :, 0:g * 128] = blk
        blk = w_dn[:, j0 * 128:(j0 + g) * 128, :].reshape(L, g, 128, D).transpose(0, 2, 1, 3)
        dst = wf[:, gi, :, 2 * NCH * 512:].reshape(L, 128, GSZ, D)
        dst[:, :, 0:g, :] = blk
    out["wffn"] = wf
    return out


_NC_CACHE = {}

FULL_PASSES = [(0, 896), (896, 896), (1792, 896), (2688, 896), (3584, 768)]


def kernel(**inputs):
    x = np.asarray(inputs["x"], np.float32)
    mem = np.asarray(inputs["mem"], np.float32)
    B, S, _ = x.shape
    L = inputs["w_in"].shape[0]
    half = S // 2
    T = half + HALO
    key = (T, L)
    if key not in _NC_CACHE:
        _NC_CACHE[key] = build_program(T, FULL_PASSES, L)
    nc = _NC_CACHE[key]
    in_maps = []
    wts = make_weights(inputs, L)
    for core in range(8):
        b, hf = core // 2, core % 2
        if hf == 0:
            xloc = np.concatenate([np.zeros((HALO, D), np.float32), x[b, :half]], axis=0)
        else:
            xloc = x[b, half - HALO:]
        im = make_in_map(inputs, xloc, mem[b], hf == 0, L)
        im.update(wts)
        in_maps.append(im)
    res = run_bass_kernel_spmd(nc, in_maps, core_ids=list(range(8)))
    out = np.empty((B, S, D), np.float32)
    for core in range(8):
        b, hf = core // 2, core % 2
        out[b, hf * half:(hf + 1) * half] = res.results[core]["out"]
    return out
```

```python
import numpy as np
from contextlib import ExitStack
import concourse.bass as bass
import concourse.mybir as mybir
from concourse.bass_utils import run_bass_kernel_spmd

F32 = mybir.dt.float32
BF16 = mybir.dt.bfloat16
ALU = mybir.AluOpType
AF = mybir.ActivationFunctionType

D = 1024
NCH = 8
AW = 384
PROJ = 1792
DFF = 2816
NJ = 22
NMEM = 256
HALO = 256
CK = 31
RMS_EPS = 1e-6
LN_EPS = 1e-5
NVL = 314
GSZ = 4
SUBMAX = 384
NOMASK = False
SKIP_HALO_TILE = True

V_NMIX, V_NX, V_NFFN, V_NMEM = 0, 8, 16, 24
V_CB, V_CLG, V_CLB = 32, 35, 38
V_PB, V_PS = 41, 43
V_CW = 45
V_FW = 138
V_FB = 270


ENGS = ['pe', 'act', 'dve', 'pool', 'sp']
BLK = {'pe': 'tensor', 'act': 'scalar', 'dve': 'vector', 'pool': 'gpsimd', 'sp': 'sync'}


class Res:
    __slots__ = ('name', 'w', 'rd')

    def __init__(self, name):
        self.name = name
        self.w = None
        self.rd = {}


class DmaSem:
    def __init__(self, h):
        self.h = h
        self.count = 0


class Prog:
    def __init__(self):
        self.ops = {e: [] for e in ENGS}

    def op(self, eng, fn, reads=(), writes=(), dsem=None):
        idx = len(self.ops[eng])
        deps = []
        for r in reads:
            if r.w is not None:
                deps.append((r.w, True))
        for r in writes:
            if r.w is not None:
                deps.append((r.w, False))
            for t in r.rd.values():
                deps.append((t, False))
        waits = []
        for tok, raw in deps:
            if tok[0] == 'eng':
                _, e, i = tok
                if e == eng:
                    if eng == 'pe' or (not raw) or idx - i >= 3:
                        continue
                self.ops[e][i]['sig'] = True
            waits.append(tok)
        rec = dict(fn=fn, waits=waits, sig=False, dsem=None)
        if dsem is not None:
            dsem.count += 16
            rec['dsem'] = dsem
            mytok = ('dma', dsem, dsem.count)
        else:
            mytok = ('eng', eng, idx)
        self.ops[eng].append(rec)
        key = (mytok[0], mytok[1])
        for r in reads:
            r.rd[key] = mytok
        for r in writes:
            r.w = mytok
            r.rd = {}
        return mytok

    def emit(self, block, sems, final_waits):
        cnt = {}
        for e in ENGS:
            c = 0
            arr = []
            for o in self.ops[e]:
                if o['sig']:
                    c += 1
                arr.append(c)
            cnt[e] = arr
        for e in ENGS:
            def body(engobj, e=e):
                seen = {}
                for o in self.ops[e]:
                    for tok in o['waits']:
                        if tok[0] == 'eng':
                            sem = sems[tok[1]]
                            val = cnt[tok[1]][tok[2]]
                            key = ('eng', tok[1])
                        else:
                            sem = tok[1].h
                            val = tok[2]
                            key = ('dma', id(tok[1]))
                        if seen.get(key, 0) >= val:
                            continue
                        seen[key] = val
                        engobj.wait_ge(sem, val)
                    ins = o['fn'](engobj)
                    if o['sig']:
                        ins.then_inc(sems[e], 1)
                    if o['dsem'] is not None:
                        ins.then_inc(o['dsem'].h, 16)
                if e == 'sp':
                    for ds in final_waits:
                        if ds.count:
                            engobj.wait_ge(ds.h, ds.count)
            getattr(block, BLK[e])(body)


def split_subs(P):
    tiles = P // 128
    sizes = [min(3, tiles)]
    r = tiles - sizes[0]
    while r > 0:
        t = 3 if r >= 5 else (2 if r >= 2 else 1)
        sizes.append(t)
        r -= t
    subs = []
    c = 0
    for t in sizes:
        subs.append((c, t * 128))
        c += t * 128
    return subs


def ffn_groups():
    gs = []
    j = 0
    first = NJ % GSZ
    if first:
        gs.append((0, first))
        j = first
    while j < NJ:
        g = min(GSZ, NJ - j)
        gs.append((j, g))
        j += g
    return gs


def build_program(T, passes, L=2, do_mix=True, do_attn=True, do_ffn=True):
    PMAX = max(p[1] for p in passes)
    TOUT = T - HALO
    NV = NVL * L + 9
    nc = bass.Bass("TRN2", target_bir_lowering=False)

    def din(name, shape):
        return nc.dram_tensor(name, shape, F32, kind="ExternalInput").ap()

    x_in = din("x_in", [T, D])
    mem_d = din("mem", [NMEM, D])
    vecs_d = din("vecs", [128, NV])
    lnbc_d = din("lnbc", [128, L * 2 * AW])
    cmask_d = din("cmask", [128, 128])
    apool_d = din("apool", [128, 3 * 4 * 128])
    ind_d = din("ind", [2, 128])
    ident_d = din("ident", [128, 128])
    w_in_d = din("w_in", [L, 128, NCH * PROJ])
    w_out_d = din("w_out", [L, 128, NCH * D])
    wq_d = din("wq", [L, 128, NCH * D])
    wk_d = din("wk", [L, 128, NCH * D])
    wv_d = din("wv", [L, 128, NCH * D])
    wo_d = din("wo", [L, 128, NCH * D])
    NGR = len(ffn_groups())
    GCOLS = 2 * NCH * 512 + GSZ * D
    wffn_d = din("wffn", [L, NGR, 128, GCOLS])
    ws_d = din("gmlp_ws", [L, 6, 128, 128])
    bs_d = din("gmlp_bs", [L, 6, 128])
    pw_d = din("pool_w", [L, 4, 64, 64])
    out_d = nc.dram_tensor("out", [TOUT, D], F32, kind="ExternalOutput").ap()
    kvs_d = nc.dram_tensor("kv_scratch", [L, 128, 2 * NCH * NMEM], BF16, kind="Internal").ap()

    pr = Prog()
    es = ExitStack()
    with es:
        def sb(name, shape, dt):
            return es.enter_context(nc.sbuf_tensor("sb_" + name, shape, dt))

        def new_sem(name):
            return es.enter_context(nc.semaphore(name))

        sems = {e: new_sem("s_" + e) for e in ENGS}
        dsems = []

        def dsem(name):
            d = DmaSem(new_sem(name))
            dsems.append(d)
            return d

        x_fm = sb("x_fm", [128, NCH, PMAX], F32)
        ident_f = sb("ident_f", [128, 128], F32)
        ident_b = sb("ident_b", [128, 128], BF16)
        ones_m = sb("ones_m", [128, 128], BF16)
        ones_1 = sb("ones_1", [128, 128], BF16)
        vecs = sb("vecs", [128, NV], F32)
        lnbc = sb("lnbc", [128, L * 2 * AW], BF16)
        cmask = sb("cmask", [128, 128], F32)
        apool = sb("apool", [128, 3 * 4 * 128], BF16)
        ind = sb("ind", [2, 128], BF16)
        bsrow = sb("bsrow", [2, L * 3 * 128], BF16)
        poolW = sb("poolW", [128, L * 2 * 128], BF16)
        WsT = sb("WsT", [128, L * 6 * 128], BF16)
        S1 = sb("S1", [128, NCH * PROJ], BF16)
        S2 = sb("S2", [128, NCH * D], BF16)
        S3 = sb("S3", [128, 2 * NCH * 512 + GSZ * D], BF16)
        S4 = sb("S4", [128, NCH * D], BF16)
        K_fm = sb("K_fm", [128, NCH, NMEM], BF16)
        V_tm = sb("V_tm", [128, 2, D], BF16)
        p_prev = sb("p_prev", [128, L, 256], BF16)
        hg_tail = sb("hg_tail", [128, L, 3 * 30], BF16)
        hp_tail = sb("hp_tail", [128, L, NCH * 2], BF16)
        sq = sb("sq", [128, NCH, SUBMAX], BF16)
        rstd = sb("rstd", [128, SUBMAX], F32)
        xs = sb("xs", [128, 2, D], F32)
        memn = sb("memn", [128, NCH, NMEM], BF16)
        st6 = sb("st6", [128, 2, 6], F32)
        mv = sb("mv", [128, 2, 2], F32)
        rsv = sb("rsv", [128, 2], F32)
        ssm = sb("ssm", [128, 2], F32)
        vst6 = sb("vst6", [128, 3, 6], F32)
        vmv = sb("vmv", [128, 3, 2], F32)
        vrs = sb("vrs", [128, 3], F32)
        cst6 = sb("cst6", [128, 3, 6], F32)
        cmv = sb("cmv", [128, 3, 2], F32)
        crsv = sb("crsv", [128, 3], F32)
        epsv = sb("epsv", [128, 2], F32)
        h_pass = sb("h_pass", [128, NCH, PMAX + 2], BF16)
        ARENA_BYTES = 35584
        arena_f = sb("arena", [128, ARENA_BYTES // 4], F32)
        arena_b = arena_f.bitcast(BF16)
        AR = {}

        def carve(name, phase, off, shape, dt):
            nel = 1
            for d_ in shape[1:]:
                nel *= d_
            esz = 4 if dt == F32 else 2
            assert off % 4 == 0 and off + nel * esz <= ARENA_BYTES, name
            base = arena_f if dt == F32 else arena_b
            v = base[:, off // esz:off // esz + nel]
            if len(shape) == 3:
                v = v.rearrange("p (a b) -> p a b", a=shape[1])
            elif len(shape) == 4:
                v = v.rearrange("p (a b c) -> p a b c", a=shape[1], b=shape[2])
            AR[name] = (phase, off, off + nel * esz)
            return v

        S_ = SUBMAX
        h_sub = carve("h_sub", "ma", 0, [128, NCH, S_], BF16)
        y_sub = carve("y_sub", "ma", 6144, [128, NCH, S_], BF16)
        ug = carve("ug", "m", 12288, [128, 3, S_], BF16)
        vg = carve("vg", "m", 14592, [128, 2, AW], F32)
        v_tm = carve("v_tm", "m", 17664, [128, 3, AW], BF16)
        p_cur = carve("p_cur", "m", 19968, [128, 3, 256], BF16)
        sig = carve("sig", "m", 21504, [128, 2, S_], F32)
        hglu = carve("hglu", "m", 24576, [128, 3, 30 + S_], BF16)
        pd = carve("pd", "m", 27136, [128, 2, S_], BF16)
        hc = carve("hc", "m", 28672, [128, 3, S_], F32)
        hn = carve("hn", "m", 33280, [128, 3, AW], BF16)
        q_sub = carve("q_sub", "a", 12288, [128, NCH, S_], BF16)
        Eb = carve("Eb", "a", 18432, [128, 2, 2, S_], BF16)
        rden = carve("rden", "a", 21504, [128, 2, S_], F32)
        tg = carve("tg", "f", 6144, [128, 2, S_], F32)
        tv = carve("tv", "f", 9216, [128, 2, S_], F32)
        sgb = carve("sgb", "f", 12288, [128, 2, S_], F32)
        gated = carve("gated", "f", 15360, [128, 2, GSZ, S_], BF16)
        hfin = carve("hfin", "f", 21504, [128, NCH, S_], F32)
        wstage = carve("wstage", "s", 28672, [128, 6, 128], F32)

        banks = [es.enter_context(nc.psum_tensor("ps%d" % i, [128, 512], F32)) for i in range(8)]
        r_bank = [Res("bank%d" % i) for i in range(8)]
        bank_ctr = [0]

        def newbank():
            i = bank_ctr[0] % 8
            bank_ctr[0] += 1
            return banks[i], r_bank[i]

        R = {}

        def res(name):
            if name not in R:
                R[name] = Res(name)
            return R[name]

        RES_VIEW = {"vg0": "vg", "vg1": "vg", "sig0": "sig", "sig1": "sig", "hn0": "hn", "hn1": "hn", "hn2": "hn",
                    "E0": "Eb", "E1": "Eb", "rden0": "rden", "rden1": "rden", "t0_0": "tg", "t0_1": "tg",
                    "t1_0": "tv", "t1_1": "tv", "sg0": "sgb", "sg1": "sgb", "gated0": "gated", "gated1": "gated"}
        PHASE_RES = {
            "m": ["h_sub", "y_sub", "ug", "vg0", "vg1", "v_tm", "p_cur", "sig0", "sig1", "hglu", "pd", "hc", "hn0", "hn1", "hn2"],
            "a": ["h_sub", "y_sub", "q_sub", "E0", "E1", "rden0", "rden1"],
            "f": ["t0_0", "t0_1", "t1_0", "t1_1", "sg0", "sg1", "gated0", "gated1", "hfin"],
            "s": ["wstage"],
        }

        def _merge(dst, tok):
            key = (tok[0], tok[1])
            old = dst.rd.get(key)
            if old is None or old[2] < tok[2]:
                dst.rd[key] = tok

        def phase_begin(ph):
            mine = PHASE_RES[ph]
            for dn in mine:
                dv = RES_VIEW.get(dn, dn)
                _, dlo, dhi = AR[dv]
                dst = res(dn)
                for sn, src in list(R.items()):
                    sv = RES_VIEW.get(sn, sn)
                    if sv not in AR or sv == dv or sn in mine:
                        continue
                    _, slo, shi = AR[sv]
                    if slo < dhi and dlo < shi:
                        if src.w is not None:
                            _merge(dst, src.w)
                        for t in src.rd.values():
                            _merge(dst, t)

        r_const = res("const")
        s_misc = dsem("d_misc")
        s_pw = dsem("d_pw")
        s_ws = dsem("d_ws")
        s_xs = [dsem("d_xs0"), dsem("d_xs1")]
        s_S = {k: dsem("d_" + k) for k in ("S1", "S2", "S3", "S4")}
        s_out = [dsem("d_out0"), dsem("d_out1")]
        s_mem = dsem("d_mem")

        def V(l, off, c=0):
            col = NVL * l + off + c
            return vecs[:, col:col + 1]

        VFIN = NVL * L
        VMASK = NVL * L + 8

        S1v = S1[:, :].rearrange("p (k n) -> p k n", k=NCH)
        S2v = S2[:, :].rearrange("p (k n) -> p k n", k=NCH)
        S4v = S4[:, :].rearrange("p (k n) -> p k n", k=NCH)
        S3d = S3[:, 0:93 * 128].rearrange("p (t n) -> p t n", n=128)

        def grp_views(buf):
            wg = buf[:, 0:NCH * 512].rearrange("p (k n) -> p k n", k=NCH)
            wv_ = buf[:, NCH * 512:2 * NCH * 512].rearrange("p (k n) -> p k n", k=NCH)
            wd = buf[:, 2 * NCH * 512:2 * NCH * 512 + GSZ * D].rearrange("p (j n) -> p j n", j=GSZ)
            return wg, wv_, wd

        apv = apool[:, :].rearrange("p (a g n) -> p a g n", a=3, g=4)
        lnv = lnbc[:, :].rearrange("p (l t n) -> p l t n", l=L, t=2)
        bsv = bsrow[:, :].rearrange("p (l m n) -> p l m n", l=L, m=3)
        pwv = poolW[:, :].rearrange("p (l m n) -> p l m n", l=L, m=2)
        wsv = WsT[:, :].rearrange("p (l h n) -> p l h n", l=L, h=6)
        hgt = hg_tail[:, :, :].rearrange("p l (c n) -> p l c n", c=3)
        hpt = hp_tail[:, :, :].rearrange("p l (c n) -> p l c n", c=NCH)

        rS = {k: res(k) for k in ("S1", "S2", "S3", "S4")}

        SLOTBUF = {"S1": S1, "S2": S2, "S4": S4}

        def load_full(slot, view, src):
            ncol = src.shape[-1]
            pr.op('pool', lambda e: e.dma_start(out=SLOTBUF[slot][:, 0:ncol].rearrange("p (a b) -> p a b", b=2048),
                                                in_=src.rearrange("p (a b) -> p a b", b=2048)),
                  writes=[rS[slot]], dsem=s_S[slot])

        load_full("S4", S4v, wk_d[0])
        load_full("S1", S1v, w_in_d[0])
        load_full("S2", S2v, w_out_d[0])

        pr.op('sp', lambda e: e.dma_start(out=ident_f[:, :], in_=ident_d), writes=[r_const], dsem=s_misc)
        pr.op('sp', lambda e: e.dma_start(out=vecs[:, :], in_=vecs_d), writes=[r_const], dsem=s_misc)
        pr.op('sp', lambda e: e.dma_start(out=cmask[:, :], in_=cmask_d), writes=[r_const], dsem=s_misc)
        pr.op('pool', lambda e: e.dma_start(out=lnbc[:, :], in_=lnbc_d), writes=[r_const], dsem=s_misc)
        pr.op('pool', lambda e: e.dma_start(out=apool[:, :], in_=apool_d), writes=[r_const], dsem=s_misc)
        pr.op('pool', lambda e: e.dma_start(out=ind[:, :], in_=ind_d), writes=[r_const], dsem=s_misc)
        pr.op('pool', lambda e: e.dma_start(
            out=bsrow[:, :].rearrange("p (l m n) -> p l m n", l=L, m=3),
            in_=bs_d.rearrange("l (m r) i -> r l m i", r=2)), writes=[r_const], dsem=s_misc)
        r_pw = res("poolW")
        pr.op('dve', lambda e: e.memset(poolW[:, :], 0.0), writes=[r_pw])
        for l in range(L):
            for g in range(4):
                gg, m2 = g % 2, g // 2
                pr.op('pool', lambda e, l=l, g=g, gg=gg, m2=m2: e.dma_start(
                    out=pwv[gg * 64:(gg + 1) * 64, l, m2, gg * 64:(gg + 1) * 64], in_=pw_d[l, g]),
                    writes=[r_pw], dsem=s_pw)
        pr.op('dve', lambda e: e.tensor_copy(out=ident_b[:, :], in_=ident_f[:, :]), reads=[r_const], writes=[res("identb")])
        pr.op('dve', lambda e: e.memset(ones_m[:, :], 1.0 / 1024.0), writes=[res("ones_m")])
        pr.op('dve', lambda e: e.memset(ones_1[:, :], 1.0), writes=[res("ones_1")])
        pr.op('dve', lambda e: e.memset(epsv[:, 0:1], RMS_EPS), writes=[res("epsv")])
        pr.op('dve', lambda e: e.memset(epsv[:, 1:2], LN_EPS), writes=[res("epsv")])
        pr.op('dve', lambda e: e.memset(p_prev[:, :, :], 0.0), writes=[res("p_prev")])
        pr.op('dve', lambda e: e.memset(hg_tail[:, :, :], 0.0), writes=[res("hg_tail")])
        pr.op('dve', lambda e: e.memset(hp_tail[:, :, :], 0.0), writes=[res("hp_tail")])
        phase_begin("s")
        r_wst = res("wstage")
        for l in range(L):
            pr.op('sp', lambda e, l=l: e.dma_start(out=wstage[:, :, :], in_=ws_d[l].rearrange("h i j -> i h j")),
                  writes=[r_wst], dsem=s_ws)
            for h in range(6):
                pr.op('dve', lambda e, h=h: e.tensor_tensor(out=wstage[:, h, :], in0=wstage[:, h, :], in1=cmask[:, :], op=ALU.mult),
                      reads=[r_wst, r_const], writes=[r_wst])
            for h0 in range(0, 6, 3):
                bk, rb = newbank()
                for h in range(h0, h0 + 3):
                    pr.op('pe', lambda e, h=h, bk=bk, h0=h0: e.transpose(
                        out=bk[:, (h - h0) * 128:(h - h0 + 1) * 128], in_=wstage[:, h, :], identity=ident_f[:, :]),
                        reads=[r_wst, r_const], writes=[rb])
                pr.op('act', lambda e, l=l, h0=h0, bk=bk: e.activation(
                    out=wsv[:, l, h0:h0 + 3, :], in_=bk[:, 0:384].rearrange("p (h n) -> p h n", h=3), func=AF.Copy),
                    reads=[rb], writes=[res("WsT")])

        def load_group(l, gi):
            slot = "S1" if gi % 2 == 0 else "S3"
            buf = S1 if gi % 2 == 0 else S3
            pr.op('pool', lambda e: e.dma_start(out=buf[:, 0:GCOLS].rearrange("p (a b) -> p a b", b=2048),
                                                in_=wffn_d[l, gi].rearrange("p (a b) -> p a b", b=2048)),
                  writes=[rS[slot]], dsem=s_S[slot])

        S3c = [res("S3c%d" % c) for c in range(3)]

        def diag_chunk(l, c):
            for k in range(CK):
                t = k * 3 + c
                wr = [S3c[c], rS["S3"]] if k == 0 else [S3c[c]]
                if c <= 1:
                    pr.op('dve', lambda e, t=t, l=l: e.tensor_scalar(
                        out=S3d[:, t, :], in0=ident_b[:, :], scalar1=V(l, V_CW, t), scalar2=None, op0=ALU.mult),
                        reads=[res("identb"), r_const], writes=wr)
                else:
                    pr.op('act', lambda e, t=t, l=l: e.activation(out=S3d[:, t, :], in_=ident_b[:, :], func=AF.Identity,
                                                                 scale=V(l, V_CW, t)),
                          reads=[res("identb"), r_const], writes=wr)

        def build_diag(l):
            for c in range(3):
                diag_chunk(l, c)

        diag_pending = []

        def diag_hook():
            if diag_pending:
                diag_pending.pop(0)()

        def xres(j, m):
            return res("x_%d_%d" % (j, m))

        xs_b = xs.bitcast(BF16)
        sq2 = xs_b[:, 0, :].rearrange("p (a b) -> p a b", a=NCH)[:, :, 0:SUBMAX] if False else None
        xs_flat_b = xs_b[:, :, :].rearrange("p a b -> p (a b)")
        sq2 = xs_flat_b[:, 0:NCH * SUBMAX].rearrange("p (a b) -> p a b", a=NCH)
        xs_flat_f = xs[:, :, :].rearrange("p a b -> p (a b)")
        rstd2 = xs_flat_f[:, (NCH * SUBMAX) // 2:(NCH * SUBMAX) // 2 + SUBMAX]
        NB = {0: (sq, rstd, [res("sq")], [res("rstd")]),
              1: (sq2, rstd2, [res("xs0"), res("xs1")], [res("xs0"), res("xs1")])}
        norm_banks = {}

        def normA(key, j, c0, n, bs=0):
            sq_, rstd_, r_sq, r_rs = NB[bs]
            for m in range(NCH):
                pr.op('act', lambda e, m=m: e.activation(out=sq_[:, m, 0:n], in_=x_fm[:, m, c0:c0 + n], func=AF.Square),
                      reads=[xres(j, m)], writes=r_sq)

        def normB(key, j, c0, n, gcol, dst, dst_res, dst_off=0, mask=False, bs=0):
            sq_, rstd_, r_sq, r_rs = NB[bs]
            bk, rb = newbank()
            for m in range(NCH):
                pr.op('pe', lambda e, m=m, bk=bk: e.matmul(bk[:, 0:n], lhsT=ones_m[:, :], rhs=sq_[:, m, 0:n],
                                                          start=(m == 0), stop=(m == NCH - 1)),
                      reads=r_sq + [res("ones_m")], writes=[rb])
            pr.op('act', lambda e, bk=bk: e.activation(out=rstd_[:, 0:n], in_=bk[:, 0:n], func=AF.Ln, bias=epsv[:, 0:1]),
                  reads=[rb, res("epsv")], writes=r_rs)
            pr.op('act', lambda e: e.activation(out=rstd_[:, 0:n], in_=rstd_[:, 0:n], func=AF.Exp, scale=-0.5), reads=r_rs, writes=r_rs)
            for m in range(NCH):
                pr.op('dve', lambda e, m=m: e.scalar_tensor_tensor(
                    out=dst[:, m, dst_off:dst_off + n], in0=x_fm[:, m, c0:c0 + n], scalar=vecs[:, gcol + m:gcol + m + 1],
                    in1=rstd_[:, 0:n], op0=ALU.mult, op1=ALU.mult),
                    reads=[xres(j, m), r_const] + r_rs, writes=[dst_res])
            if mask and not NOMASK:
                mw = HALO - c0
                pr.op('dve', lambda e: e.tensor_scalar(
                    out=dst[:, :, dst_off:dst_off + mw], in0=dst[:, :, dst_off:dst_off + mw],
                    scalar1=vecs[:, VMASK:VMASK + 1], scalar2=None, op0=ALU.mult),
                    reads=[dst_res, r_const], writes=[dst_res])

        def norm(j, c0, n, gcol, dst, dst_res, dst_off=0, mask=False, bs=0):
            normA(None, j, c0, n, bs)
            normB(None, j, c0, n, gcol, dst, dst_res, dst_off, mask, bs)

        def proj_add(j, c0, n, Wv, slot, src, src_res):
            for m in range(NCH):
                bk, rb = newbank()
                for k in range(NCH):
                    pr.op('pe', lambda e, m=m, k=k, bk=bk: e.matmul(bk[:, 0:n], lhsT=Wv[:, k, m * 128:(m + 1) * 128],
                                                                   rhs=src[:, k, 0:n], start=(k == 0), stop=(k == NCH - 1)),
                          reads=[rS[slot], src_res], writes=[rb])
                pr.op('dve', lambda e, m=m, bk=bk: e.tensor_tensor(out=x_fm[:, m, c0:c0 + n], in0=bk[:, 0:n],
                                                                  in1=x_fm[:, m, c0:c0 + n], op=ALU.add),
                      reads=[rb, xres(j, m)], writes=[xres(j, m)])

        def ln_rows(src_ap, k, width):
            rk = res("stat%d" % k)
            pr_reads = []
            return rk

        def load_x_sub(ps0, subs, j):
            c0j, nj = subs[j]
            for ti in range(c0j // 128, (c0j + nj) // 128):
                b = ti % 2
                t0 = ps0 + ti * 128
                rx = res("xs%d" % b)
                pr.op('sp', lambda e, b=b, t0=t0: e.dma_start(out=xs[:, b, :], in_=x_in[t0:t0 + 128, :]),
                      writes=[rx], dsem=s_xs[b])
                for half in range(2):
                    bk, rb = newbank()
                    for c in range(4):
                        pr.op('pe', lambda e, b=b, c=c, half=half, bk=bk: e.transpose(
                            out=bk[:, c * 128:(c + 1) * 128], in_=xs[:, b, (half * 4 + c) * 128:(half * 4 + c + 1) * 128],
                            identity=ident_f[:, :]), reads=[rx, r_const], writes=[rb])
                    eng = 'act' if half == 0 else 'dve'
                    outap = x_fm[:, half * 4:half * 4 + 4, ti * 128:(ti + 1) * 128]
                    inap = bk[:, 0:512].rearrange("p (c t) -> p c t", c=4)
                    if eng == 'act':
                        f = lambda e, outap=outap, inap=inap: e.activation(out=outap, in_=inap, func=AF.Copy)
                    else:
                        f = lambda e, outap=outap, inap=inap: e.tensor_copy(out=outap, in_=inap)
                    pr.op(eng, f, reads=[rb], writes=[xres(j, half * 4 + c) for c in range(4)])

        def load_x(ps0, P, subs):
            for j in range(len(subs)):
                load_x_sub(ps0, subs, j)

        def kv_k(l):
            rxs = [res("xs0"), res("xs1")]
            pr.op('sp', lambda e: e.dma_start(out=xs[:, :, :], in_=mem_d.rearrange("(t p) d -> p t d", p=128)),
                  reads=[], writes=rxs, dsem=s_mem)
            r_ss = res("ssm")
            sqf = sq[:, :, :].rearrange("p a b -> p (a b)")
            for t in range(2):
                pr.op('act', lambda e, t=t: e.activation(out=sqf[:, 0:D], in_=xs[:, t, :], func=AF.Square,
                                                        accum_out=ssm[:, t:t + 1]),
                      reads=rxs, writes=[res("sq"), r_ss])
            pr.op('act', lambda e: e.activation(out=ssm[:, :], in_=ssm[:, :], func=AF.Ln, scale=1.0 / D, bias=epsv[:, 0:1]),
                  reads=[r_ss, res("epsv")], writes=[r_ss])
            pr.op('act', lambda e: e.activation(out=ssm[:, :], in_=ssm[:, :], func=AF.Exp, scale=-0.5), reads=[r_ss], writes=[r_ss])
            for t in range(2):
                pr.op('dve', lambda e, t=t: e.tensor_scalar(out=xs[:, t, :], in0=xs[:, t, :], scalar1=ssm[:, t:t + 1],
                                                           scalar2=None, op0=ALU.mult),
                      reads=rxs + [r_ss], writes=rxs)
            r_memn = res("memn")
            for k in range(NCH):
                bk, rb = newbank()
                for t in range(2):
                    pr.op('pe', lambda e, k=k, t=t, bk=bk: e.transpose(
                        out=bk[:, t * 128:(t + 1) * 128], in_=xs[:, t, k * 128:(k + 1) * 128], identity=ident_f[:, :]),
                        reads=rxs + [r_const], writes=[rb])
                pr.op('act', lambda e, k=k, bk=bk: e.activation(out=memn[:, k, :], in_=bk[:, 0:256], func=AF.Identity,
                                                               scale=V(l, V_NMEM, k)),
                      reads=[rb, r_const], writes=[r_memn])
            for m in range(NCH):
                bk, rb = newbank()
                for k in range(NCH):
                    pr.op('pe', lambda e, m=m, k=k, bk=bk: e.matmul(bk[:, 0:NMEM], lhsT=S4v[:, k, m * 128:(m + 1) * 128],
                                                                   rhs=memn[:, k, :], start=(k == 0), stop=(k == NCH - 1)),
                          reads=[rS["S4"], r_memn], writes=[rb])
                pr.op('act', lambda e, m=m, bk=bk: e.activation(out=K_fm[:, m, :], in_=bk[:, 0:NMEM], func=AF.Identity, scale=1.0 / 16.0),
                      reads=[rb], writes=[res("K_fm")])

        def kv_v(l):
            r_memn = res("memn")
            for t in range(2):
                for half in range(2):
                    bk, rb = newbank()
                    for k in range(NCH):
                        pr.op('pe', lambda e, t=t, half=half, k=k, bk=bk: e.matmul(
                            bk[:, 0:512], lhsT=memn[:, k, t * 128:(t + 1) * 128], rhs=S4v[:, k, half * 512:(half + 1) * 512],
                            start=(k == 0), stop=(k == NCH - 1)), reads=[rS["S4"], r_memn], writes=[rb])
                    pr.op('act', lambda e, t=t, half=half, bk=bk: e.activation(
                        out=V_tm[:, t, half * 512:(half + 1) * 512], in_=bk[:, 0:512], func=AF.Copy),
                        reads=[rb], writes=[res("V_tm")])

        s_kvs = dsem("d_kvs")
        s_kvl = dsem("d_kvl")
        KVW = NCH * NMEM

        def store_kv(l):
            pr.op('sp', lambda e: e.dma_start(out=kvs_d[l][:, 0:KVW], in_=K_fm[:, :, :].rearrange("p a b -> p (a b)")),
                  reads=[res("K_fm")], writes=[res("kvs%d" % l)], dsem=s_kvs)
            pr.op('sp', lambda e: e.dma_start(out=kvs_d[l][:, KVW:2 * KVW], in_=V_tm[:, :, :].rearrange("p a b -> p (a b)")),
                  reads=[res("V_tm")], writes=[res("kvs%d" % l)], dsem=s_kvs)

        def load_kv(l):
            pr.op('sp', lambda e: e.dma_start(out=K_fm[:, :, :].rearrange("p a b -> p (a b)"), in_=kvs_d[l][:, 0:KVW]),
                  reads=[res("kvs%d" % l)], writes=[res("K_fm")], dsem=s_kvl)
            pr.op('sp', lambda e: e.dma_start(out=V_tm[:, :, :].rearrange("p a b -> p (a b)"), in_=kvs_d[l][:, KVW:2 * KVW]),
                  reads=[res("kvs%d" % l)], writes=[res("V_tm")], dsem=s_kvl)

        def mix_stages(l, ip, j, c0, n, gt0):
            nt = n // 128
            r_h = res("h_sub")
            W = S1v
            rW = rS["S1"]
            r_ug = res("ug")
            r_vtm = res("v_tm")
            r_pc = res("p_cur")
            r_hg = res("hglu")
            r_hgt = res("hg_tail")
            r_y = res("y_sub")
            r_pp = res("p_prev")
            r_pd = res("pd")
            r_hc = res("hc")

            def st_P():
                VG = [(vg[:, 0, :], res("vg0")), (vg[:, 1, :], res("vg1")), (hc[:, 0, :], res("hc"))]
                r_vs = res("vst")
                for i in range(nt):
                    vb, r_vg = VG[i]
                    bk, rb = newbank()
                    for k in range(NCH):
                        pr.op('pe', lambda e, i=i, k=k, bk=bk: e.matmul(bk[:, 0:AW], lhsT=h_sub[:, k, i * 128:(i + 1) * 128],
                                                                       rhs=W[:, k, AW:2 * AW], start=(k == 0), stop=(k == NCH - 1)),
                              reads=[rW, r_h], writes=[rb])
                    pr.op('act', lambda e, vb=vb, bk=bk: e.activation(out=vb, in_=bk[:, 0:AW], func=AF.Gelu),
                          reads=[rb], writes=[r_vg])
                    pr.op('dve', lambda e, i=i, vb=vb: e.bn_stats(out=vst6[:, i, :], in_=vb), reads=[r_vg], writes=[r_vs])
                    pr.op('dve', lambda e, i=i: e.bn_aggr(out=vmv[:, i, :], in_=vst6[:, i, :]), reads=[r_vs], writes=[r_vs])
                diag_hook()
                for m in range(3):
                    bk, rb = newbank()
                    for k in range(NCH):
                        pr.op('pe', lambda e, m=m, k=k, bk=bk: e.matmul(bk[:, 0:n], lhsT=W[:, k, m * 128:(m + 1) * 128],
                                                                       rhs=h_sub[:, k, 0:n], start=(k == 0), stop=(k == NCH - 1)),
                              reads=[rW, r_h], writes=[rb])
                    pr.op('act', lambda e, m=m, bk=bk: e.activation(out=ug[:, m, 0:n], in_=bk[:, 0:n], func=AF.Gelu),
                          reads=[rb], writes=[r_ug])
                for i in range(nt):
                    bk, rb = newbank()
                    for k in range(NCH):
                        pr.op('pe', lambda e, i=i, k=k, bk=bk: e.matmul(bk[:, 0:256], lhsT=h_sub[:, k, i * 128:(i + 1) * 128],
                                                                       rhs=W[:, k, 768:1024], start=(k == 0), stop=(k == NCH - 1)),
                              reads=[rW, r_h], writes=[rb])
                    pr.op('act', lambda e, i=i, bk=bk: e.activation(out=p_cur[:, i, :], in_=bk[:, 0:256], func=AF.Copy),
                          reads=[rb], writes=[r_pc])
                pr.op('act', lambda e: e.activation(out=hglu[:, :, 0:30], in_=hgt[:, l, :, :], func=AF.Copy),
                      reads=[r_hgt], writes=[r_hg])
                for c in range(3):
                    b = c % 2
                    r_sig = res("sig%d" % b)
                    bk, rb = newbank()
                    for k in range(NCH):
                        pr.op('pe', lambda e, c=c, k=k, bk=bk: e.matmul(bk[:, 0:n], lhsT=W[:, k, 1408 + c * 128:1408 + (c + 1) * 128],
                                                                       rhs=h_sub[:, k, 0:n], start=(k == 0), stop=(k == NCH - 1)),
                              reads=[rW, r_h], writes=[rb])
                    pr.op('act', lambda e, b=b, bk=bk: e.activation(out=sig[:, b, 0:n], in_=bk[:, 0:n], func=AF.Sigmoid),
                          reads=[rb], writes=[r_sig])
                    bk2, rb2 = newbank()
                    for k in range(NCH):
                        pr.op('pe', lambda e, c=c, k=k, bk2=bk2: e.matmul(bk2[:, 0:n], lhsT=W[:, k, 1024 + c * 128:1024 + (c + 1) * 128],
                                                                         rhs=h_sub[:, k, 0:n], start=(k == 0), stop=(k == NCH - 1)),
                              reads=[rW, r_h], writes=[rb2])
                    pr.op('dve', lambda e, c=c, b=b, bk2=bk2: e.tensor_tensor(out=hglu[:, c, 30:30 + n], in0=bk2[:, 0:n],
                                                                             in1=sig[:, b, 0:n], op=ALU.mult),
                          reads=[rb2, r_sig], writes=[r_hg])
                pr.op('act', lambda e: e.activation(out=hgt[:, l, :, :], in_=hglu[:, :, n:n + 30], func=AF.Copy),
                      reads=[r_hg], writes=[r_hgt])
                diag_hook()
                diag_hook()
                pr.op('act', lambda e: e.activation(out=vrs[:, 0:nt], in_=vmv[:, 0:nt, 1], func=AF.Ln, bias=epsv[:, 1:2]),
                      reads=[r_vs, res("epsv")], writes=[r_vs])
                pr.op('act', lambda e: e.activation(out=vrs[:, 0:nt], in_=vrs[:, 0:nt], func=AF.Exp, scale=-0.5),
                      reads=[r_vs], writes=[r_vs])
                for i in range(nt):
                    vb, r_vg = VG[i]
                    pr.op('dve', lambda e, i=i, vb=vb: e.tensor_scalar(out=vb, in0=vb, scalar1=vmv[:, i, 0:1],
                                                                      scalar2=vrs[:, i:i + 1], op0=ALU.subtract, op1=ALU.mult),
                          reads=[r_vg, r_vs], writes=[r_vg])
                    pr.op('pool', lambda e, vb=vb: e.tensor_tensor(out=vb, in0=vb, in1=lnv[:, l, 0, :], op=ALU.mult),
                          reads=[r_vg, r_const], writes=[r_vg])
                    pr.op('pool', lambda e, vb=vb, i=i: e.tensor_tensor(out=v_tm[:, i, :], in0=vb, in1=lnv[:, l, 1, :], op=ALU.add),
                          reads=[r_vg, r_const], writes=[r_vtm])

            def st_conv():
                for c in range(3):
                    bk, rb = newbank()
                    for k in range(CK):
                        pr.op('pe', lambda e, c=c, k=k, bk=bk: e.matmul(bk[:, 0:n], lhsT=S3d[:, k * 3 + c, :], rhs=hglu[:, c, k:k + n],
                                                                       start=(k == 0), stop=(k == CK - 1)),
                              reads=[S3c[c], rS["S3"], r_hg], writes=[rb])
                    pr.op('act', lambda e, c=c, bk=bk: e.activation(out=hc[:, c, 0:n], in_=bk[:, 0:n], func=AF.Identity,
                                                                   bias=V(l, V_CB, c)),
                          reads=[rb, r_const], writes=[r_hc])

            def st_fwd():
                r_st = res("cst")
                tb = []
                for i in range(nt):
                    bk, rb = newbank()
                    tb.append((bk, rb))
                    for c in range(3):
                        pr.op('pe', lambda e, i=i, c=c, bk=bk: e.transpose(out=bk[:, c * 128:(c + 1) * 128],
                                                                          in_=hc[:, c, i * 128:(i + 1) * 128], identity=ident_f[:, :]),
                              reads=[r_hc, r_const], writes=[rb])
                    pr.op('dve', lambda e, i=i, bk=bk: e.bn_stats(out=cst6[:, i, :], in_=bk[:, 0:AW]), reads=[rb], writes=[r_st])
                    pr.op('dve', lambda e, i=i: e.bn_aggr(out=cmv[:, i, :], in_=cst6[:, i, :]), reads=[r_st], writes=[r_st])
                pr.op('act', lambda e: e.activation(out=crsv[:, 0:nt], in_=cmv[:, 0:nt, 1], func=AF.Ln, bias=epsv[:, 1:2]),
                      reads=[r_st, res("epsv")], writes=[r_st])
                pr.op('act', lambda e: e.activation(out=crsv[:, 0:nt], in_=crsv[:, 0:nt], func=AF.Exp, scale=-0.5), reads=[r_st], writes=[r_st])
                for i in range(nt):
                    bk, rb = tb[i]
                    pr.op('dve', lambda e, i=i, bk=bk: e.tensor_scalar(out=hn[:, i, :], in0=bk[:, 0:AW], scalar1=cmv[:, i, 0:1],
                                                                      scalar2=crsv[:, i:i + 1], op0=ALU.subtract, op1=ALU.mult),
                          reads=[rb, r_st], writes=[res("hn%d" % i)])

            def st_gp():
                for m in range(3):
                    bk, rb = newbank()
                    for i in range(nt):
                        cs = slice(i * 128, (i + 1) * 128)
                        pr.op('pe', lambda e, m=m, cs=cs, bk=bk: e.matmul(bk[:, cs], lhsT=ind[0:2, :], rhs=bsv[0:2, l, m, :],
                                                                         start=True, stop=False),
                              reads=[r_const], writes=[rb])
                        for hh in range(2):
                            h = 2 * m + hh
                            pr.op('pe', lambda e, i=i, h=h, hh=hh, cs=cs, bk=bk: e.matmul(
                                bk[hh * 64:(hh + 1) * 64, cs], lhsT=v_tm[:, i, h * 64:(h + 1) * 64], rhs=wsv[:, l, h, :],
                                start=False, stop=(hh == 1), tile_position=(0, hh * 64), skip_group_check=True),
                                reads=[r_vtm, res("WsT")], writes=[rb])
                    pr.op('dve', lambda e, m=m, bk=bk: e.tensor_tensor(out=y_sub[:, m, 0:n], in0=bk[:, 0:n], in1=ug[:, m, 0:n], op=ALU.mult),
                          reads=[rb, r_ug], writes=[r_y])
                for m2 in range(2):
                    bk, rb = newbank()
                    for i in range(nt):
                        cs = slice(i * 128, (i + 1) * 128)
                        first = (gt0 + i == HALO // 128)
                        for gg in range(2):
                            g = 2 * m2 + gg
                            A = apv[:, 2 if first else 0, g, :]
                            prev = p_prev[:, l, g * 64:(g + 1) * 64] if i == 0 else p_cur[:, i - 1, g * 64:(g + 1) * 64]
                            pr.op('pe', lambda e, i=i, g=g, gg=gg, A=A, cs=cs, bk=bk: e.matmul(
                                bk[gg * 64:(gg + 1) * 64, cs], lhsT=p_cur[:, i, g * 64:(g + 1) * 64], rhs=A,
                                start=True, stop=False, tile_position=(0, gg * 64), skip_group_check=True),
                                reads=[r_pc, r_const], writes=[rb])
                            pr.op('pe', lambda e, g=g, gg=gg, prev=prev, cs=cs, bk=bk: e.matmul(
                                bk[gg * 64:(gg + 1) * 64, cs], lhsT=prev, rhs=apv[:, 1, g, :],
                                start=False, stop=True, tile_position=(0, gg * 64), skip_group_check=True),
                                reads=[r_pc, r_pp, r_const], writes=[rb])
                    pr.op('act', lambda e, m2=m2, bk=bk: e.activation(out=pd[:, m2, 0:n], in_=bk[:, 0:n], func=AF.Copy),
                          reads=[rb], writes=[r_pd])
                    bk2, rb2 = newbank()
                    pr.op('pe', lambda e, m2=m2, bk2=bk2: e.matmul(bk2[:, 0:n], lhsT=pwv[:, l, m2, :], rhs=pd[:, m2, 0:n],
                                                                  start=True, stop=True),
                          reads=[r_pd, r_pw], writes=[rb2])
                    pr.op('dve', lambda e, m2=m2, bk2=bk2: e.tensor_scalar(
                        out=y_sub[:, 3 + m2, 0:n], in0=bk2[:, 0:n], scalar1=V(l, V_PB, m2), scalar2=V(l, V_PS, m2),
                        op0=ALU.add, op1=ALU.mult), reads=[rb2, r_const], writes=[r_y])
                pr.op('act', lambda e: e.activation(out=p_prev[:, l, :], in_=p_cur[:, nt - 1, :], func=AF.Copy),
                      reads=[r_pc], writes=[r_pp])

            def st_back():
                for i in range(nt):
                    b = i
                    r_hn = res("hn%d" % b)
                    bk2, rb2 = newbank()
                    bk2b = bk2.bitcast(BF16)
                    for c in range(3):
                        pr.op('pe', lambda e, b=b, c=c, bk2b=bk2b: e.transpose(out=bk2b[:, c * 128:(c + 1) * 128],
                                                                              in_=hn[:, b, c * 128:(c + 1) * 128], identity=ident_b[:, :]),
                              reads=[r_hn, res("identb")], writes=[rb2])
                    for c in range(3):
                        pr.op('act', lambda e, i=i, c=c, bk2b=bk2b: e.activation(
                            out=y_sub[:, 5 + c, i * 128:(i + 1) * 128], in_=bk2b[:, c * 128:(c + 1) * 128], func=AF.Silu,
                            scale=V(l, V_CLG, c), bias=V(l, V_CLB, c)), reads=[rb2, r_const], writes=[r_y])

            def st_wout():
                proj_add(j, c0, n, S2v, "S2", y_sub, r_y)
            return dict(P=st_P, conv=st_conv, fwd=st_fwd, gp=st_gp, back=st_back, wout=st_wout)

        def attn_stages(l, ip, j, c0, n):
            r_h = res("hp_%d" % j)
            r_q = res("q_sub")
            r_o = res("y_sub")
            r_K = res("K_fm")
            r_V = res("V_tm")

            def st_Q():
              for m in range(NCH):
                bk, rb = newbank()
                for k in range(NCH):
                    pr.op('pe', lambda e, m=m, k=k, bk=bk: e.matmul(bk[:, 0:n], lhsT=S4v[:, k, m * 128:(m + 1) * 128],
                                                                   rhs=h_pass[:, k, 2 + c0:2 + c0 + n], start=(k == 0), stop=(k == NCH - 1)),
                          reads=[rS["S4"], r_h], writes=[rb])
                pr.op('act', lambda e, m=m, bk=bk: e.activation(out=q_sub[:, m, 0:n], in_=bk[:, 0:n], func=AF.Copy),
                      reads=[rb], writes=[r_q])

            def scores(hd):
                b = hd % 2
                r_E = res("E%d" % b)
                for mc in range(2):
                    bk, rb = newbank()
                    for dc in range(2):
                        pr.op('pe', lambda e, hd=hd, mc=mc, dc=dc, bk=bk: e.matmul(
                            bk[:, 0:n], lhsT=K_fm[:, hd * 2 + dc, mc * 128:(mc + 1) * 128], rhs=q_sub[:, hd * 2 + dc, 0:n],
                            start=(dc == 0), stop=(dc == 1)), reads=[r_K, r_q], writes=[rb])
                    pr.op('act', lambda e, b=b, mc=mc, bk=bk: e.activation(out=Eb[:, b, mc, 0:n], in_=bk[:, 0:n], func=AF.Exp),
                          reads=[rb], writes=[r_E])

            def pv(hd):
                b = hd % 2
                r_E = res("E%d" % b)
                r_rd = res("rden%d" % b)
                bk, rb = newbank()
                for mc in range(2):
                    pr.op('pe', lambda e, b=b, mc=mc, bk=bk: e.matmul(bk[:, 0:n], lhsT=ones_1[:, :], rhs=Eb[:, b, mc, 0:n],
                                                                     start=(mc == 0), stop=(mc == 1)),
                          reads=[r_E, res("ones_1")], writes=[rb])
                pr.op('act', lambda e, b=b, bk=bk: e.activation(out=rden[:, b, 0:n], in_=bk[:, 0:n], func=AF.Ln), reads=[rb], writes=[r_rd])
                pr.op('act', lambda e, b=b: e.activation(out=rden[:, b, 0:n], in_=rden[:, b, 0:n], func=AF.Exp, scale=-1.0),
                      reads=[r_rd], writes=[r_rd])
                for dc in range(2):
                    bk, rb = newbank()
                    for mc in range(2):
                        pr.op('pe', lambda e, hd=hd, b=b, mc=mc, dc=dc, bk=bk: e.matmul(
                            bk[:, 0:n], lhsT=V_tm[:, mc, (hd * 2 + dc) * 128:(hd * 2 + dc + 1) * 128], rhs=Eb[:, b, mc, 0:n],
                            start=(mc == 0), stop=(mc == 1)), reads=[r_V, r_E], writes=[rb])
                    pr.op('dve', lambda e, hd=hd, b=b, dc=dc, bk=bk: e.tensor_tensor(
                        out=y_sub[:, hd * 2 + dc, 0:n], in0=bk[:, 0:n], in1=rden[:, b, 0:n], op=ALU.mult),
                        reads=[rb, r_rd], writes=[r_o])

            def st_heads_a():
                scores(0)
                scores(1)
                pv(0)

            def st_heads_b():
                for hd in range(1, 4):
                    if hd + 1 < 4:
                        scores(hd + 1)
                    pv(hd)

            def st_wo():
                proj_add(j, c0, n, S2v, "S2", y_sub, r_o)
            return dict(Q=st_Q, heads_a=st_heads_a, heads_b=st_heads_b, wo=st_wo)

        def ffn_tail_in(l):
            pr.op('act', lambda e: e.activation(out=h_pass[:, :, 0:2], in_=hpt[:, l, :, :], func=AF.Copy),
                  reads=[res("hp_tail")], writes=[res("hp_0")])

        def ffn_tail_out(l, P, jl):
            pr.op('act', lambda e: e.activation(out=hpt[:, l, :, :], in_=h_pass[:, :, P:P + 2], func=AF.Copy),
                  reads=[res("hp_%d" % jl)], writes=[res("hp_tail")])

        def ffn_up(l, gi, u, j, c0, n):
            j0, g = ffn_groups()[gi]
            slot = "S1" if gi % 2 == 0 else "S3"
            buf = S1 if gi % 2 == 0 else S3
            wg, wv_, wd = grp_views(buf)
            hp_reads = [res("hp_%d" % j)] + ([res("hp_%d" % (j - 1))] if j > 0 else [])
            gb = u % 2
            r_gt = res("gated%d" % gb)
            for jj in range(g):
                jc = j0 + jj
                b = jj % 2
                outs = []
                for which, wsrc, dstt, jcol in ((0, wg, tg, jc), (1, wv_, tv, NJ + jc)):
                    r_t = res("t%d_%d" % (which, b))
                    bk, rb = newbank()
                    for k in range(NCH):
                        pr.op('pe', lambda e, wsrc=wsrc, jj=jj, k=k, bk=bk: e.matmul(
                            bk[:, 0:n + 2], lhsT=wsrc[:, k, jj * 128:(jj + 1) * 128], rhs=h_pass[:, k, c0:c0 + n + 2],
                            start=(k == 0), stop=(k == NCH - 1)), reads=[rS[slot]] + hp_reads, writes=[rb])
                    pr.op('act', lambda e, dstt=dstt, b=b, jcol=jcol, bk=bk: e.activation(
                        out=dstt[:, b, 0:n], in_=bk[:, 2:n + 2], func=AF.Identity,
                        scale=V(l, V_FW, 2 * 44 + jcol), bias=V(l, V_FB, jcol)), reads=[rb, r_const], writes=[r_t])
                    for tap, sh in ((1, 1), (0, 0)):
                        pr.op('dve', lambda e, dstt=dstt, b=b, jcol=jcol, tap=tap, sh=sh, bk=bk: e.scalar_tensor_tensor(
                            out=dstt[:, b, 0:n], in0=bk[:, sh:sh + n], scalar=V(l, V_FW, tap * 44 + jcol),
                            in1=dstt[:, b, 0:n], op0=ALU.mult, op1=ALU.add), reads=[rb, r_t, r_const], writes=[r_t])
                    outs.append(r_t)
                r_sg = res("sg%d" % b)
                pr.op('act', lambda e, b=b: e.activation(out=sgb[:, b, 0:n], in_=tg[:, b, 0:n], func=AF.Silu),
                      reads=[outs[0]], writes=[r_sg])
                pr.op('pool', lambda e, b=b, gb=gb, jj=jj: e.tensor_tensor(out=gated[:, gb, jj, 0:n], in0=sgb[:, b, 0:n],
                                                                          in1=tv[:, b, 0:n], op=ALU.mult),
                      reads=[r_sg, outs[1]], writes=[r_gt])

        def ffn_down(l, gi, u, j, c0, n):
            j0, g = ffn_groups()[gi]
            slot = "S1" if gi % 2 == 0 else "S3"
            buf = S1 if gi % 2 == 0 else S3
            wg, wv_, wd = grp_views(buf)
            gb = u % 2
            r_gt = res("gated%d" % gb)
            for m in range(NCH):
                bk, rb = newbank()
                for jj in range(g):
                    pr.op('pe', lambda e, m=m, jj=jj, bk=bk: e.matmul(
                        bk[:, 0:n], lhsT=wd[:, jj, m * 128:(m + 1) * 128], rhs=gated[:, gb, jj, 0:n],
                        start=(jj == 0), stop=(jj == g - 1)), reads=[rS[slot], r_gt], writes=[rb])
                pr.op('dve', lambda e, m=m, bk=bk: e.tensor_tensor(out=x_fm[:, m, c0:c0 + n], in0=bk[:, 0:n],
                                                                  in1=x_fm[:, m, c0:c0 + n], op=ALU.add),
                      reads=[rb, xres(j, m)], writes=[xres(j, m)])

        out_ctr = [0]

        def final_store(ps0, j, c0, n):
            r_hf = res("hfin")
            for i in range(n // 128):
                t0 = ps0 + c0 + i * 128
                if t0 < HALO:
                    continue
                b = out_ctr[0] % 2
                out_ctr[0] += 1
                r_os = res("xs%d" % b)
                for half in range(2):
                    bk, rb = newbank()
                    for c in range(4):
                        pr.op('pe', lambda e, i=i, c=c, half=half, bk=bk: e.transpose(
                            out=bk[:, c * 128:(c + 1) * 128], in_=hfin[:, half * 4 + c, i * 128:(i + 1) * 128],
                            identity=ident_f[:, :]), reads=[r_hf, r_const], writes=[rb])
                    if half == 0:
                        pr.op('act', lambda e, b=b, bk=bk: e.activation(out=xs[:, b, 0:512], in_=bk[:, 0:512], func=AF.Copy),
                              reads=[rb], writes=[r_os])
                    else:
                        pr.op('dve', lambda e, b=b, bk=bk: e.tensor_copy(out=xs[:, b, 512:1024], in_=bk[:, 0:512]),
                              reads=[rb], writes=[r_os])
                pr.op('sp', lambda e, b=b, t0=t0: e.dma_start(out=out_d[t0 - HALO:t0 - HALO + 128, :], in_=xs[:, b, :]),
                      reads=[r_os], dsem=s_out[b])

        early_norm = [False]
        kv_done = [False]
        x_loaded = [False]
        lps = [(ip, l) for ip in range(len(passes)) for l in range(L)]
        ngr = len(ffn_groups())
        build_diag(0)
        for idx, (ip, l) in enumerate(lps):
            ps0, P = passes[ip]
            def subs_for(ip_, l_):
                sb_ = split_subs(passes[ip_][1])
                if ip_ == 0 and l_ >= 1 and SKIP_HALO_TILE:
                    sb_ = [(128, sb_[0][1] - 128)] + sb_[1:]
                return sb_
            subs = subs_for(ip, l)
            nxt = lps[idx + 1] if idx + 1 < len(lps) else None
            if l == 0 and not x_loaded[0]:
                load_x(ps0, P, subs)
            x_loaded[0] = False
            if do_attn and not kv_done[0]:
                kv_k(l)
                load_full("S4", S4v, wv_d[l])
            phase_begin("m")
            ns = len(subs)
            r_hs = res("h_sub")
            r_hp = res("h_pass")

            def mixnorm(j, part, l_=l, ip_=ip, subs_=subs):
                c0_, n_ = subs_[j]
                if part == 'A':
                    normA(None, j, c0_, n_, 0)
                else:
                    normB(None, j, c0_, n_, NVL * l_ + V_NMIX, h_sub, r_hs, 0, (ip_ == 0 and j == 0), 0)

            def attnorm(j, part):
                c0_, n_ = subs[j]
                if part == 'A':
                    normA(None, j, c0_, n_, 1)
                else:
                    normB(None, j, c0_, n_, NVL * l + V_NX, h_pass, res("hp_%d" % j), 2 + c0_, (ip == 0 and j == 0), 1)

            def ffnnorm(j, part):
                c0_, n_ = subs[j]
                if part == 'A':
                    normA(None, j, c0_, n_, 1)
                else:
                    if j == 0:
                        ffn_tail_in(l)
                    normB(None, j, c0_, n_, NVL * l + V_NFFN, h_pass, res("hp_%d" % j), 2 + c0_, (ip == 0 and j == 0), 1)
                    if j == ns - 1:
                        ffn_tail_out(l, P, j)

            assert do_mix and do_attn and do_ffn
            if not kv_done[0]:
                pass
            if not early_norm[0]:
                mixnorm(0, 'A')
                mixnorm(0, 'B')
            early_norm[0] = False
            MS = [mix_stages(l, ip, j, c0, n, (ps0 + c0) // 128) for j, (c0, n) in enumerate(subs)]
            if ns > 1:
                mixnorm(1, 'A')
            MS[0]['P']()
            for j in range(ns):
                if j + 1 < ns:
                    mixnorm(j + 1, 'B')
                MS[j]['conv']()
                MS[j]['fwd']()
                if j > 0:
                    attnorm(j - 1, 'B')
                MS[j]['gp']()
                MS[j]['back']()
                if j + 2 < ns:
                    mixnorm(j + 2, 'A')
                if j + 1 < ns:
                    MS[j + 1]['P']()
                elif ns > 1:
                    phase_begin("a")
                    AS = [attn_stages(l, ip, jj, cc, nn) for jj, (cc, nn) in enumerate(subs)]
                    AS[0]['Q']()
                MS[j]['wout']()
                attnorm(j, 'A')
                if j == 0 and not kv_done[0]:
                    kv_v(l)
                    store_kv(l)
                    load_full("S4", S4v, wq_d[l])
            kv_done[0] = False
            load_full("S2", S2v, wo_d[l])
            load_group(l, 0)
            load_group(l, 1)
            attnorm(ns - 1, 'B')
            if ns == 1:
                phase_begin("a")
                AS = [attn_stages(l, ip, j, c0, n) for j, (c0, n) in enumerate(subs)]
                AS[0]['Q']()
            for j in range(ns):
                AS[j]['heads_a']()
                if j > 0:
                    ffnnorm(j - 1, 'B')
                AS[j]['heads_b']()
                if j + 1 < ns:
                    AS[j + 1]['Q']()
                AS[j]['wo']()
                ffnnorm(j, 'A')
            if nxt is not None:
                load_full("S2", S2v, w_out_d[nxt[1]])
                if nxt[0] == 0:
                    load_full("S4", S4v, wk_d[nxt[1]])
                else:
                    load_kv(nxt[1])
                    load_full("S4", S4v, wq_d[nxt[1]])
                    kv_done[0] = True
            phase_begin("f")
            units = [(gi, j, c0, n) for gi in range(ngr) for j, (c0, n) in enumerate(subs)]

            def after_group(gi):
                if gi + 2 < ngr:
                    load_group(l, gi + 2)
                elif nxt is not None:
                    if gi % 2 == 0:
                        load_full("S1", S1v, w_in_d[nxt[1]])
                    else:
                        for c_ in range(3):
                            diag_pending.append(lambda c_=c_, l_=nxt[1]: diag_chunk(l_, c_))
            def tail_stages(pj):
                pc0_, pn_ = subs[pj]
                st = [lambda: normA(None, pj, pc0_, pn_, 0),
                      lambda: normB(None, pj, pc0_, pn_, VFIN, hfin, res("hfin"), 0, False, 0),
                      lambda: final_store(ps0, pj, pc0_, pn_)]
                if ip + 1 < len(passes):
                    nps0, nP = passes[ip + 1]
                    nsubs = split_subs(nP)
                    assert all(nsubs[q][0] + nsubs[q][1] <= subs[q][0] + subs[q][1] and nsubs[q][0] >= (subs[q][0] if q else 0) for q in range(len(nsubs)))
                    if pj < len(nsubs):
                        st.append(lambda: load_x_sub(nps0, nsubs, pj))
                        if pj == 0:
                            def en():
                                mixnorm(0, 'A', 0, ip + 1, nsubs)
                                mixnorm(0, 'B', 0, ip + 1, nsubs)
                                early_norm[0] = True
                            st.append(en)
                    x_loaded[0] = True
                return st
            tail_q = []

            def drain(k):
                for _ in range(k):
                    if tail_q:
                        tail_q.pop(0)()

            prev = None
            for u, (gi, j, c0, n) in enumerate(units):
                if u == 0 and ns == 1:
                    ffnnorm(ns - 1, 'B')
                ffn_up(l, gi, u, j, c0, n)
                if u == 0 and ns > 1:
                    ffnnorm(ns - 1, 'B')
                drain(1)
                if prev is not None:
                    pu, (pgi, pj, pc0, pn) = prev
                    ffn_down(l, pgi, pu, pj, pc0, pn)
                    if pj == ns - 1:
                        after_group(pgi)
                    if pgi == ngr - 1 and pj == 0 and l + 1 < L:
                        mixnorm(0, 'A', l + 1, ip, subs_for(ip, l + 1))
                        mixnorm(0, 'B', l + 1, ip, subs_for(ip, l + 1))
                        early_norm[0] = True
                    if pgi == ngr - 1 and l == L - 1:
                        tail_q.extend(tail_stages(pj))
                        drain(1)
                if nxt is not None and nxt[0] == 0 and u == 2:
                    kv_k(nxt[1])
                    load_full("S4", S4v, wv_d[nxt[1]])
                if nxt is not None and nxt[0] == 0 and u == 5:
                    kv_v(nxt[1])
                    store_kv(nxt[1])
                    load_full("S4", S4v, wq_d[nxt[1]])
                    kv_done[0] = True
                prev = (u, (gi, j, c0, n))
            pu, (pgi, pj, pc0, pn) = prev
            ffn_down(l, pgi, pu, pj, pc0, pn)
            after_group(pgi)
            if l == L - 1:
                tail_q.extend(tail_stages(pj))
            drain(len(tail_q))

        with nc.Block() as block:
            pr.emit(block, sems, dsems)
    return nc


def _cols(v):
    v = np.asarray(v, np.float32).reshape(-1)
    return v.reshape(-1, 128).T


def make_vecs(inp, L, mask):
    NV = NVL * L + 9
    a = np.zeros((128, NV), np.float32)
    for l in range(L):
        b = NVL * l
        a[:, b + V_NMIX:b + V_NMIX + 8] = _cols(inp["norm_mix"][l])
        a[:, b + V_NX:b + V_NX + 8] = _cols(inp["norm_x"][l])
        a[:, b + V_NFFN:b + V_NFFN + 8] = _cols(inp["norm_ffn"][l])
        a[:, b + V_NMEM:b + V_NMEM + 8] = _cols(inp["norm_mem"][l])
        a[:, b + V_CB:b + V_CB + 3] = _cols(inp["conv_b"][l])
        a[:, b + V_CLG:b + V_CLG + 3] = _cols(inp["conv_ln_g"][l])
        a[:, b + V_CLB:b + V_CLB + 3] = _cols(inp["conv_ln_b"][l])
        a[:, b + V_PB:b + V_PB + 2] = _cols(inp["pool_b"][l])
        a[:, b + V_PS:b + V_PS + 2] = _cols(inp["pool_scale"][l])
        cw = np.asarray(inp["conv_w"][l], np.float32)
        for k in range(CK):
            a[:, b + V_CW + 3 * k:b + V_CW + 3 * k + 3] = _cols(cw[k])
        fw = np.asarray(inp["ffn_conv_w"][l], np.float32)
        for k in range(3):
            a[:, b + V_FW + 44 * k:b + V_FW + 44 * k + 44] = _cols(fw[k])
        a[:, b + V_FB:b + V_FB + 44] = _cols(inp["ffn_conv_b"][l])
    a[:, NVL * L:NVL * L + 8] = _cols(inp["norm_final"])
    a[:, NVL * L + 8] = mask
    return a


def make_consts(first_half):
    i = np.arange(128)
    cmask = ((i[None, :] // 64) <= (i[:, None] // 64)).astype(np.float32)
    ap = np.zeros((128, 3, 4, 128), np.float32)
    tp = i[:, None]
    t = i[None, :]
    for g, win in enumerate((2, 4, 8, 16)):
        diag = ((tp <= t) & (tp > t - win)).astype(np.float32) / win - (tp == t)
        off = ((tp - 128) > (t - win)).astype(np.float32) / win
        cnt = np.minimum(t + 1, win).astype(np.float32)
        fst = ((tp <= t) & (tp > t - win)).astype(np.float32) / cnt - (tp == t)
        ap[:, 0, g, :] = diag
        ap[:, 1, g, :] = off
        ap[:, 2, g, :] = fst if first_half else diag
    ind = np.zeros((2, 128), np.float32)
    ind[0, :64] = 1.0
    ind[1, 64:] = 1.0
    return cmask, ap.reshape(128, -1), ind, np.eye(128, dtype=np.float32)


def make_in_map(inp, xloc, memb, first_half, L):
    cmask, ap, ind, ident = make_consts(first_half)
    f = lambda k: np.ascontiguousarray(np.asarray(inp[k], np.float32))
    lnbc = np.stack([np.stack([np.asarray(inp["gmlp_ln_g"][l], np.float32), np.asarray(inp["gmlp_ln_b"][l], np.float32)])
                     for l in range(L)]).reshape(1, -1)
    lnbc = np.ascontiguousarray(np.broadcast_to(lnbc, (128, lnbc.shape[1])))
    return {
        "x_in": np.ascontiguousarray(xloc, np.float32), "mem": np.ascontiguousarray(memb, np.float32),
        "vecs": make_vecs(inp, L, 0.0 if first_half else 1.0), "lnbc": lnbc, "cmask": cmask, "apool": ap, "ind": ind,
        "ident": ident, "gmlp_ws": f("gmlp_ws"), "gmlp_bs": f("gmlp_bs"), "pool_w": f("pool_w"),
    }


def _pk(w):
    w = np.asarray(w, np.float32)
    Lw, R, N = w.shape
    return np.ascontiguousarray(w.reshape(Lw, R // 128, 128, N).transpose(0, 2, 1, 3).reshape(Lw, 128, (R // 128) * N))


def make_weights(inp, L):
    out = {k: _pk(inp[k][:L]) for k in ("w_in", "w_out", "wq", "wk", "wv", "wo")}
    groups = ffn_groups()
    gcols = 2 * NCH * 512 + GSZ * D
    wf = np.zeros((L, len(groups), 128, gcols), np.float32)
    w_up = np.asarray(inp["w_up"], np.float32)[:L]
    w_dn = np.asarray(inp["w_down"], np.float32)[:L]
    for gi, (j0, g) in enumerate(groups):
        for part, off in ((0, 0), (1, DFF)):
            blk = w_up[:, :, off + j0 * 128:off + (j0 + g) * 128]
            blk = blk.reshape(L, NCH, 128, g * 128).transpose(0, 2, 1, 3)
            dst = wf[:, gi, :, part * NCH * 512:(part + 1) * NCH * 512].reshape(L, 128, NCH, 512)
            dst[:, :, :, 0:g * 128] = blk
        blk = w_dn[:, j0 * 128:(j0 + g) * 128, :].reshape(L, g, 128, D).transpose(0, 2, 1, 3)
        dst = wf[:, gi, :, 2 * NCH * 512:].reshape(L, 128, GSZ, D)
        dst[:, :, 0:g, :] = blk
    out["wffn"] = wf
    return out


_NC_CACHE = {}

FULL_PASSES = [(0, 896), (896, 896), (1792, 896), (2688, 896), (3584, 768)]


def kernel(**inputs):
    x = np.asarray(inputs["x"], np.float32)
    mem = np.asarray(inputs["mem"], np.float32)
    B, S, _ = x.shape
    L = inputs["w_in"].shape[0]
    half = S // 2
    T = half + HALO
    key = (T, L)
    if key not in _NC_CACHE:
        _NC_CACHE[key] = build_program(T, FULL_PASSES, L)
    nc = _NC_CACHE[key]
    in_maps = []
    wts = make_weights(inputs, L)
    for core in range(8):
        b, hf = core // 2, core % 2
        if hf == 0:
            xloc = np.concatenate([np.zeros((HALO, D), np.float32), x[b, :half]], axis=0)
        else:
            xloc = x[b, half - HALO:]
        im = make_in_map(inputs, xloc, mem[b], hf == 0, L)
        im.update(wts)
        in_maps.append(im)
    res = run_bass_kernel_spmd(nc, in_maps, core_ids=list(range(8)))
    out = np.empty((B, S, D), np.float32)
    for core in range(8):
        b, hf = core // 2, core % 2
        out[b, hf * half:(hf + 1) * half] = res.results[core]["out"]
    return out
```
